# Optimizing a Trainium2 kernel written in Bass

```python
import jax, jax.numpy as jnp
from jax import lax
import numpy as np

D_MODEL = 1024
BATCH = 4
SEQ = 8192
DEPTH = 2

CHUNK = 64
NORM_EPS = 1e-6
A_HEADS = 8
A_HEAD_DIM = 64
A_WIDTH = A_HEADS * A_HEAD_DIM
A_LEFT_CHUNKS = 8
A_BAND = (A_LEFT_CHUNKS + 1) * CHUNK
A_MAX_REL = 256
B_GROUPS = 4
B_BLOCK = 128
B_WIDTH = D_MODEL // 2
B_GROUP_DIM = B_WIDTH // B_GROUPS
AB_IN = 3 * A_WIDTH + 2 * B_WIDTH
AB_MIX = A_WIDTH + B_WIDTH
C_HEADS = 4
C_KEY_DIM = D_MODEL // 2
C_VAL_DIM = D_MODEL
C_DK = C_KEY_DIM // C_HEADS
C_DV = C_VAL_DIM // C_HEADS
C_GATE_RANK = 16
C_GATE_TAU = 16.0
C_IN = 2 * C_KEY_DIM + 2 * C_VAL_DIM + C_GATE_RANK
D_FF = ((-(-8 * D_MODEL // 3) + 255) // 256) * 256
N_EVEN = (DEPTH + 1) // 2
N_ODD = DEPTH // 2

kernel_name = "hybrid_chunk_attn_gmlp_gla"


def rms_norm(x, g):
    xf = x.astype(jnp.float32)
    y = xf * lax.rsqrt(jnp.mean(xf * xf, axis=-1, keepdims=True) + NORM_EPS)
    return (y * g.astype(jnp.float32)).astype(x.dtype)


def layer_norm(x, g, b):
    xf = x.astype(jnp.float32)
    mu = jnp.mean(xf, axis=-1, keepdims=True)
    var = jnp.mean(jnp.square(xf - mu), axis=-1, keepdims=True)
    y = (xf - mu) * lax.rsqrt(var + NORM_EPS)
    return (y * g.astype(jnp.float32) + b.astype(jnp.float32)).astype(x.dtype)


def chunk_band_attention(q, k, v, rel_bias):
    b, s, h, d = q.shape
    nc = s // CHUNK
    f32 = jnp.float32
    qc = q.reshape(b, nc, CHUNK, h, d).astype(f32) * (d ** -0.5)
    pad = ((0, 0), (A_LEFT_CHUNKS, 0), (0, 0), (0, 0), (0, 0))
    kp = jnp.pad(k.reshape(b, nc, CHUNK, h, d), pad)
    vp = jnp.pad(v.reshape(b, nc, CHUNK, h, d), pad)
    idx = jnp.arange(nc)[:, None] + jnp.arange(A_LEFT_CHUNKS + 1)[None, :]
    kb = kp[:, idx].reshape(b, nc, A_BAND, h, d).astype(f32)
    vb = vp[:, idx].reshape(b, nc, A_BAND, h, d).astype(f32)
    scores = jnp.einsum('bcqhd,bckhd->bhcqk', qc, kb)
    qi = jnp.arange(CHUNK)[:, None]
    kj = jnp.arange(A_BAND)[None, :]
    rel = jnp.clip(qi + A_LEFT_CHUNKS * CHUNK - kj, -A_MAX_REL, A_MAX_REL) + A_MAX_REL
    bias = rel_bias.astype(f32)[:, rel]
    valid = jnp.repeat((idx - A_LEFT_CHUNKS) >= 0, CHUNK, axis=1)
    scores = jnp.where(valid[None, None, :, None, :], scores + bias[None, :, None],
                       jnp.finfo(f32).min)
    p = jax.nn.softmax(scores, axis=-1)
    out = jnp.einsum('bhcqk,bckhd->bcqhd', p, vb)
    return out.reshape(b, s, h * d).astype(q.dtype)


def chunk_spatial_gating(u, v, ln_g, ln_b, w_s, b_s):
    b, s, _ = u.shape
    nb = s // B_BLOCK
    v = layer_norm(v, ln_g, ln_b)
    vg = v.reshape(b, nb, B_BLOCK, B_GROUPS, B_GROUP_DIM)
    causal = jnp.tril(jnp.ones((B_BLOCK, B_BLOCK), dtype=bool))
    w = jnp.where(causal[None], w_s, jnp.zeros_like(w_s))
    f = jnp.einsum('gts,bnsgc->bntgc', w, vg) + b_s.T[None, None, :, :, None]
    return u * f.reshape(b, s, B_WIDTH).astype(u.dtype)


def attn_gmlp_mixer(h, w_in, rel_bias, ln_g, ln_b, w_s, b_s, w_out):
    b, s, _ = h.shape
    proj = h @ w_in
    q, k, v, zu, zv = jnp.split(
        proj, [A_WIDTH, 2 * A_WIDTH, 3 * A_WIDTH, 3 * A_WIDTH + B_WIDTH], axis=-1)
    heads = lambda t: t.reshape(b, s, A_HEADS, A_HEAD_DIM)
    a_out = chunk_band_attention(heads(q), heads(k), heads(v), rel_bias)
    b_out = chunk_spatial_gating(jax.nn.gelu(zu, approximate=False),
                                 jax.nn.gelu(zv, approximate=False),
                                 ln_g, ln_b, w_s, b_s)
    return jnp.concatenate([a_out, b_out], axis=-1) @ w_out


def gla_chunk_scan(q, k, v, log_a):
    b, s, h, dk = q.shape
    dv = v.shape[-1]
    nc = s // CHUNK

    def to_chunks(t):
        return t.reshape(b, nc, CHUNK, h, t.shape[-1]).transpose(1, 0, 3, 2, 4)

    causal = jnp.tril(jnp.ones((CHUNK, CHUNK), dtype=bool))[:, :, None]

    def step(state, inp):
        qc, kc, vc, lac = inp
        cum = jnp.cumsum(lac, axis=2)
        diff = cum[:, :, :, None, :] - cum[:, :, None, :, :]
        decay = jnp.exp(jnp.where(causal, diff, -jnp.inf))
        attn = jnp.einsum('bhid,bhjd,bhijd->bhij', qc, kc, decay)
        o = (jnp.einsum('bhij,bhje->bhie', attn, vc)
             + jnp.einsum('bhid,bhde->bhie', qc * jnp.exp(cum), state))
        last = cum[:, :, -1:, :]
        state = (jnp.exp(last[:, :, 0, :])[..., None] * state
                 + jnp.einsum('bhjd,bhje->bhde', kc * jnp.exp(last - cum), vc))
        return state, o

    s0 = jnp.zeros((b, h, dk, dv), jnp.float32)
    _, o = lax.scan(step, s0, (to_chunks(q), to_chunks(k), to_chunks(v), to_chunks(log_a)))
    return o.transpose(1, 0, 3, 2, 4).reshape(b, s, h, dv)


def gla_mixer(h, w_in, w_a2, b_a, norm_g, w_out):
    b, s, _ = h.shape
    proj = h @ w_in
    q, k, v, g, a_low = jnp.split(
        proj, [C_KEY_DIM, 2 * C_KEY_DIM, 2 * C_KEY_DIM + C_VAL_DIM,
               2 * C_KEY_DIM + 2 * C_VAL_DIM], axis=-1)
    log_a = jax.nn.log_sigmoid((a_low @ w_a2 + b_a).astype(jnp.float32)) / C_GATE_TAU
    heads = lambda t, d: t.reshape(b, s, C_HEADS, d).astype(jnp.float32)
    o = gla_chunk_scan(heads(q, C_DK) * (C_DK ** -0.5), heads(k, C_DK),
                       heads(v, C_DV), heads(log_a, C_DK))
    o = rms_norm(o, norm_g.reshape(C_HEADS, C_DV))
    o = o.reshape(b, s, C_VAL_DIM).astype(h.dtype) * jax.nn.silu(g)
    return o @ w_out


def swiglu(h, w_gate, w_up, w_down):
    return (jax.nn.silu(h @ w_gate) * (h @ w_up)) @ w_down


def setup_inputs(seed: int = 0) -> dict:
    key = jax.random.key(seed)
    ks = jax.random.split(key, 24)
    f32 = jnp.float32
    nrm = lambda k, shape, scale: jax.random.normal(k, shape, f32) * scale
    return {
        "x": nrm(ks[0], (BATCH, SEQ, D_MODEL), 1.0),
        "pre_mix_g": 1.0 + nrm(ks[1], (DEPTH, D_MODEL), 0.05),
        "post_mix_g": 1.0 + nrm(ks[2], (DEPTH, D_MODEL), 0.05),
        "pre_ffn_g": 1.0 + nrm(ks[3], (DEPTH, D_MODEL), 0.05),
        "post_ffn_g": 1.0 + nrm(ks[4], (DEPTH, D_MODEL), 0.05),
        "ab_w_in": nrm(ks[5], (N_EVEN, D_MODEL, AB_IN), D_MODEL ** -0.5),
        "a_rel_bias": nrm(ks[6], (N_EVEN, A_HEADS, 2 * A_MAX_REL + 1), 0.5),
        "b_ln_g": 1.0 + nrm(ks[7], (N_EVEN, B_WIDTH), 0.05),
        "b_ln_b": nrm(ks[8], (N_EVEN, B_WIDTH), 0.05),
        "b_w_s": nrm(ks[9], (N_EVEN, B_GROUPS, B_BLOCK, B_BLOCK), B_BLOCK ** -0.5),
        "b_b_s": 1.0 + nrm(ks[10], (N_EVEN, B_GROUPS, B_BLOCK), 0.1),
        "ab_w_out": nrm(ks[11], (N_EVEN, AB_MIX, D_MODEL), AB_MIX ** -0.5),
        "c_w_in": nrm(ks[12], (N_ODD, D_MODEL, C_IN), D_MODEL ** -0.5),
        "c_w_a2": nrm(ks[13], (N_ODD, C_GATE_RANK, C_KEY_DIM), C_GATE_RANK ** -0.5),
        "c_b_a": nrm(ks[14], (N_ODD, C_KEY_DIM), 0.1),
        "c_norm_g": 1.0 + nrm(ks[15], (N_ODD, C_VAL_DIM), 0.05),
        "c_w_out": nrm(ks[16], (N_ODD, C_VAL_DIM, D_MODEL), C_VAL_DIM ** -0.5),
        "ffn_w_gate": nrm(ks[17], (DEPTH, D_MODEL, D_FF), D_MODEL ** -0.5),
        "ffn_w_up": nrm(ks[18], (DEPTH, D_MODEL, D_FF), D_MODEL ** -0.5),
        "ffn_w_down": nrm(ks[19], (DEPTH, D_FF, D_MODEL), D_FF ** -0.5),
    }


def reference(x, pre_mix_g, post_mix_g, pre_ffn_g, post_ffn_g,
              ab_w_in, a_rel_bias, b_ln_g, b_ln_b, b_w_s, b_b_s, ab_w_out,
              c_w_in, c_w_a2, c_b_a, c_norm_g, c_w_out,
              ffn_w_gate, ffn_w_up, ffn_w_down):
    for i in range(DEPTH):
        j = i // 2
        h = rms_norm(x, pre_mix_g[i])
        if i % 2 == 0:
            m = attn_gmlp_mixer(h, ab_w_in[j], a_rel_bias[j], b_ln_g[j], b_ln_b[j],
                                b_w_s[j], b_b_s[j], ab_w_out[j])
        else:
            m = gla_mixer(h, c_w_in[j], c_w_a2[j], c_b_a[j], c_norm_g[j], c_w_out[j])
        x = x + rms_norm(m, post_mix_g[i])
        h = rms_norm(x, pre_ffn_g[i])
        x = x + rms_norm(swiglu(h, ffn_w_gate[i], ffn_w_up[i], ffn_w_down[i]), post_ffn_g[i])
    return x
```

```python
import contextlib
import numpy as np
import concourse.bass as bass
import concourse.mybir as mybir
from concourse.bass_utils import run_bass_kernel_spmd

F32 = mybir.dt.float32
BF16 = mybir.dt.bfloat16
ALU = mybir.AluOpType
AF = mybir.ActivationFunctionType

D = 1024
DFF = 2816
NFB = DFF // 128
EPS = 1e-6
ENGS = ("pe", "act", "dve", "pool", "sp")


class Buf:
    __slots__ = ("name", "excl", "last_w", "readers")

    def __init__(self, name, excl=False):
        self.name = name
        self.excl = excl
        self.last_w = None
        self.readers = []


class Op:
    __slots__ = ("eng", "fn", "deps", "dma", "signal", "ticket", "ndma", "semkey", "odeps", "cost", "table",
                 "seg", "fdep", "idx", "nun", "ready", "fin", "users")

    def __init__(self, eng, fn, dma):
        self.eng = eng
        self.fn = fn
        self.deps = []
        self.dma = dma
        self.signal = False
        self.ticket = None
        self.ndma = 0
        self.semkey = None
        self.odeps = []
        self.cost = 0.5
        self.table = None
        self.seg = 0
        self.fdep = None
        self.idx = 0


class _Dummy:
    def then_inc(self, *a, **k):
        return self


class CostEngine:
    def __init__(self, eng):
        self.eng = eng
        self.cost = 0.0
        self.table = None
        self.bytes = 0

    def _free(self, ap):
        n = 1
        for d in ap.shape[1:]:
            n *= d
        return n

    def __getattr__(self, name):
        def f(*a, **k):
            out = k.get("out", a[0] if a else None)
            cols = self._free(out) if out is not None and hasattr(out, "shape") else 1
            if name in ("matmul", "transpose"):
                lhs = k.get("lhsT", a[1] if len(a) > 1 else None)
                mult = 4.0 if (lhs is not None and lhs.dtype == F32) else 1.0
                self.cost += mult * max(cols, 64) / 2400.0 + 0.035
            elif name == "dma_start":
                src = k.get("in_", a[1] if len(a) > 1 else None)
                nb = out.shape[0] * cols * (4 if out.dtype == F32 else 2)
                self.bytes += nb
                self.cost += 0.05
            elif self.eng == "act":
                self.cost += 0.25 + cols / 1200.0
                fn = k.get("func", None)
                if fn in (AF.Exp, AF.Ln):
                    self.table = "A"
                elif fn == AF.Gelu:
                    self.table = "B"
                elif fn == AF.Silu:
                    self.table = "C"
            elif self.eng == "dve":
                self.cost += 0.13 + cols / 960.0
            else:
                self.cost += 0.2 + cols / 480.0
            return _Dummy()
        return f


class Prog:
    def __init__(self, nc):
        self.nc = nc
        self.ops = []
        self.pools = {}
        self.last = {e: None for e in ENGS}
        self.seg = 0
        self.pending_fence = {e: None for e in ENGS}
        self.reorder = True
        self.pe_mix = 2

    def _add(self, eng, fn, reads, writes, dma):
        op = Op(eng, fn, dma)
        deps = []

        def need(o, kind):
            if o is None or o is op:
                return
            if o.eng == eng and not dma and not o.dma:
                if eng == "pe" or kind != "raw":
                    op.odeps.append(o)
                    return
            deps.append(o)

        for b in reads:
            need(b.last_w, "raw")
            if b.excl:
                for r in b.readers:
                    need(r, "raw")
        for b in writes:
            need(b.last_w, "waw")
            for r in b.readers:
                need(r, "war")
        for b in reads:
            if b.excl:
                b.last_w = op
                b.readers = []
            else:
                b.readers.append(op)
        for b in writes:
            b.last_w = op
            b.readers = []
        if self.pending_fence[eng] is not None:
            op.fdep = self.pending_fence[eng]
            self.pending_fence[eng] = None
        seen = set()
        for d in deps:
            if id(d) not in seen:
                seen.add(id(d))
                op.deps.append(d)
                d.signal = True
        op.seg = self.seg
        op.idx = len(self.ops)
        if dma:
            lastd = self.last.get(("dma", eng, dma))
            if lastd is not None:
                op.odeps.append(lastd)
            self.last[("dma", eng, dma)] = op
        ce = CostEngine(eng)
        fn(ce)
        op.cost = ce.cost if not dma else (2.0 + ce.bytes / 120000.0)
        op.table = ce.table
        self.ops.append(op)
        return op

    def op(self, eng, fn, reads=(), writes=()):
        return self._add(eng, fn, list(reads), list(writes), False)

    def dma(self, eng, fn, n, reads=(), writes=(), pool="ld", K=6):
        op = self._add(eng, fn, list(reads), list(writes), pool)
        op.ndma = n
        pl = self.pools.setdefault(pool, [K, 0, {}])
        i = pl[1]
        pl[1] += 1
        slot = i % pl[0]
        prev = pl[2].get(slot)
        if prev is not None and prev not in op.deps:
            op.deps.append(prev)
        pl[2][slot] = op
        op.semkey = (pool, slot)
        return op

    def fence(self):
        for e in ENGS:
            self.pending_fence[e] = self.seg
        self.seg += 1

    def schedule(self):
        order = {e: [] for e in ENGS}
        self.seg_tail = {}
        nseg = self.seg + 1
        by_seg = [[] for _ in range(nseg)]
        for op in self.ops:
            by_seg[op.seg].append(op)
        for k, ops in enumerate(by_seg):
            for e in ENGS:
                ops_e = [o for o in ops if o.eng == e]
                if k > 0 and ops_e:
                    assert ops_e[0].fdep == k - 1, (e, k, ops_e[0].fdep)
                    for o in ops_e[1:]:
                        o.odeps.append(ops_e[0])
            if not self.reorder:
                for op in ops:
                    order[op.eng].append(op)
            else:
                self._sched_segment(ops, order)
            tail = []
            for e in ENGS:
                comp = [o for o in order[e] if o.seg == k and not o.dma]
                if comp:
                    tail.append(comp[-1])
            tail += [o for o in ops if o.dma]
            self.seg_tail[k] = tail
        return order

    def _sched_segment(self, ops, order):
        import heapq
        inseg = set(id(o) for o in ops)
        for o in ops:
            o.users = []
            o.nun = 0
            o.ready = 0.0
            o.fin = None
        for o in ops:
            for d in o.deps + o.odeps:
                if id(d) in inseg:
                    d.users.append(o)
                    o.nun += 1
        main = {e: [] for e in ENGS}
        avail = {e: [] for e in ENGS}
        free = {e: 0.0 for e in ENGS}
        cur_table = None
        small_run = [0]
        for o in ops:
            if o.nun == 0:
                heapq.heappush(main[o.eng], (o.ready, o.idx, o))
        nleft = len(ops)
        while nleft:
            best = None
            for e in ENGS:
                m, a = main[e], avail[e]
                while m and m[0][0] <= free[e] + 1e-9:
                    r, ix, o = heapq.heappop(m)
                    heapq.heappush(a, (ix, o))
                if a:
                    cand = (free[e], a[0][0], e, True)
                elif m:
                    cand = (m[0][0], m[0][1], e, False)
                else:
                    continue
                if best is None or cand[:2] < best[:2]:
                    best = cand
            st, _, e, from_avail = best
            if from_avail:
                a = avail[e]
                if e == "act" and len(a) > 1:
                    pick = None
                    small = heapq.nsmallest(6, a)
                    for ix, o in small:
                        if o.table is None or o.table == cur_table:
                            pick = (ix, o)
                            break
                    if pick is None or pick is small[0]:
                        ix, o = heapq.heappop(a)
                    else:
                        a.remove(pick)
                        heapq.heapify(a)
                        ix, o = pick
                elif e == "pe" and len(a) > 1 and self.pe_mix and small_run[0] >= self.pe_mix:
                    pick = None
                    for ix, o in a:
                        if o.cost >= 1.0 and (pick is None or ix < pick[0]):
                            pick = (ix, o)
                    if pick is None:
                        ix, o = heapq.heappop(a)
                    else:
                        a.remove(pick)
                        heapq.heapify(a)
                        ix, o = pick
                else:
                    ix, o = heapq.heappop(a)
            else:
                r, ix, o = heapq.heappop(main[e])
            if e == "pe":
                small_run[0] = small_run[0] + 1 if o.cost < 1.0 else 0
            if e == "act" and o.table is not None:
                if o.table != cur_table:
                    st += 1.3
                cur_table = o.table
            o.fin = st + o.cost
            free[e] = st + (0.06 if o.dma else o.cost)
            order[e].append(o)
            nleft -= 1
            for u in o.users:
                u.nun -= 1
                if u.eng == e and not o.dma and (o not in u.deps):
                    r = st
                else:
                    r = o.fin + (0.3 if (u.eng == e and not o.dma) else 1.2)
                if r > u.ready:
                    u.ready = r
                if u.nun == 0:
                    heapq.heappush(main[u.eng], (u.ready, u.idx, u))

    def emit(self, final_wait_ops=()):
        nc = self.nc
        with contextlib.ExitStack() as st:
            esem = {e: st.enter_context(nc.semaphore("s_" + e)) for e in ENGS}
            dsem = {}
            dcnt = {}
            for op in self.ops:
                if op.dma and op.semkey not in dsem:
                    dsem[op.semkey] = st.enter_context(nc.semaphore("d_%s%d" % op.semkey))
                    dcnt[op.semkey] = 0
            block = st.enter_context(nc.Block())
            order = self.schedule()
            for op in self.ops:
                if op.fdep is not None:
                    for o in self.seg_tail[op.fdep]:
                        if o is not op and not (o.eng == op.eng and not o.dma and not op.dma):
                            op.deps.append(o)
                            o.signal = True
            ecnt = {e: 0 for e in ENGS}
            for e in ENGS:
                for op in order[e]:
                    if op.dma:
                        dcnt[op.semkey] += 16 * op.ndma
                        op.ticket = (dsem[op.semkey], dcnt[op.semkey])
                    elif op.signal:
                        ecnt[op.eng] += 1
                        op.ticket = (esem[op.eng], ecnt[op.eng])
            final = list(final_wait_ops)

            def run(engname):
                def body(eng):
                    waited = {}
                    for op in order[engname]:
                        for d in op.deps:
                            sem, val = d.ticket
                            k = id(sem)
                            if waited.get(k, 0) < val:
                                eng.wait_ge(sem, val)
                                waited[k] = val
                        res = op.fn(eng)
                        if op.dma:
                            assert len(res) == op.ndma, (len(res), op.ndma)
                            for ins in res:
                                ins.then_inc(op.ticket[0], 16)
                        elif op.signal:
                            res.then_inc(op.ticket[0], 1)
                    if engname == "sp":
                        for d in final:
                            sem, val = d.ticket
                            eng.wait_ge(sem, val)
                return body

            block.tensor(run("pe"))
            block.scalar(run("act"))
            block.vector(run("dve"))
            block.gpsimd(run("pool"))
            block.sync(run("sp"))


class Arena:
    def __init__(self, nc, nbytes):
        self.t = nc.alloc_sbuf_tensor("arena", [128, nbytes // 2], BF16)
        self.nbytes = nbytes
        self.off = 0

    def alloc(self, shape, dtype):
        n = int(np.prod(shape))
        sz = n * (4 if dtype == F32 else 2)
        self.off = (self.off + 31) // 32 * 32
        assert self.off + sz <= self.nbytes, ("arena overflow", self.off, sz, self.nbytes)
        o2 = self.off // 2
        ap = self.t[:, o2:o2 + sz // 2]
        if dtype == F32:
            ap = ap.bitcast(F32)
        self.off += sz
        if len(shape) == 2:
            ap = ap.rearrange("p (a b) -> p a b", a=shape[0])
        elif len(shape) == 3:
            ap = ap.rearrange("p (a b c) -> p a b c", a=shape[0], b=shape[1])
        return ap


class Ctx:
    pass


def dbg(c, name, ap, bufs, dtype=F32):
    if not getattr(c, "debug", False):
        return
    shp = list(ap.shape)
    o = c.nc.dram_tensor("dbg_" + name, shp, dtype, kind="ExternalOutput").ap()
    c.dbg_ops.append(c.P.dma("sp", lambda e: [e.dma_start(out=o, in_=ap)], 1, reads=list(bufs), pool="dbg", K=1))


def load_weight(c, dst, src, nk, cols, name, gcol=None, chunk=None):
    P = c.P
    bufs = []
    for k in range(nk):
        b = Buf("%s_k%d" % (name, k))
        bufs.append(b)
        P.dma("pool", (lambda k: lambda e: [e.dma_start(out=dst[:, k, :], in_=src[k * 128:(k + 1) * 128, :])])(k),
              1, writes=[b], pool="w", K=6)
        if gcol is not None:
            P.op("dve", (lambda k: lambda e: e.tensor_scalar(out=dst[:, k, :], in0=dst[:, k, :], scalar1=gcol[:, k:k + 1],
                                                             scalar2=None, op0=ALU.mult))(k),
                 reads=[b, c.Bconst], writes=[b])
    return bufs


def norm_transpose_group(c, src_dram, g, hT, hTbuf):
    for t in range(4):
        norm_transpose_tile(c, src_dram, g, t, hT, hTbuf)


def norm_transpose_tile(c, src_dram, g, t, hT, hTbuf):
    hb = norm_tile(c, src_dram, g, t)
    transpose_tile(c, hb, t, hT, hTbuf)


def norm_tile(c, src_dram, g, t):
    P = c.P
    T = 4 * g + t
    s = c.xi % 3
    c.xi += 1
    xt, xb = c.xt[s], c.xtB[s]
    P.dma("sp", (lambda xt, T: lambda e: [e.dma_start(out=xt, in_=src_dram[T * 128:(T + 1) * 128, :])])(xt, T),
          1, reads=[c.Bstream[T]], writes=[xb], pool="ldx", K=3)
    ss, ssB = c.ss[s], c.ssB[s]
    P.op("act", (lambda xt, ss: lambda e: e.activation(out=c.junk, in_=xt, func=AF.Square, accum_out=ss[:, 0:1]))(xt, ss),
         reads=[xb], writes=[ssB, c.Bjunk])
    P.op("act", (lambda ss: lambda e: e.activation(out=ss[:, 1:2], in_=ss[:, 0:1], func=AF.Ln, scale=1.0 / D, bias=c.epsc))(ss),
         reads=[ssB, c.Bconst], writes=[ssB])
    P.op("act", (lambda ss: lambda e: e.activation(out=ss[:, 2:3], in_=ss[:, 1:2], func=AF.Exp, scale=-0.5))(ss),
         reads=[ssB], writes=[ssB])
    hs = c.hbi % 2
    c.hbi += 1
    hb, hbB = c.hb[hs], c.hbB[hs]
    P.op("dve", (lambda xt, ss, hb: lambda e: e.tensor_scalar(out=hb, in0=xt, scalar1=ss[:, 2:3], scalar2=None,
                                                              op0=ALU.mult))(xt, ss, hb),
         reads=[xb, ssB], writes=[hbB])
    return hb, hbB


def transpose_tile(c, hbp, t, hT, hTbuf):
    P = c.P
    hb, hbB = hbp
    bk, bB = c.bank()
    bkb = bk.bitcast(BF16)

    def tr(e, hb=hb, bkb=bkb):
        r = None
        for k in range(8):
            r = e.transpose(bkb[:, k * 128:(k + 1) * 128], hb[:, k * 128:(k + 1) * 128], c.ident)
        return r
    P.op("pe", tr, reads=[hbB, c.Bconst], writes=[bB])
    P.op("act", (lambda bkb, t: lambda e: e.copy(out=hT[:, :, t * 128:(t + 1) * 128],
                                                 in_=bkb.rearrange("p (k n) -> p k n", k=8)))(bkb, t),
         reads=[bB], writes=[hTbuf])


def post_norm_residual(c, g, t, banks, bankBs, gpost, src_dram, dst_dram):
    P = c.P
    T = 4 * g + t
    s = c.oi % c.n_ot
    c.oi += 1
    ps, psB = c.pss[s], c.pssB[s]
    for hf in range(2):
        P.op("act", (lambda hf, ps: lambda e: e.activation(out=c.junk[:, 0:512], in_=banks[hf], func=AF.Square,
                                                           accum_out=ps[:, hf:hf + 1]))(hf, ps),
             reads=[bankBs[hf]], writes=[psB, c.Bjunk])
    P.op("dve", (lambda ps: lambda e: e.tensor_tensor(out=ps[:, 2:3], in0=ps[:, 0:1], in1=ps[:, 1:2], op=ALU.add))(ps),
         reads=[psB], writes=[psB])
    P.op("act", (lambda ps: lambda e: e.activation(out=ps[:, 3:4], in_=ps[:, 2:3], func=AF.Ln, scale=1.0 / D, bias=c.epsc))(ps),
         reads=[psB, c.Bconst], writes=[psB])
    P.op("act", (lambda ps: lambda e: e.activation(out=ps[:, 4:5], in_=ps[:, 3:4], func=AF.Exp, scale=-0.5))(ps),
         reads=[psB], writes=[psB])
    ot, otB = c.ot[s], c.otB[s]
    xr, xrB = c.xr[s], c.xrB[s]
    P.dma("sp", (lambda xr, T: lambda e: [e.dma_start(out=xr, in_=src_dram[T * 128:(T + 1) * 128, :])])(xr, T),
          1, reads=[c.Bstream[T]], writes=[xrB], pool="ldr", K=2)
    for hf in range(2):
        P.op("dve", (lambda hf, ps, ot: lambda e: e.scalar_tensor_tensor(
            out=ot[:, hf * 512:(hf + 1) * 512], in0=banks[hf], scalar=ps[:, 4:5], in1=gpost[:, hf * 512:(hf + 1) * 512],
            op0=ALU.mult, op1=ALU.mult))(hf, ps, ot), reads=[bankBs[hf], psB, c.Bconst], writes=[otB])
    P.op("pool", (lambda ot, xr: lambda e: e.tensor_tensor(out=ot, in0=ot, in1=xr, op=ALU.add))(ot, xr),
         reads=[otB, xrB], writes=[otB])
    return P.dma("pool", (lambda ot, T: lambda e: [e.dma_start(out=dst_dram[T * 128:(T + 1) * 128, :], in_=ot)])(ot, T),
                 1, reads=[otB], writes=[c.Bstream[T]], pool="st", K=2)


def stage_l0_mixer(c, src_dram, dst_dram, NG):
    P, A, nc = c.P, c.A, c.nc
    I = c.inp
    A.off = c.stage_base
    Win = A.alloc([8, 2560], BF16)
    Wout = A.alloc([8, 1024], BF16)
    hT = [A.alloc([8, 512], BF16) for _ in range(2)]
    hTB = [Buf("hT0"), Buf("hT1")]
    qTs2 = [A.alloc([4, 512], BF16) for _ in range(2)]
    qTBs2 = [Buf("qT0"), Buf("qT1")]
    kR = A.alloc([4, 1536], BF16)
    kRB = [Buf("kR0"), Buf("kR1"), Buf("kR2")]
    vR = A.alloc([12, 8, 65], BF16)
    vRB = [Buf("vR0"), Buf("vR1"), Buf("vR2")]
    Ep = A.alloc([8, 5, 128], BF16)
    uT = A.alloc([4, 512], BF16)
    uTB = Buf("uT")
    vln = A.alloc([4, 512], BF16)
    vlnB = [Buf("vln%d" % t) for t in range(4)]
    vg = [A.alloc([512], F32) for _ in range(2)]
    vgB = [Buf("vg0"), Buf("vg1")]
    vt = [A.alloc([512], F32) for _ in range(2)]
    vtB = [Buf("vt0"), Buf("vt1")]
    st = [A.alloc([16], F32) for _ in range(2)]
    stB = [Buf("st0"), Buf("st1")]
    wsT = A.alloc([4, 128], BF16)
    wsF = c.xr[0][:, 0:512].rearrange("p (a b) -> p a b", a=4)
    tri = A.alloc([128], F32)
    bsb = A.alloc([4, 128], F32)
    lng = A.alloc([512], F32)
    lnb = A.alloc([512], F32)
    gpost = A.alloc([1024], F32)
    gcol = A.alloc([8], F32)
    PP = [A.alloc([5, 4, 128], BF16) for _ in range(2)]
    PPB = [[Buf("PP%d_a" % i), Buf("PP%d_b" % i)] for i in range(2)]
    aout = A.alloc([4, 512], BF16)
    aoutB = Buf("aout")
    rden = [A.alloc([4], F32) for _ in range(2)]
    rdenB = [Buf("rden0"), Buf("rden1")]
    catT = A.alloc([8, 512], BF16)
    catTB = Buf("catTa")
    catTBb = Buf("catTb")
    gt, gtB = vt, vtB
    valid = A.alloc([c.NTILES], F32)

    def ld_consts(e):
        return [e.dma_start(out=wsF, in_=I["wsT"]), e.dma_start(out=tri, in_=I["tri"]),
                e.dma_start(out=bsb, in_=I["bsb"]), e.dma_start(out=lng, in_=I["lng"]),
                e.dma_start(out=lnb, in_=I["lnb"]), e.dma_start(out=gpost, in_=I["gpost0"]),
                e.dma_start(out=gcol, in_=I["gcol0"]),
                e.dma_start(out=valid, in_=I["valid"])]
    P.dma("sp", ld_consts, 8, writes=[c.Bconst, c.xrB[0]], pool="ldc", K=1)
    P.op("dve", lambda e: e.tensor_tensor(out=wsT, in0=wsF, in1=tri.unsqueeze(1).to_broadcast([128, 4, 128]), op=ALU.mult),
         reads=[c.Bconst, c.xrB[0]], writes=[c.Bconst])
    for h8 in range(8):
        stg = c.xt[h8 % 3][:, 0:640].rearrange("p (a b) -> p a b", a=5)
        P.dma("sp", (lambda h8, stg: lambda e: [e.dma_start(out=stg, in_=I["biasT"][:, h8, :, :])])(h8, stg), 1,
              writes=[c.xtB[h8 % 3]], pool="ldx", K=3)
        P.op("act", (lambda h8, stg: lambda e: e.activation(out=Ep[:, h8, :, :], in_=stg, func=AF.Exp))(h8, stg),
             reads=[c.xtB[h8 % 3], c.Bconst], writes=[c.Bconst])
    P.op("act", lambda e: e.memzero(Ep[64:128, :, 0, 0:64]), reads=[c.Bconst], writes=[c.Bconst])
    P.op("act", lambda e: e.memzero(Ep[0:64, :, 4, 64:128]), reads=[c.Bconst], writes=[c.Bconst])
    WinB = load_weight(c, Win, I["ab_w_in"], 8, 2560, "win", gcol=gcol)
    WoutB = load_weight(c, Wout, I["ab_w_out"], 8, 1024, "wout")

    def proj_fm(h, hB, col0, evac):
        bk, bB = c.bank()

        def mm(e):
            r = None
            for k in range(8):
                r = e.matmul(bk, lhsT=Win[:, k, col0:col0 + 128], rhs=h[:, k, :], start=(k == 0), stop=(k == 7))
            return r
        P.op("pe", mm, reads=[hB] + WinB, writes=[bB])
        evac(bk, bB)

    def proj_tm(h, hB, t, col0, evac):
        bk, bB = c.bank()

        def mm(e):
            r = None
            for k in range(8):
                r = e.matmul(bk, lhsT=h[:, k, t * 128:(t + 1) * 128], rhs=Win[:, k, col0:col0 + 512],
                             start=(k == 0), stop=(k == 7))
            return r
        P.op("pe", mm, reads=[hB] + WinB, writes=[bB])
        evac(bk, bB)

    fin = []
    for g in range(NG):
        h, hB = hT[g % 2], hTB[g % 2]
        norm_transpose_group(c, src_dram, g, h, hB)
        half = g % 3
        qT, qTB = qTs2[g % 2], qTBs2[g % 2]
        for ob in range(4):
            proj_fm(h, hB, ob * 128, lambda bk, bB, ob=ob, qT=qT, qTB=qTB: P.op(
                "act", lambda e: e.activation(out=qT[:, ob, :], in_=bk, func=AF.Copy, scale=0.125), reads=[bB], writes=[qTB]))
        for ob in range(4):
            proj_fm(h, hB, 512 + ob * 128, lambda bk, bB, ob=ob, half=half: P.op(
                "act", lambda e: e.activation(out=kR[:, ob, half * 512:(half + 1) * 512], in_=bk, func=AF.Copy), reads=[bB], writes=[kRB[half]]))
        for t in range(4):
            T = 4 * g + t
            sl = T % 12

            def ev(bk, bB, T=T, sl=sl, half=half):
                P.op("dve", lambda e: e.tensor_scalar(out=vR[:, sl, :, 0:64], in0=bk.rearrange("p (h d) -> p h d", h=8),
                                                      scalar1=valid[:, T:T + 1], scalar2=None, op0=ALU.mult),
                     reads=[bB, c.Bconst], writes=[vRB[half]])
                P.op("dve", lambda e: e.tensor_copy(out=vR[:, sl, :, 64:65],
                                                    in_=valid[:, T:T + 1].unsqueeze(1).to_broadcast([128, 8, 1])),
                     reads=[c.Bconst], writes=[vRB[half]])
            proj_tm(h, hB, t, 1024, ev)
        for ob in range(4):
            proj_fm(h, hB, 1536 + ob * 128, lambda bk, bB, ob=ob: P.op(
                "act", lambda e: e.activation(out=uT[:, ob, :], in_=bk, func=AF.Gelu), reads=[bB], writes=[uTB]))
        for t in range(4):
            s = t % 2

            def ev(bk, bB, t=t, s=s):
                P.op("act", lambda e: e.activation(out=vg[s], in_=bk, func=AF.Gelu), reads=[bB], writes=[vgB[s]])
                P.op("dve", lambda e: e.bn_stats(out=st[s][:, 0:6], in_=vg[s]), reads=[vgB[s]], writes=[stB[s]])
                P.op("dve", lambda e: e.bn_aggr(out=st[s][:, 6:8], in_=st[s][:, 0:6]), reads=[stB[s]], writes=[stB[s]])
                P.op("act", lambda e: e.activation(out=st[s][:, 9:10], in_=st[s][:, 7:8], func=AF.Ln, bias=c.epsc),
                     reads=[stB[s], c.Bconst], writes=[stB[s]])
                P.op("act", lambda e: e.activation(out=st[s][:, 8:9], in_=st[s][:, 9:10], func=AF.Exp, scale=-0.5),
                     reads=[stB[s]], writes=[stB[s]])
                P.op("dve", lambda e: e.tensor_scalar(out=vt[s], in0=vg[s], scalar1=st[s][:, 6:7], scalar2=st[s][:, 8:9],
                                                      op0=ALU.subtract, op1=ALU.mult), reads=[vgB[s], stB[s]], writes=[vtB[s]])
                P.op("pool", lambda e: e.tensor_tensor(out=vt[s], in0=vt[s], in1=lng, op=ALU.mult),
                     reads=[vtB[s], c.Bconst], writes=[vtB[s]])
                P.op("pool", lambda e: e.tensor_tensor(out=vln[:, t, :], in0=vt[s], in1=lnb, op=ALU.add),
                     reads=[vtB[s], c.Bconst], writes=[vlnB[t]])
            proj_tm(h, hB, t, 2048, ev)
        for t in range(4):
            bk, bB = c.bank()

            def mm(e, t=t, bk=bk):
                r = None
                for grp in range(4):
                    r = e.matmul(bk[:, grp * 128:(grp + 1) * 128], lhsT=vln[:, t, grp * 128:(grp + 1) * 128],
                                 rhs=wsT[:, grp, :], start=True, stop=True)
                return r
            P.op("pe", mm, reads=[vlnB[t], c.Bconst], writes=[bB])
            s = t % 2
            P.op("dve", (lambda bk, s: lambda e: e.tensor_tensor(out=gt[s], in0=bk, in1=bsb.rearrange("p a b -> p (a b)"),
                                                                 op=ALU.add))(bk, s),
                 reads=[bB, c.Bconst], writes=[gtB[s]])
            P.op("dve", (lambda s, t: lambda e: e.tensor_tensor(out=catT[:, 4:8, t * 128:(t + 1) * 128],
                                                                in0=gt[s].rearrange("p (a b) -> p a b", a=4),
                                                                in1=uT[:, :, t * 128:(t + 1) * 128], op=ALU.mult))(s, t),
                 reads=[gtB[s], uTB], writes=[catTBb])
        SPLIT = 3

        def emit_scores(h8):
            hp, hh = h8 // 2, h8 % 2
            pr = slice(hh * 64, hh * 64 + 64)
            for part, (ra, rb) in enumerate(((0, SPLIT), (SPLIT, 5))):
                def sm(e, ra=ra, rb=rb, pr=pr, hp=hp, g=g, qT=qT):
                    r = None
                    for rr in range(ra, rb):
                        for qb in range(4):
                            Tk = 4 * g + qb - rr
                            if Tk < 0:
                                continue
                            slk = Tk % 12
                            r = e.matmul(c.banks[rr][:, qb * 128:(qb + 1) * 128], lhsT=kR[pr, hp, slk * 128:(slk + 1) * 128],
                                         rhs=qT[pr, hp, qb * 128:(qb + 1) * 128], start=True, stop=True)
                    return r
                if g == 0 and ra > 3:
                    continue
                kth = sorted(set(((4 * g + qb - rr) // 4) % 3 for rr in range(ra, rb) for qb in range(4) if 4 * g + qb - rr >= 0))
                P.op("pe", sm, reads=[kRB[x] for x in kth] + [qTB], writes=[c.bankB[rr] for rr in range(ra, rb)])

        def emit_softmax(h8):
            pp = h8 % 2
            for part, (ra, rb) in enumerate(((0, SPLIT), (SPLIT, 5))):
                if g == 0 and ra > 3:
                    continue
                src = c.psall[:, ra * 512:rb * 512].rearrange("p (a b c) -> p a b c", a=rb - ra, b=4)
                P.op("act", (lambda src, pp, ra, rb: lambda e: e.activation(out=PP[pp][:, ra:rb, :, :], in_=src, func=AF.Exp))(src, pp, ra, rb),
                     reads=[c.bankB[rr] for rr in range(ra, rb)], writes=[PPB[pp][part]])
                P.op("dve", (lambda pp, ra, rb, h8: lambda e: e.tensor_tensor(
                    out=PP[pp][:, ra:rb, :, :], in0=PP[pp][:, ra:rb, :, :],
                    in1=Ep[:, h8, ra:rb, :].unsqueeze(2).to_broadcast([128, rb - ra, 4, 128]), op=ALU.mult))(pp, ra, rb, h8),
                    reads=[PPB[pp][part], c.Bconst], writes=[PPB[pp][part]])

        def emit_pv(h8):
            pp = h8 % 2
            ib = 5 + (h8 % 2)
            bk, bB = c.banks[ib], c.bankB[ib]

            def pv(e, bk=bk, h8=h8, pp=pp, g=g):
                r = None
                for qb in range(4):
                    rs_ = [rr for rr in range(5) if 4 * g + qb - rr >= 0]
                    for i, rr in enumerate(rs_):
                        Tk = 4 * g + qb - rr
                        r = e.matmul(bk[:, qb * 128:qb * 128 + 65], lhsT=PP[pp][:, rr, qb, :],
                                     rhs=vR[:, Tk % 12, h8, :], start=(i == 0), stop=(i == len(rs_) - 1))
                return r
            vth = [vRB[g % 3]] + ([vRB[(g - 1) % 3]] if g > 0 else [])
            P.op("pe", pv, reads=PPB[pp] + vth, writes=[bB])
            rs = h8 % 2
            bk3 = bk.rearrange("p (a b) -> p a b", a=4)
            P.op("dve", (lambda bk3, rs: lambda e: e.tensor_scalar(out=rden[rs], in0=bk3[:, :, 64], scalar1=1e-20, scalar2=None,
                                                                   op0=ALU.max))(bk3, rs), reads=[bB], writes=[rdenB[rs]])
            P.op("dve", (lambda rs: lambda e: e.reciprocal(out=rden[rs], in_=rden[rs]))(rs), reads=[rdenB[rs]], writes=[rdenB[rs]])
            P.op("dve", (lambda bk3, rs, h8: lambda e: e.tensor_tensor(
                out=aout[:, :, h8 * 64:(h8 + 1) * 64], in0=bk3[:, :, 0:64],
                in1=rden[rs].unsqueeze(2).to_broadcast([128, 4, 64]), op=ALU.mult))(bk3, rs, h8),
                reads=[bB, rdenB[rs]], writes=[aoutB])

        emit_scores(0)
        emit_softmax(0)
        for h8 in range(8):
            if h8 < 7:
                emit_scores(h8 + 1)
                emit_softmax(h8 + 1)
            emit_pv(h8)
        if g == 0:
            dbg(c, "ss", c.ss[0], [c.ssB[0]])
            dbg(c, "hT", h, [hB], BF16)
            dbg(c, "uT", uT, [uTB], BF16)
            dbg(c, "vln", vln, vlnB, BF16)
            dbg(c, "Ep", Ep, [c.Bconst], BF16)
            dbg(c, "aout", aout, [aoutB], BF16)
        for hf in range(2):
            bk, bB = c.bank()
            bkb = bk.bitcast(BF16)

            def tr(e, bkb=bkb, hf=hf):
                r = None
                for fb in range(2 * hf, 2 * hf + 2):
                    for t in range(4):
                        i = (fb - 2 * hf) * 4 + t
                        r = e.transpose(bkb[:, i * 128:(i + 1) * 128], aout[:, t, fb * 128:(fb + 1) * 128], c.ident)
                return r
            P.op("pe", tr, reads=[aoutB, c.Bconst], writes=[bB])
            P.op("act", (lambda bkb, hf: lambda e: e.copy(out=catT[:, 2 * hf:2 * hf + 2, :],
                                                          in_=bkb.rearrange("p (a b) -> p a b", a=2)))(bkb, hf),
                 reads=[bB], writes=[catTB])
        for t in range(4):
            bks = [c.bank(), c.bank()]
            for hf in range(2):
                bk = bks[hf][0]

                def mm(e, bk=bk, hf=hf, t=t):
                    r = None
                    for k in range(8):
                        r = e.matmul(bk, lhsT=catT[:, k, t * 128:(t + 1) * 128], rhs=Wout[:, k, hf * 512:(hf + 1) * 512],
                                     start=(k == 0), stop=(k == 7))
                    return r
                P.op("pe", mm, reads=[catTB, catTBb] + WoutB, writes=[bks[hf][1]])
            fin.append(post_norm_residual(c, g, t, [bks[0][0], bks[1][0]], [bks[0][1], bks[1][1]], gpost, src_dram, dst_dram))
        if g == 0:
            dbg(c, "catT", catT, [catTB, catTBb], BF16)
    return fin


def stage_ffn(c, L, src_dram, dst_dram, g0, NG):
    P, A = c.P, c.A
    I = c.inp
    A.off = c.stage_base_small
    c.n_ot = 1
    Wg = A.alloc([8, DFF], BF16)
    Wu = A.alloc([8, DFF], BF16)
    Wd = A.alloc([NFB, 1024], BF16)
    hTs = [A.alloc([8, 512], BF16) for _ in range(2)]
    hTBs = [Buf("f_hT0"), Buf("f_hT1")]
    actT = A.alloc([NFB, 512], BF16)
    actB = [Buf("f_act%d" % i) for i in range(NFB)]
    sg = [A.alloc([512], BF16) for _ in range(2)]
    sgB = [Buf("f_sg0"), Buf("f_sg1")]
    gpost = A.alloc([1024], F32)
    gcol = A.alloc([8], F32)
    P.dma("sp", lambda e: [e.dma_start(out=gpost, in_=I["gpostf%d" % L]), e.dma_start(out=gcol, in_=I["gcolf%d" % L])], 2,
          writes=[c.Bconst], pool="ldc", K=1)
    WgB = load_weight(c, Wg, I["ffn_w_gate%d" % L], 8, DFF, "wg", gcol=gcol)
    WuB = load_weight(c, Wu, I["ffn_w_up%d" % L], 8, DFF, "wu", gcol=gcol)
    WdB = load_weight(c, Wd, I["ffn_w_down%d" % L], NFB, 1024, "wd")
    fin = []
    norm_transpose_group(c, src_dram, g0, hTs[0], hTBs[0])
    for g in range(g0, g0 + NG):
        hT, hTB = hTs[(g - g0) % 2], hTBs[(g - g0) % 2]
        for fb in range(NFB):
            if g + 1 < g0 + NG and fb in (1, 6, 11, 16):
                pend_hb = norm_tile(c, src_dram, g + 1, (fb - 1) // 5)
            if g + 1 < g0 + NG and fb in (5, 10, 15, 20):
                transpose_tile(c, pend_hb, (fb - 5) // 5, hTs[(g + 1 - g0) % 2], hTBs[(g + 1 - g0) % 2])
            bg, bgB = c.bank()
            bu, buB = c.bank()

            def mm(e, fb=fb, bg=bg, bu=bu, hT=hT):
                r = None
                for k in range(8):
                    r = e.matmul(bg, lhsT=Wg[:, k, fb * 128:(fb + 1) * 128], rhs=hT[:, k, :], start=(k == 0), stop=(k == 7))
                for k in range(8):
                    r = e.matmul(bu, lhsT=Wu[:, k, fb * 128:(fb + 1) * 128], rhs=hT[:, k, :], start=(k == 0), stop=(k == 7))
                return r
            P.op("pe", mm, reads=[hTB] + WgB + WuB, writes=[bgB, buB])
            s = fb % 2
            P.op("act", (lambda bg, s: lambda e: e.activation(out=sg[s], in_=bg, func=AF.Silu))(bg, s), reads=[bgB], writes=[sgB[s]])
            P.op("dve", (lambda bu, s, fb: lambda e: e.tensor_tensor(out=actT[:, fb, :], in0=bu, in1=sg[s], op=ALU.mult))(bu, s, fb),
                 reads=[buB, sgB[s]], writes=[actB[fb]])
        for t in range(4):
            bks = [c.bank(), c.bank()]
            for hf in range(2):
                bk = bks[hf][0]

                def mm2(e, bk=bk, hf=hf, t=t):
                    r = None
                    for fb in range(NFB):
                        r = e.matmul(bk, lhsT=actT[:, fb, t * 128:(t + 1) * 128], rhs=Wd[:, fb, hf * 512:(hf + 1) * 512],
                                     start=(fb == 0), stop=(fb == NFB - 1))
                    return r
                P.op("pe", mm2, reads=actB + WdB, writes=[bks[hf][1]])
            fin.append(post_norm_residual(c, g, t, [bks[0][0], bks[1][0]], [bks[0][1], bks[1][1]], gpost, src_dram, dst_dram))
    c.n_ot = 2
    return fin


def stage_l1_mixer(c, src_dram, dst_dram, NG, g_own):
    P, A = c.P, c.A
    I = c.inp
    A.off = c.stage_base
    Win = A.alloc([8, 3088], BF16)
    Wout = A.alloc([8, 1024], BF16)
    hT = [A.alloc([8, 512], BF16) for _ in range(2)]
    hTB = [Buf("g_hT0"), Buf("g_hT1")]
    qTs = A.alloc([4, 512], F32)
    qTsB = Buf("g_qTs")
    kTs = A.alloc([4, 512], F32)
    kTsB = Buf("g_kTs")
    vsb = A.alloc([4, 1024], BF16)
    vsbB = [Buf("g_v%d" % t) for t in range(4)]
    sgb = A.alloc([4, 1024], BF16)
    sgbB = [Buf("g_sg%d" % t) for t in range(4)]
    sp = [A.alloc([512], F32) for _ in range(2)]
    spB = [Buf("g_sp0"), Buf("g_sp1")]
    e1, e1B = sp, spB
    eq = [A.alloc([4, 128], F32) for _ in range(2)]
    eqB = [Buf("g_eq0"), Buf("g_eq1")]
    ek = [A.alloc([4, 128], F32) for _ in range(2)]
    ekB = [Buf("g_ek0"), Buf("g_ek1")]
    qtT = [A.alloc([4, 128], BF16) for _ in range(2)]
    qtTB = [Buf("g_qt0"), Buf("g_qt1")]
    ktT = [A.alloc([4, 128], BF16) for _ in range(2)]
    ktTB = [Buf("g_kt0"), Buf("g_kt1")]
    ktm = [A.alloc([4, 128], BF16) for _ in range(2)]
    ktmB = [Buf("g_ktm0"), Buf("g_ktm1")]
    atT = [A.alloc([4, 128], BF16) for _ in range(2)]
    atTB = [Buf("g_at0"), Buf("g_at1")]
    S = A.alloc([4, 256], F32)
    SB = Buf("g_S")
    Sbf = A.alloc([4, 256], BF16)
    SbfB = Buf("g_Sbf")
    aT = A.alloc([512], F32)
    aTB = Buf("g_aT")
    w2 = A.alloc([512], F32)
    Utri = A.alloc([128], F32)
    tri = A.alloc([128], F32)
    normg = A.alloc([1024], F32)
    gpost = A.alloc([1024], F32)
    gcol = A.alloc([8], F32)
    tmp = A.alloc([1024], F32)
    tmpB = Buf("g_tmp")
    gated = [A.alloc([1024], BF16) for _ in range(2)]
    gatedB = [Buf("g_gated0"), Buf("g_gated1")]
    oT = A.alloc([8, 512], BF16)
    oTB = [Buf("g_oT%d" % t) for t in range(4)]
    r4 = [A.alloc([12], F32) for _ in range(2)]
    r4B = [Buf("g_r40"), Buf("g_r41")]

    def ld_consts(e):
        return [e.dma_start(out=w2[0:33, :], in_=I["w2aug"]), e.dma_start(out=tri, in_=I["tri"]),
                e.dma_start(out=normg, in_=I["normg"]), e.dma_start(out=gpost, in_=I["gpost1"]),
                e.dma_start(out=gcol, in_=I["gcol1"])]
    P.dma("sp", ld_consts, 5, writes=[c.Bconst], pool="ldc", K=1)
    P.op("dve", lambda e: e.tensor_scalar(out=Utri, in0=tri, scalar1=-1.0 / 16.0, scalar2=None, op0=ALU.mult),
         reads=[c.Bconst], writes=[c.Bconst])
    P.op("dve", lambda e: e.memset(aT[0:32, :], 0.0), writes=[aTB])
    P.op("dve", lambda e: e.memset(aT[32:33, :], 1.0), writes=[aTB])
    aT2 = tmp[:, 0:512]
    P.op("dve", lambda e: e.memset(aT2[0:32, :], 0.0), writes=[tmpB])
    P.op("dve", lambda e: e.memset(aT2[32:33, :], 1.0), writes=[tmpB])
    P.op("dve", lambda e: e.memset(S, 0.0), writes=[SB])
    P.op("dve", lambda e: e.memset(Sbf, 0.0), writes=[SbfB])
    WinB = load_weight(c, Win, I["c_w_in"], 8, 3088, "cwin", gcol=gcol)
    WoutB = load_weight(c, Wout, I["c_w_out"], 8, 1024, "cwout")
    QS = 128.0 ** -0.5

    def proj_fm(h, hB, col0, M, evac):
        bk, bB = c.bank()

        def mm(e):
            r = None
            for k in range(8):
                r = e.matmul(bk[0:M, :], lhsT=Win[:, k, col0:col0 + M], rhs=h[:, k, :], start=(k == 0), stop=(k == 7))
            return r
        P.op("pe", mm, reads=[hB] + WinB, writes=[bB])
        evac(bk, bB)

    def proj_tm(h, hB, t, col0, evac):
        bk, bB = c.bank()

        def mm(e):
            r = None
            for k in range(8):
                r = e.matmul(bk, lhsT=h[:, k, t * 128:(t + 1) * 128], rhs=Win[:, k, col0:col0 + 512],
                             start=(k == 0), stop=(k == 7))
            return r
        P.op("pe", mm, reads=[hB] + WinB, writes=[bB])
        evac(bk, bB)

    fin = []
    for g in range(NG):
        full = g >= g_own
        h, hB = hT[g % 2], hTB[g % 2]
        if (not full) and g % 2 == 1:
            kTs_g, kTsB_g, vsb_g, vsbB_g, aT_g, aTB_g = qTs, qTsB, sgb, sgbB, aT2, tmpB
        else:
            kTs_g, kTsB_g, vsb_g, vsbB_g, aT_g, aTB_g = kTs, kTsB, vsb, vsbB, aT, aTB
        norm_transpose_group(c, src_dram, g, h, hB)
        if full:
            for hd in range(4):
                proj_fm(h, hB, hd * 128, 128, lambda bk, bB, hd=hd: P.op(
                    "act", lambda e: e.activation(out=qTs[:, hd, :], in_=bk, func=AF.Copy, scale=QS), reads=[bB], writes=[qTsB]))
        for hd in range(4):
            proj_fm(h, hB, 512 + hd * 128, 128, lambda bk, bB, hd=hd, kTs_g=kTs_g, kTsB_g=kTsB_g: P.op(
                "act", lambda e: e.activation(out=kTs_g[:, hd, :], in_=bk, func=AF.Copy, scale=1.0), reads=[bB], writes=[kTsB_g]))
        proj_fm(h, hB, 3072, 16, lambda bk, bB, aT_g=aT_g, aTB_g=aTB_g: P.op(
            "act", lambda e: e.activation(out=aT_g[0:16, :], in_=bk[0:16, :], func=AF.Copy, scale=1.0), reads=[bB], writes=[aTB_g]))
        for t in range(4):
            for hf in range(2):
                proj_tm(h, hB, t, 1024 + hf * 512, lambda bk, bB, t=t, hf=hf, vsb_g=vsb_g, vsbB_g=vsbB_g: P.op(
                    "dve", lambda e: e.tensor_copy(out=vsb_g[:, t, hf * 512:(hf + 1) * 512], in_=bk), reads=[bB], writes=[vsbB_g[t]]))
            if full:
                for hf in range(2):
                    proj_tm(h, hB, t, 2048 + hf * 512, lambda bk, bB, t=t, hf=hf: P.op(
                        "act", lambda e: e.activation(out=sgb[:, t, hf * 512:(hf + 1) * 512], in_=bk, func=AF.Silu),
                        reads=[bB], writes=[sgbB[t]]))
        def make_steps(t, g=g, full=full, h=h, hB=hB, kTs=kTs_g, kTsB=kTsB_g, vsb=vsb_g, vsbB=vsbB_g, aT=aT_g, aTB=aTB_g):
            s = t % 2
            st8 = {}
            steps = []

            def s0():
                bl, blB = c.bank()
                st8["bl"] = (bl, blB)
                P.op("pe", (lambda bl, t: lambda e: e.matmul(bl, lhsT=aT[0:33, t * 128:(t + 1) * 128], rhs=w2[0:33, :],
                                                             start=True, stop=True))(bl, t),
                     reads=[aTB, c.Bconst], writes=[blB])
            steps.append(s0)

            def s1():
                bl, blB = st8["bl"]
                P.op("act", (lambda bl, s: lambda e: e.activation(out=e1[s], in_=bl, func=AF.Exp, scale=-1.0))(bl, s),
                     reads=[blB], writes=[e1B[s]])
                P.op("act", (lambda s: lambda e: e.activation(out=sp[s], in_=e1[s], func=AF.Ln, bias=1.0))(s),
                     reads=[e1B[s]], writes=[spB[s]])
            steps.append(s1)

            def s2():
                bc, bcB = c.bank()
                st8["bc"] = (bc, bcB)

                def cm(e, bc=bc, s=s):
                    r = None
                    for hd in range(4):
                        r = e.matmul(bc[:, hd * 128:(hd + 1) * 128], lhsT=sp[s][:, hd * 128:(hd + 1) * 128], rhs=Utri,
                                     start=True, stop=True)
                    return r
                P.op("pe", cm, reads=[spB[s], c.Bconst], writes=[bcB])
            steps.append(s2)

            def s3():
                bc, bcB = st8["bc"]
                bc3 = bc.rearrange("p (a b) -> p a b", a=4)
                P.op("act", (lambda bc3, s: lambda e: e.activation(out=ek[s], in_=bc3, func=AF.Exp, scale=-1.0))(bc3, s),
                     reads=[bcB], writes=[ekB[s]])
                if full:
                    P.op("act", (lambda bc3, s: lambda e: e.activation(out=eq[s], in_=bc3, func=AF.Exp))(bc3, s),
                         reads=[bcB], writes=[eqB[s]])
                else:
                    P.op("act", (lambda bc3, s: lambda e: e.activation(out=eq[s][:, :, 127:128], in_=bc3[:, :, 127:128], func=AF.Exp))(bc3, s),
                         reads=[bcB], writes=[eqB[s]])
            steps.append(s3)

            def s4():
                P.op("dve", (lambda s, t: lambda e: e.tensor_tensor(out=ktT[s], in0=kTs[:, :, t * 128:(t + 1) * 128], in1=ek[s],
                                                                    op=ALU.mult))(s, t),
                     reads=[kTsB, ekB[s]], writes=[ktTB[s]])
                if full:
                    P.op("dve", (lambda s, t: lambda e: e.tensor_tensor(out=qtT[s], in0=qTs[:, :, t * 128:(t + 1) * 128], in1=eq[s],
                                                                        op=ALU.mult))(s, t),
                         reads=[qTsB, eqB[s]], writes=[qtTB[s]])
            steps.append(s4)

            def s5():
                if full:
                    ba, baB = c.bank()
                    st8["ba"] = (ba, baB)

                    def am(e, ba=ba, s=s):
                        r = None
                        for hd in range(4):
                            r = e.matmul(ba[:, hd * 128:(hd + 1) * 128], lhsT=ktT[s][:, hd, :], rhs=qtT[s][:, hd, :],
                                         start=True, stop=True)
                        return r
                    P.op("pe", am, reads=[ktTB[s], qtTB[s]], writes=[baB])
                bt, btB = c.bank()
                st8["bt"] = (bt, btB)
                btb = bt.bitcast(BF16)

                def ktr(e, btb=btb, s=s):
                    r = None
                    for hd in range(4):
                        r = e.transpose(btb[:, hd * 128:(hd + 1) * 128], ktT[s][:, hd, :], c.ident)
                    return r
                P.op("pe", ktr, reads=[ktTB[s], c.Bconst], writes=[btB])
            steps.append(s5)

            def s6():
                if full:
                    ba, baB = st8["ba"]
                    P.op("dve", (lambda ba, s: lambda e: e.tensor_tensor(
                        out=atT[s], in0=ba.rearrange("p (a b) -> p a b", a=4),
                        in1=tri.unsqueeze(1).to_broadcast([128, 4, 128]), op=ALU.mult))(ba, s),
                        reads=[baB, c.Bconst], writes=[atTB[s]])
                bt, btB = st8["bt"]
                btb = bt.bitcast(BF16)
                P.op("act", (lambda btb, s: lambda e: e.copy(out=ktm[s], in_=btb[:, 0:512].rearrange("p (a b) -> p a b", a=4)))(btb, s),
                     reads=[btB], writes=[ktmB[s]])
            steps.append(s6)

            def s7():
                kv = [c.bank(), c.bank()]
                st8["kv"] = kv

                def kvm(e, kv=kv, s=s, t=t):
                    r = None
                    for hd in range(4):
                        r = e.matmul(kv[hd // 2][0][:, (hd % 2) * 256:(hd % 2 + 1) * 256], lhsT=ktm[s][:, hd, :],
                                     rhs=vsb[:, t, hd * 256:(hd + 1) * 256], start=True, stop=True)
                    return r
                P.op("pe", kvm, reads=[ktmB[s], vsbB[t]], writes=[kv[0][1], kv[1][1]])
                if full:
                    ob = [c.bank(), c.bank()]
                    st8["ob"] = ob

                    def om(e, ob=ob, s=s, t=t):
                        r = None
                        for hd in range(4):
                            o_ap = ob[hd // 2][0][:, (hd % 2) * 256:(hd % 2 + 1) * 256]
                            e.matmul(o_ap, lhsT=atT[s][:, hd, :], rhs=vsb[:, t, hd * 256:(hd + 1) * 256], start=True, stop=False)
                            r = e.matmul(o_ap, lhsT=qtT[s][:, hd, :], rhs=Sbf[:, hd, :], start=False, stop=True)
                        return r
                    P.op("pe", om, reads=[atTB[s], qtTB[s], vsbB[t], SbfB], writes=[ob[0][1], ob[1][1]])
            steps.append(s7)

            def s8():
                kv = st8["kv"]
                for hf in range(2):
                    P.op("dve", (lambda hf, kv: lambda e: e.tensor_tensor(
                        out=S[:, 2 * hf:2 * hf + 2, :], in0=kv[hf][0].rearrange("p (a b) -> p a b", a=2), in1=S[:, 2 * hf:2 * hf + 2, :],
                        op=ALU.add))(hf, kv), reads=[kv[hf][1], SB], writes=[SB])
                P.op("dve", (lambda s: lambda e: e.tensor_tensor(out=S, in0=S, in1=eq[s][:, :, 127:128].to_broadcast([128, 4, 256]),
                                                                 op=ALU.mult))(s), reads=[SB, eqB[s]], writes=[SB])
                P.op("pool", lambda e: e.tensor_copy(out=Sbf, in_=S), reads=[SB], writes=[SbfB])
            steps.append(s8)
            if not full:
                return steps

            def s9():
                ob = st8["ob"]
                rs = c.r4i % 2
                c.r4i += 1
                r4_, r4B_ = r4[rs], r4B[rs]
                st8["r4"] = (r4_, r4B_)
                for hd in range(4):
                    P.op("act", (lambda hd, ob, r4_: lambda e: e.activation(
                        out=c.junk[:, 0:256], in_=ob[hd // 2][0][:, (hd % 2) * 256:(hd % 2 + 1) * 256], func=AF.Square,
                        accum_out=r4_[:, hd:hd + 1]))(hd, ob, r4_), reads=[ob[hd // 2][1]], writes=[r4B_, c.Bjunk])
                P.op("act", (lambda r4_: lambda e: e.activation(out=r4_[:, 4:8], in_=r4_[:, 0:4], func=AF.Ln, scale=1.0 / 256.0,
                                                                bias=c.epsc))(r4_), reads=[r4B_, c.Bconst], writes=[r4B_])
                P.op("act", (lambda r4_: lambda e: e.activation(out=r4_[:, 8:12], in_=r4_[:, 4:8], func=AF.Exp, scale=-0.5))(r4_),
                     reads=[r4B_], writes=[r4B_])
            steps.append(s9)

            def s10():
                ob = st8["ob"]
                r4_, r4B_ = st8["r4"]
                for hd in range(4):
                    P.op("dve", (lambda hd, ob, r4_: lambda e: e.scalar_tensor_tensor(
                        out=tmp[:, hd * 256:(hd + 1) * 256], in0=ob[hd // 2][0][:, (hd % 2) * 256:(hd % 2 + 1) * 256],
                        scalar=r4_[:, 8 + hd:9 + hd], in1=normg[:, hd * 256:(hd + 1) * 256], op0=ALU.mult, op1=ALU.mult))(hd, ob, r4_),
                        reads=[ob[hd // 2][1], r4B_, c.Bconst], writes=[tmpB])
                gd, gdB = gated[t % 2], gatedB[t % 2]
                P.op("pool", (lambda gd, t: lambda e: e.tensor_tensor(out=gd, in0=tmp, in1=sgb[:, t, :], op=ALU.mult))(gd, t),
                     reads=[tmpB, sgbB[t]], writes=[gdB])
            steps.append(s10)

            def s11():
                gd, gdB = gated[t % 2], gatedB[t % 2]
                bo, boB = c.bank()
                bob = bo.bitcast(BF16)

                def otr(e, bob=bob, gd=gd):
                    r = None
                    for k in range(8):
                        r = e.transpose(bob[:, k * 128:(k + 1) * 128], gd[:, k * 128:(k + 1) * 128], c.ident)
                    return r
                P.op("pe", otr, reads=[gdB, c.Bconst], writes=[boB])
                P.op("act", (lambda bob, t: lambda e: e.copy(out=oT[:, :, t * 128:(t + 1) * 128],
                                                             in_=bob.rearrange("p (k n) -> p k n", k=8)))(bob, t),
                     reads=[boB], writes=[oTB[t]])
            steps.append(s11)

            def s12():
                bks = [c.bank(), c.bank()]
                st8["bks"] = bks
                for hf in range(2):
                    bk = bks[hf][0]

                    def mm3(e, bk=bk, hf=hf, t=t):
                        r = None
                        for k in range(8):
                            r = e.matmul(bk, lhsT=oT[:, k, t * 128:(t + 1) * 128], rhs=Wout[:, k, hf * 512:(hf + 1) * 512],
                                         start=(k == 0), stop=(k == 7))
                        return r
                    P.op("pe", mm3, reads=[oTB[t]] + WoutB, writes=[bks[hf][1]])
            steps.append(s12)

            def s13():
                bks = st8["bks"]
                fin.append(post_norm_residual(c, g, t, [bks[0][0], bks[1][0]], [bks[0][1], bks[1][1]], gpost, src_dram, dst_dram))
            steps.append(s13)
            return steps

        LAG = 3
        for ta, tb in ((0, 1), (2, 3)):
            sa, sb_ = make_steps(ta), make_steps(tb)
            n = len(sa)
            for k in range(n + LAG):
                if k < n:
                    sa[k]()
                if 0 <= k - LAG < n:
                    sb_[k - LAG]()
    return fin

INPUT_SHAPES = {
    "ident": ([128, 128], F32),
}


def build(NT, stages, inputs_np, debug=False, g_own=0):
    nc = bass.Bass("TRN2", target_bir_lowering=False)
    c = Ctx()
    c.nc = nc
    c.debug = debug
    c.g_own = g_own
    c.dbg_ops = []
    c.P = Prog(nc)
    c.NT = NT
    c.NTILES = NT // 128
    c.inp = {}
    for name, arr in inputs_np.items():
        c.inp[name] = nc.dram_tensor(name, list(arr.shape), F32, kind="ExternalInput").ap()
    y = nc.dram_tensor("y", [NT, D], F32, kind="ExternalOutput").ap()
    c.Bstream = [Buf("ys%d" % i) for i in range(c.NTILES)]
    A = Arena(nc, 204 * 1024)
    c.A = A
    c.identf = A.alloc([128], F32)
    c.ident = A.alloc([128], BF16)
    c.junk = A.alloc([1024], BF16)
    c.epsc = A.alloc([1], F32)
    c.Bjunk = Buf("junk")
    c.Bconst = Buf("const")
    c.xt = [A.alloc([1024], F32) for _ in range(3)]
    c.xtB = [Buf("xt%d" % i) for i in range(3)]
    c.ss = [A.alloc([4], F32) for _ in range(3)]
    c.ssB = [Buf("ss%d" % i) for i in range(3)]
    c.hb = [A.alloc([1024], BF16) for _ in range(2)]
    c.hbB = [Buf("hb0"), Buf("hb1")]
    c.pss = [A.alloc([8], F32) for _ in range(2)]
    c.pssB = [Buf("pss0"), Buf("pss1")]
    ot0 = A.alloc([1024], F32)
    xr0 = A.alloc([1024], F32)
    c.stage_base_small = A.off
    c.ot = [ot0, A.alloc([1024], F32)]
    c.otB = [Buf("ot0"), Buf("ot1")]
    c.xr = [xr0, A.alloc([1024], F32)]
    c.xrB = [Buf("xr0"), Buf("xr1")]
    c.n_ot = 2
    c.xi = 0
    c.oi = 0
    c.pei = 0
    c.hbi = 0
    c.r4i = 0
    c.stage_base = A.off
    psall = nc.alloc_psum_tensor("psall", [128, 4096], F32)
    c.psall = psall[:]
    banks = [psall[:, i * 512:(i + 1) * 512] for i in range(8)]
    bankB = [Buf("bank%d" % i, excl=True) for i in range(8)]
    c.banks = banks
    c.bankB = bankB
    c.bi = 0

    def bank():
        i = c.bi % 8
        c.bi += 1
        return banks[i], bankB[i]
    c.bank = bank
    P = c.P
    P.dma("sp", lambda e: [e.dma_start(out=c.identf, in_=c.inp["ident"])], 1, writes=[c.Bconst], pool="ldc", K=1)
    P.op("dve", lambda e: e.tensor_copy(out=c.ident, in_=c.identf), reads=[c.Bconst], writes=[c.Bconst])
    P.op("dve", lambda e: e.memset(c.epsc, EPS), reads=[], writes=[c.Bconst])
    fin = []
    src = c.inp["x"]
    for stg in stages:
        if stg == "l0mix":
            fin = stage_l0_mixer(c, src, y, NT // 512)
        elif stg == "l1mix":
            fin = stage_l1_mixer(c, src, y, NT // 512, c.g_own)
        elif stg == "ffn0":
            fin = stage_ffn(c, 0, src, y, 0, NT // 512)
        elif stg == "ffn1":
            fin = stage_ffn(c, 1, src, y, c.g_own, NT // 512 - c.g_own)
        else:
            raise ValueError(stg)
        src = y
        P.fence()
    P.emit(final_wait_ops=list(fin) + c.dbg_ops)
    return nc


def rep128(v):
    return np.ascontiguousarray(np.broadcast_to(np.asarray(v, np.float32).reshape(1, -1), (128, v.size)))


def col128(v):
    v = np.asarray(v, np.float32)
    return np.ascontiguousarray(v.reshape(-1, 128).T)


def prep_common(inp):
    d = {}
    d["ident"] = np.eye(128, dtype=np.float32)
    rb = np.asarray(inp["a_rel_bias"][0], np.float32)
    kj = np.arange(128)[:, None, None]
    r = np.arange(5)[None, :, None]
    qi = np.arange(128)[None, None, :]
    idx = np.clip(r * 128 + qi - kj, -256, 256) + 256
    d["biasT"] = np.ascontiguousarray(rb[:, idx].transpose(1, 0, 2, 3))
    d["wsT"] = np.ascontiguousarray(np.asarray(inp["b_w_s"][0], np.float32).transpose(2, 0, 1))
    d["tri"] = (np.arange(128)[:, None] <= np.arange(128)[None, :]).astype(np.float32)
    d["bsb"] = np.ascontiguousarray(np.broadcast_to(np.asarray(inp["b_b_s"][0], np.float32)[None], (128, 4, 128)))
    d["lng"] = rep128(inp["b_ln_g"][0])
    d["lnb"] = rep128(inp["b_ln_b"][0])
    d["gpost0"] = rep128(inp["post_mix_g"][0])
    d["gcol0"] = col128(inp["pre_mix_g"][0])
    d["ab_w_in"] = np.ascontiguousarray(inp["ab_w_in"][0], dtype=np.float32)
    d["ab_w_out"] = np.ascontiguousarray(inp["ab_w_out"][0], dtype=np.float32)
    d["gcol1"] = col128(inp["pre_mix_g"][1])
    d["gpost1"] = rep128(inp["post_mix_g"][1])
    d["normg"] = rep128(inp["c_norm_g"][0])
    w2 = np.zeros((33, 512), np.float32)
    w2[0:16] = np.asarray(inp["c_w_a2"][0], np.float32)
    w2[32] = np.asarray(inp["c_b_a"][0], np.float32)
    d["w2aug"] = w2
    d["c_w_in"] = np.ascontiguousarray(inp["c_w_in"][0], dtype=np.float32)
    d["c_w_out"] = np.ascontiguousarray(inp["c_w_out"][0], dtype=np.float32)
    for L in range(2):
        d["gpostf%d" % L] = rep128(inp["post_ffn_g"][L])
        d["gcolf%d" % L] = col128(inp["pre_ffn_g"][L])
        d["ffn_w_gate%d" % L] = np.ascontiguousarray(inp["ffn_w_gate"][L], dtype=np.float32)
        d["ffn_w_up%d" % L] = np.ascontiguousarray(inp["ffn_w_up"][L], dtype=np.float32)
        d["ffn_w_down%d" % L] = np.ascontiguousarray(inp["ffn_w_down"][L], dtype=np.float32)
    return d


STAGES = ["l0mix", "ffn0", "l1mix", "ffn1"]
SEQ_HALF = 4096
_CACHE = {}


def kernel(**inputs):
    inp = {k: np.asarray(v) for k, v in inputs.items()}
    x = np.asarray(inp["x"], np.float32)
    B = x.shape[0]
    common = prep_common(inp)
    NT = 2 * SEQ_HALF
    in_maps = []
    for core in range(8):
        b, half = core // 2, core % 2
        d = dict(common)
        own = x[b, half * SEQ_HALF:(half + 1) * SEQ_HALF]
        if half == 1:
            prev = x[b, 0:SEQ_HALF]
            vprev = np.ones(SEQ_HALF, np.float32)
        else:
            prev = np.zeros_like(own)
            vprev = np.zeros(SEQ_HALF, np.float32)
        d["x"] = np.ascontiguousarray(np.concatenate([prev, own], axis=0))
        d["valid"] = col128(np.concatenate([vprev, np.ones(SEQ_HALF, np.float32)]))
        in_maps.append(d)
    if "nc" not in _CACHE:
        _CACHE["nc"] = build(NT, STAGES, in_maps[0], g_own=SEQ_HALF // 512)
    nc = _CACHE["nc"]
    res = run_bass_kernel_spmd(nc, in_maps, core_ids=list(range(8)))
    out = np.empty((B, 2 * SEQ_HALF, D), np.float32)
    for core in range(8):
        b, half = core // 2, core % 2
        out[b, half * SEQ_HALF:(half + 1) * SEQ_HALF] = np.asarray(res.results[core]["y"])[SEQ_HALF:]
    return out
```

```python
import contextlib
import numpy as np
import concourse.bass as bass
import concourse.mybir as mybir
from concourse.bass_utils import run_bass_kernel_spmd

F32 = mybir.dt.float32
BF16 = mybir.dt.bfloat16
ALU = mybir.AluOpType
AF = mybir.ActivationFunctionType

D = 1024
DFF = 2816
NFB = DFF // 128
EPS = 1e-6
ENGS = ("pe", "act", "dve", "pool", "sp")


class Buf:
    __slots__ = ("name", "excl", "last_w", "readers")

    def __init__(self, name, excl=False):
        self.name = name
        self.excl = excl
        self.last_w = None
        self.readers = []


class Op:
    __slots__ = ("eng", "fn", "deps", "dma", "signal", "ticket", "ndma", "semkey", "odeps", "cost", "table",
                 "seg", "fdep", "idx", "nun", "ready", "fin", "users")

    def __init__(self, eng, fn, dma):
        self.eng = eng
        self.fn = fn
        self.deps = []
        self.dma = dma
        self.signal = False
        self.ticket = None
        self.ndma = 0
        self.semkey = None
        self.odeps = []
        self.cost = 0.5
        self.table = None
        self.seg = 0
        self.fdep = None
        self.idx = 0


class _Dummy:
    def then_inc(self, *a, **k):
        return self


class CostEngine:
    def __init__(self, eng):
        self.eng = eng
        self.cost = 0.0
        self.table = None
        self.bytes = 0

    def _free(self, ap):
        n = 1
        for d in ap.shape[1:]:
            n *= d
        return n

    def __getattr__(self, name):
        def f(*a, **k):
            out = k.get("out", a[0] if a else None)
            cols = self._free(out) if out is not None and hasattr(out, "shape") else 1
            if name in ("matmul", "transpose"):
                lhs = k.get("lhsT", a[1] if len(a) > 1 else None)
                mult = 4.0 if (lhs is not None and lhs.dtype == F32) else 1.0
                self.cost += mult * max(cols, 64) / 2400.0 + 0.035
            elif name == "dma_start":
                src = k.get("in_", a[1] if len(a) > 1 else None)
                nb = out.shape[0] * cols * (4 if out.dtype == F32 else 2)
                self.bytes += nb
                self.cost += 0.05
            elif self.eng == "act":
                self.cost += 0.25 + cols / 1200.0
                fn = k.get("func", None)
                if fn in (AF.Exp, AF.Ln):
                    self.table = "A"
                elif fn == AF.Gelu:
                    self.table = "B"
                elif fn == AF.Silu:
                    self.table = "C"
            elif self.eng == "dve":
                self.cost += 0.13 + cols / 960.0
            else:
                self.cost += 0.2 + cols / 480.0
            return _Dummy()
        return f


class Prog:
    def __init__(self, nc):
        self.nc = nc
        self.ops = []
        self.pools = {}
        self.last = {e: None for e in ENGS}
        self.seg = 0
        self.pending_fence = {e: None for e in ENGS}
        self.reorder = True
        self.pe_mix = 0

    def _add(self, eng, fn, reads, writes, dma):
        op = Op(eng, fn, dma)
        deps = []

        def need(o, kind):
            if o is None or o is op:
                return
            if o.eng == eng and not dma and not o.dma:
                if eng == "pe" or kind != "raw":
                    op.odeps.append(o)
                    return
            deps.append(o)

        for b in reads:
            need(b.last_w, "raw")
            if b.excl:
                for r in b.readers:
                    need(r, "raw")
        for b in writes:
            need(b.last_w, "waw")
            for r in b.readers:
                need(r, "war")
        for b in reads:
            if b.excl:
                b.last_w = op
                b.readers = []
            else:
                b.readers.append(op)
        for b in writes:
            b.last_w = op
            b.readers = []
        if self.pending_fence[eng] is not None:
            op.fdep = self.pending_fence[eng]
            self.pending_fence[eng] = None
        seen = set()
        for d in deps:
            if id(d) not in seen:
                seen.add(id(d))
                op.deps.append(d)
                d.signal = True
        op.seg = self.seg
        op.idx = len(self.ops)
        if dma:
            lastd = self.last.get(("dma", eng, dma))
            if lastd is not None:
                op.odeps.append(lastd)
            self.last[("dma", eng, dma)] = op
        ce = CostEngine(eng)
        fn(ce)
        op.cost = ce.cost if not dma else (2.0 + ce.bytes / 120000.0)
        op.table = ce.table
        self.ops.append(op)
        return op

    def op(self, eng, fn, reads=(), writes=()):
        return self._add(eng, fn, list(reads), list(writes), False)

    def dma(self, eng, fn, n, reads=(), writes=(), pool="ld", K=6):
        op = self._add(eng, fn, list(reads), list(writes), pool)
        op.ndma = n
        pl = self.pools.setdefault(pool, [K, 0, {}])
        i = pl[1]
        pl[1] += 1
        slot = i % pl[0]
        prev = pl[2].get(slot)
        if prev is not None and prev not in op.deps:
            op.deps.append(prev)
        pl[2][slot] = op
        op.semkey = (pool, slot)
        return op

    def fence(self):
        for e in ENGS:
            self.pending_fence[e] = self.seg
        self.seg += 1

    def schedule(self):
        order = {e: [] for e in ENGS}
        self.seg_tail = {}
        nseg = self.seg + 1
        by_seg = [[] for _ in range(nseg)]
        for op in self.ops:
            by_seg[op.seg].append(op)
        for k, ops in enumerate(by_seg):
            for e in ENGS:
                ops_e = [o for o in ops if o.eng == e]
                if k > 0 and ops_e:
                    assert ops_e[0].fdep == k - 1, (e, k, ops_e[0].fdep)
                    for o in ops_e[1:]:
                        o.odeps.append(ops_e[0])
            if not self.reorder:
                for op in ops:
                    order[op.eng].append(op)
            else:
                self._sched_segment(ops, order)
            tail = []
            for e in ENGS:
                comp = [o for o in order[e] if o.seg == k and not o.dma]
                if comp:
                    tail.append(comp[-1])
            tail += [o for o in ops if o.dma]
            self.seg_tail[k] = tail
        return order

    def _sched_segment(self, ops, order):
        import heapq
        inseg = set(id(o) for o in ops)
        for o in ops:
            o.users = []
            o.nun = 0
            o.ready = 0.0
            o.fin = None
        for o in ops:
            for d in o.deps + o.odeps:
                if id(d) in inseg:
                    d.users.append(o)
                    o.nun += 1
        main = {e: [] for e in ENGS}
        avail = {e: [] for e in ENGS}
        free = {e: 0.0 for e in ENGS}
        cur_table = None
        small_run = [0]
        for o in ops:
            if o.nun == 0:
                heapq.heappush(main[o.eng], (o.ready, o.idx, o))
        nleft = len(ops)
        while nleft:
            best = None
            for e in ENGS:
                m, a = main[e], avail[e]
                while m and m[0][0] <= free[e] + 1e-9:
                    r, ix, o = heapq.heappop(m)
                    heapq.heappush(a, (ix, o))
                if a:
                    cand = (free[e], a[0][0], e, True)
                elif m:
                    cand = (m[0][0], m[0][1], e, False)
                else:
                    continue
                if best is None or cand[:2] < best[:2]:
                    best = cand
            st, _, e, from_avail = best
            if from_avail:
                a = avail[e]
                if e == "act" and len(a) > 1:
                    pick = None
                    small = heapq.nsmallest(6, a)
                    for ix, o in small:
                        if o.table is None or o.table == cur_table:
                            pick = (ix, o)
                            break
                    if pick is None or pick is small[0]:
                        ix, o = heapq.heappop(a)
                    else:
                        a.remove(pick)
                        heapq.heapify(a)
                        ix, o = pick
                elif e == "pe" and len(a) > 1 and self.pe_mix and small_run[0] >= self.pe_mix:
                    pick = None
                    for ix, o in a:
                        if o.cost >= 1.0 and (pick is None or ix < pick[0]):
                            pick = (ix, o)
                    if pick is None:
                        ix, o = heapq.heappop(a)
                    else:
                        a.remove(pick)
                        heapq.heapify(a)
                        ix, o = pick
                else:
                    ix, o = heapq.heappop(a)
            else:
                r, ix, o = heapq.heappop(main[e])
            if e == "pe":
                small_run[0] = small_run[0] + 1 if o.cost < 1.0 else 0
            if e == "act" and o.table is not None:
                if o.table != cur_table:
                    st += 1.3
                cur_table = o.table
            o.fin = st + o.cost
            free[e] = st + (0.06 if o.dma else o.cost)
            order[e].append(o)
            nleft -= 1
            for u in o.users:
                u.nun -= 1
                if u.eng == e and not o.dma and (o not in u.deps):
                    r = st
                else:
                    r = o.fin + (0.3 if (u.eng == e and not o.dma) else 1.2)
                if r > u.ready:
                    u.ready = r
                if u.nun == 0:
                    heapq.heappush(main[u.eng], (u.ready, u.idx, u))

    def emit(self, final_wait_ops=()):
        nc = self.nc
        with contextlib.ExitStack() as st:
            esem = {e: st.enter_context(nc.semaphore("s_" + e)) for e in ENGS}
            dsem = {}
            dcnt = {}
            for op in self.ops:
                if op.dma and op.semkey not in dsem:
                    dsem[op.semkey] = st.enter_context(nc.semaphore("d_%s%d" % op.semkey))
                    dcnt[op.semkey] = 0
            block = st.enter_context(nc.Block())
            order = self.schedule()
            for op in self.ops:
                if op.fdep is not None:
                    for o in self.seg_tail[op.fdep]:
                        if o is not op and not (o.eng == op.eng and not o.dma and not op.dma):
                            op.deps.append(o)
                            o.signal = True
            ecnt = {e: 0 for e in ENGS}
            for e in ENGS:
                for op in order[e]:
                    if op.dma:
                        dcnt[op.semkey] += 16 * op.ndma
                        op.ticket = (dsem[op.semkey], dcnt[op.semkey])
                    elif op.signal:
                        ecnt[op.eng] += 1
                        op.ticket = (esem[op.eng], ecnt[op.eng])
            final = list(final_wait_ops)

            def run(engname):
                def body(eng):
                    waited = {}
                    for op in order[engname]:
                        for d in op.deps:
                            sem, val = d.ticket
                            k = id(sem)
                            if waited.get(k, 0) < val:
                                eng.wait_ge(sem, val)
                                waited[k] = val
                        res = op.fn(eng)
                        if op.dma:
                            assert len(res) == op.ndma, (len(res), op.ndma)
                            for ins in res:
                                ins.then_inc(op.ticket[0], 16)
                        elif op.signal:
                            res.then_inc(op.ticket[0], 1)
                    if engname == "sp":
                        for d in final:
                            sem, val = d.ticket
                            eng.wait_ge(sem, val)
                return body

            block.tensor(run("pe"))
            block.scalar(run("act"))
            block.vector(run("dve"))
            block.gpsimd(run("pool"))
            block.sync(run("sp"))


class Arena:
    def __init__(self, nc, nbytes):
        self.t = nc.alloc_sbuf_tensor("arena", [128, nbytes // 2], BF16)
        self.nbytes = nbytes
        self.off = 0

    def alloc(self, shape, dtype):
        n = int(np.prod(shape))
        sz = n * (4 if dtype == F32 else 2)
        self.off = (self.off + 31) // 32 * 32
        assert self.off + sz <= self.nbytes, ("arena overflow", self.off, sz, self.nbytes)
        o2 = self.off // 2
        ap = self.t[:, o2:o2 + sz // 2]
        if dtype == F32:
            ap = ap.bitcast(F32)
        self.off += sz
        if len(shape) == 2:
            ap = ap.rearrange("p (a b) -> p a b", a=shape[0])
        elif len(shape) == 3:
            ap = ap.rearrange("p (a b c) -> p a b c", a=shape[0], b=shape[1])
        return ap


class Ctx:
    pass


def dbg(c, name, ap, bufs, dtype=F32):
    if not getattr(c, "debug", False):
        return
    shp = list(ap.shape)
    o = c.nc.dram_tensor("dbg_" + name, shp, dtype, kind="ExternalOutput").ap()
    c.dbg_ops.append(c.P.dma("sp", lambda e: [e.dma_start(out=o, in_=ap)], 1, reads=list(bufs), pool="dbg", K=1))


def load_weight(c, dst, src, nk, cols, name, gcol=None, chunk=None):
    P = c.P
    bufs = []
    for k in range(nk):
        b = Buf("%s_k%d" % (name, k))
        bufs.append(b)
        P.dma("pool", (lambda k: lambda e: [e.dma_start(out=dst[:, k, :], in_=src[k * 128:(k + 1) * 128, :])])(k),
              1, writes=[b], pool="w", K=6)
        if gcol is not None:
            P.op("dve", (lambda k: lambda e: e.tensor_scalar(out=dst[:, k, :], in0=dst[:, k, :], scalar1=gcol[:, k:k + 1],
                                                             scalar2=None, op0=ALU.mult))(k),
                 reads=[b, c.Bconst], writes=[b])
    return bufs


def norm_transpose_group(c, src_dram, g, hT, hTbuf):
    for t in range(4):
        norm_transpose_tile(c, src_dram, g, t, hT, hTbuf)


def norm_transpose_tile(c, src_dram, g, t, hT, hTbuf):
    hb = norm_tile(c, src_dram, g, t)
    transpose_tile(c, hb, t, hT, hTbuf)


def norm_tile(c, src_dram, g, t):
    P = c.P
    T = 4 * g + t
    s = c.xi % 3
    c.xi += 1
    xt, xb = c.xt[s], c.xtB[s]
    P.dma("sp", (lambda xt, T: lambda e: [e.dma_start(out=xt, in_=src_dram[T * 128:(T + 1) * 128, :])])(xt, T),
          1, reads=[c.Bstream[T]], writes=[xb], pool="ldx", K=3)
    ss, ssB = c.ss[s], c.ssB[s]
    P.op("act", (lambda xt, ss: lambda e: e.activation(out=c.junk, in_=xt, func=AF.Square, accum_out=ss[:, 0:1]))(xt, ss),
         reads=[xb], writes=[ssB, c.Bjunk])
    P.op("act", (lambda ss: lambda e: e.activation(out=ss[:, 1:2], in_=ss[:, 0:1], func=AF.Ln, scale=1.0 / D, bias=c.epsc))(ss),
         reads=[ssB, c.Bconst], writes=[ssB])
    P.op("act", (lambda ss: lambda e: e.activation(out=ss[:, 2:3], in_=ss[:, 1:2], func=AF.Exp, scale=-0.5))(ss),
         reads=[ssB], writes=[ssB])
    hs = c.hbi % 2
    c.hbi += 1
    hb, hbB = c.hb[hs], c.hbB[hs]
    P.op("dve", (lambda xt, ss, hb: lambda e: e.tensor_scalar(out=hb, in0=xt, scalar1=ss[:, 2:3], scalar2=None,
                                                              op0=ALU.mult))(xt, ss, hb),
         reads=[xb, ssB], writes=[hbB])
    return hb, hbB


def transpose_tile(c, hbp, t, hT, hTbuf):
    P = c.P
    hb, hbB = hbp
    bk, bB = c.bank()
    bkb = bk.bitcast(BF16)

    def tr(e, hb=hb, bkb=bkb):
        r = None
        for k in range(8):
            r = e.transpose(bkb[:, k * 128:(k + 1) * 128], hb[:, k * 128:(k + 1) * 128], c.ident)
        return r
    P.op("pe", tr, reads=[hbB, c.Bconst], writes=[bB])
    P.op("act", (lambda bkb, t: lambda e: e.copy(out=hT[:, :, t * 128:(t + 1) * 128],
                                                 in_=bkb.rearrange("p (k n) -> p k n", k=8)))(bkb, t),
         reads=[bB], writes=[hTbuf])


def post_norm_residual(c, g, t, banks, bankBs, gpost, src_dram, dst_dram):
    P = c.P
    T = 4 * g + t
    s = c.oi % c.n_ot
    c.oi += 1
    ps, psB = c.pss[s], c.pssB[s]
    for hf in range(2):
        P.op("act", (lambda hf, ps: lambda e: e.activation(out=c.junk[:, 0:512], in_=banks[hf], func=AF.Square,
                                                           accum_out=ps[:, hf:hf + 1]))(hf, ps),
             reads=[bankBs[hf]], writes=[psB, c.Bjunk])
    P.op("dve", (lambda ps: lambda e: e.tensor_tensor(out=ps[:, 2:3], in0=ps[:, 0:1], in1=ps[:, 1:2], op=ALU.add))(ps),
         reads=[psB], writes=[psB])
    P.op("act", (lambda ps: lambda e: e.activation(out=ps[:, 3:4], in_=ps[:, 2:3], func=AF.Ln, scale=1.0 / D, bias=c.epsc))(ps),
         reads=[psB, c.Bconst], writes=[psB])
    P.op("act", (lambda ps: lambda e: e.activation(out=ps[:, 4:5], in_=ps[:, 3:4], func=AF.Exp, scale=-0.5))(ps),
         reads=[psB], writes=[psB])
    ot, otB = c.ot[s], c.otB[s]
    xr, xrB = c.xr[s], c.xrB[s]
    P.dma("sp", (lambda xr, T: lambda e: [e.dma_start(out=xr, in_=src_dram[T * 128:(T + 1) * 128, :])])(xr, T),
          1, reads=[c.Bstream[T]], writes=[xrB], pool="ldr", K=2)
    for hf in range(2):
        P.op("dve", (lambda hf, ps, ot: lambda e: e.scalar_tensor_tensor(
            out=ot[:, hf * 512:(hf + 1) * 512], in0=banks[hf], scalar=ps[:, 4:5], in1=gpost[:, hf * 512:(hf + 1) * 512],
            op0=ALU.mult, op1=ALU.mult))(hf, ps, ot), reads=[bankBs[hf], psB, c.Bconst], writes=[otB])
    P.op("pool", (lambda ot, xr: lambda e: e.tensor_tensor(out=ot, in0=ot, in1=xr, op=ALU.add))(ot, xr),
         reads=[otB, xrB], writes=[otB])
    return P.dma("pool", (lambda ot, T: lambda e: [e.dma_start(out=dst_dram[T * 128:(T + 1) * 128, :], in_=ot)])(ot, T),
                 1, reads=[otB], writes=[c.Bstream[T]], pool="st", K=2)


def stage_l0_mixer(c, src_dram, dst_dram, NG):
    P, A, nc = c.P, c.A, c.nc
    I = c.inp
    A.off = c.stage_base
    Win = A.alloc([8, 2560], BF16)
    Wout = A.alloc([8, 1024], BF16)
    hT = [A.alloc([8, 512], BF16) for _ in range(2)]
    hTB = [Buf("hT0"), Buf("hT1")]
    qTs2 = [A.alloc([4, 512], BF16) for _ in range(2)]
    qTBs2 = [Buf("qT0"), Buf("qT1")]
    kR = A.alloc([4, 1536], BF16)
    kRB = [Buf("kR0"), Buf("kR1"), Buf("kR2")]
    vR = A.alloc([12, 8, 65], BF16)
    vRB = [Buf("vR0"), Buf("vR1"), Buf("vR2")]
    Ep = A.alloc([8, 5, 128], BF16)
    uT = A.alloc([4, 512], BF16)
    uTB = Buf("uT")
    vln = A.alloc([4, 512], BF16)
    vlnB = [Buf("vln%d" % t) for t in range(4)]
    vg = [A.alloc([512], F32) for _ in range(2)]
    vgB = [Buf("vg0"), Buf("vg1")]
    vt = [A.alloc([512], F32) for _ in range(2)]
    vtB = [Buf("vt0"), Buf("vt1")]
    st = [A.alloc([16], F32) for _ in range(2)]
    stB = [Buf("st0"), Buf("st1")]
    wsT = A.alloc([4, 128], BF16)
    wsF = c.xr[0][:, 0:512].rearrange("p (a b) -> p a b", a=4)
    tri = A.alloc([128], F32)
    bsb = A.alloc([4, 128], F32)
    lng = A.alloc([512], F32)
    lnb = A.alloc([512], F32)
    gpost = A.alloc([1024], F32)
    gcol = A.alloc([8], F32)
    PP = [A.alloc([5, 4, 128], BF16) for _ in range(2)]
    PPB = [[Buf("PP%d_a" % i), Buf("PP%d_b" % i)] for i in range(2)]
    aout = A.alloc([4, 512], BF16)
    aoutB = Buf("aout")
    rden = [A.alloc([4], F32) for _ in range(2)]
    rdenB = [Buf("rden0"), Buf("rden1")]
    catT = A.alloc([8, 512], BF16)
    catTB = Buf("catTa")
    catTBb = Buf("catTb")
    gt, gtB = vt, vtB
    valid = A.alloc([c.NTILES], F32)

    def ld_consts(e):
        return [e.dma_start(out=wsF, in_=I["wsT"]), e.dma_start(out=tri, in_=I["tri"]),
                e.dma_start(out=bsb, in_=I["bsb"]), e.dma_start(out=lng, in_=I["lng"]),
                e.dma_start(out=lnb, in_=I["lnb"]), e.dma_start(out=gpost, in_=I["gpost0"]),
                e.dma_start(out=gcol, in_=I["gcol0"]),
                e.dma_start(out=valid, in_=I["valid"])]
    P.dma("sp", ld_consts, 8, writes=[c.Bconst, c.xrB[0]], pool="ldc", K=1)
    P.op("dve", lambda e: e.tensor_tensor(out=wsT, in0=wsF, in1=tri.unsqueeze(1).to_broadcast([128, 4, 128]), op=ALU.mult),
         reads=[c.Bconst, c.xrB[0]], writes=[c.Bconst])
    for h8 in range(8):
        stg = c.xt[h8 % 3][:, 0:640].rearrange("p (a b) -> p a b", a=5)
        P.dma("sp", (lambda h8, stg: lambda e: [e.dma_start(out=stg, in_=I["biasT"][:, h8, :, :])])(h8, stg), 1,
              writes=[c.xtB[h8 % 3]], pool="ldx", K=3)
        P.op("act", (lambda h8, stg: lambda e: e.activation(out=Ep[:, h8, :, :], in_=stg, func=AF.Exp))(h8, stg),
             reads=[c.xtB[h8 % 3], c.Bconst], writes=[c.Bconst])
    P.op("act", lambda e: e.memzero(Ep[64:128, :, 0, 0:64]), reads=[c.Bconst], writes=[c.Bconst])
    P.op("act", lambda e: e.memzero(Ep[0:64, :, 4, 64:128]), reads=[c.Bconst], writes=[c.Bconst])
    WinB = load_weight(c, Win, I["ab_w_in"], 8, 2560, "win", gcol=gcol)
    WoutB = load_weight(c, Wout, I["ab_w_out"], 8, 1024, "wout")

    def proj_fm(h, hB, col0, evac):
        bk, bB = c.bank()

        def mm(e):
            r = None
            for k in range(8):
                r = e.matmul(bk, lhsT=Win[:, k, col0:col0 + 128], rhs=h[:, k, :], start=(k == 0), stop=(k == 7))
            return r
        P.op("pe", mm, reads=[hB] + WinB, writes=[bB])
        evac(bk, bB)

    def proj_tm(h, hB, t, col0, evac):
        bk, bB = c.bank()

        def mm(e):
            r = None
            for k in range(8):
                r = e.matmul(bk, lhsT=h[:, k, t * 128:(t + 1) * 128], rhs=Win[:, k, col0:col0 + 512],
                             start=(k == 0), stop=(k == 7))
            return r
        P.op("pe", mm, reads=[hB] + WinB, writes=[bB])
        evac(bk, bB)

    fin = []
    for g in range(NG):
        h, hB = hT[g % 2], hTB[g % 2]
        norm_transpose_group(c, src_dram, g, h, hB)
        half = g % 3
        qT, qTB = qTs2[g % 2], qTBs2[g % 2]
        for ob in range(4):
            proj_fm(h, hB, ob * 128, lambda bk, bB, ob=ob, qT=qT, qTB=qTB: P.op(
                "act", lambda e: e.activation(out=qT[:, ob, :], in_=bk, func=AF.Copy, scale=0.125), reads=[bB], writes=[qTB]))
        for ob in range(4):
            proj_fm(h, hB, 512 + ob * 128, lambda bk, bB, ob=ob, half=half: P.op(
                "act", lambda e: e.activation(out=kR[:, ob, half * 512:(half + 1) * 512], in_=bk, func=AF.Copy), reads=[bB], writes=[kRB[half]]))
        for t in range(4):
            T = 4 * g + t
            sl = T % 12

            def ev(bk, bB, T=T, sl=sl, half=half):
                P.op("dve", lambda e: e.tensor_scalar(out=vR[:, sl, :, 0:64], in0=bk.rearrange("p (h d) -> p h d", h=8),
                                                      scalar1=valid[:, T:T + 1], scalar2=None, op0=ALU.mult),
                     reads=[bB, c.Bconst], writes=[vRB[half]])
                P.op("dve", lambda e: e.tensor_copy(out=vR[:, sl, :, 64:65],
                                                    in_=valid[:, T:T + 1].unsqueeze(1).to_broadcast([128, 8, 1])),
                     reads=[c.Bconst], writes=[vRB[half]])
            proj_tm(h, hB, t, 1024, ev)
        for ob in range(4):
            proj_fm(h, hB, 1536 + ob * 128, lambda bk, bB, ob=ob: P.op(
                "act", lambda e: e.activation(out=uT[:, ob, :], in_=bk, func=AF.Gelu), reads=[bB], writes=[uTB]))
        for t in range(4):
            s = t % 2

            def ev(bk, bB, t=t, s=s):
                P.op("act", lambda e: e.activation(out=vg[s], in_=bk, func=AF.Gelu), reads=[bB], writes=[vgB[s]])
                P.op("dve", lambda e: e.bn_stats(out=st[s][:, 0:6], in_=vg[s]), reads=[vgB[s]], writes=[stB[s]])
                P.op("dve", lambda e: e.bn_aggr(out=st[s][:, 6:8], in_=st[s][:, 0:6]), reads=[stB[s]], writes=[stB[s]])
                P.op("act", lambda e: e.activation(out=st[s][:, 9:10], in_=st[s][:, 7:8], func=AF.Ln, bias=c.epsc),
                     reads=[stB[s], c.Bconst], writes=[stB[s]])
                P.op("act", lambda e: e.activation(out=st[s][:, 8:9], in_=st[s][:, 9:10], func=AF.Exp, scale=-0.5),
                     reads=[stB[s]], writes=[stB[s]])
                P.op("dve", lambda e: e.tensor_scalar(out=vt[s], in0=vg[s], scalar1=st[s][:, 6:7], scalar2=st[s][:, 8:9],
                                                      op0=ALU.subtract, op1=ALU.mult), reads=[vgB[s], stB[s]], writes=[vtB[s]])
                P.op("pool", lambda e: e.tensor_tensor(out=vt[s], in0=vt[s], in1=lng, op=ALU.mult),
                     reads=[vtB[s], c.Bconst], writes=[vtB[s]])
                P.op("pool", lambda e: e.tensor_tensor(out=vln[:, t, :], in0=vt[s], in1=lnb, op=ALU.add),
                     reads=[vtB[s], c.Bconst], writes=[vlnB[t]])
            proj_tm(h, hB, t, 2048, ev)
        for t in range(4):
            bk, bB = c.bank()

            def mm(e, t=t, bk=bk):
                r = None
                for grp in range(4):
                    r = e.matmul(bk[:, grp * 128:(grp + 1) * 128], lhsT=vln[:, t, grp * 128:(grp + 1) * 128],
                                 rhs=wsT[:, grp, :], start=True, stop=True)
                return r
            P.op("pe", mm, reads=[vlnB[t], c.Bconst], writes=[bB])
            s = t % 2
            P.op("dve", (lambda bk, s: lambda e: e.tensor_tensor(out=gt[s], in0=bk, in1=bsb.rearrange("p a b -> p (a b)"),
                                                                 op=ALU.add))(bk, s),
                 reads=[bB, c.Bconst], writes=[gtB[s]])
            P.op("dve", (lambda s, t: lambda e: e.tensor_tensor(out=catT[:, 4:8, t * 128:(t + 1) * 128],
                                                                in0=gt[s].rearrange("p (a b) -> p a b", a=4),
                                                                in1=uT[:, :, t * 128:(t + 1) * 128], op=ALU.mult))(s, t),
                 reads=[gtB[s], uTB], writes=[catTBb])
        SPLIT = 3

        def emit_scores(h8):
            hp, hh = h8 // 2, h8 % 2
            pr = slice(hh * 64, hh * 64 + 64)
            for part, (ra, rb) in enumerate(((0, SPLIT), (SPLIT, 5))):
                def sm(e, ra=ra, rb=rb, pr=pr, hp=hp, g=g, qT=qT):
                    r = None
                    for rr in range(ra, rb):
                        for qb in range(4):
                            Tk = 4 * g + qb - rr
                            if Tk < 0:
                                continue
                            slk = Tk % 12
                            r = e.matmul(c.banks[rr][:, qb * 128:(qb + 1) * 128], lhsT=kR[pr, hp, slk * 128:(slk + 1) * 128],
                                         rhs=qT[pr, hp, qb * 128:(qb + 1) * 128], start=True, stop=True)
                    return r
                if g == 0 and ra > 3:
                    continue
                kth = sorted(set(((4 * g + qb - rr) // 4) % 3 for rr in range(ra, rb) for qb in range(4) if 4 * g + qb - rr >= 0))
                P.op("pe", sm, reads=[kRB[x] for x in kth] + [qTB], writes=[c.bankB[rr] for rr in range(ra, rb)])

        def emit_softmax(h8):
            pp = h8 % 2
            for part, (ra, rb) in enumerate(((0, SPLIT), (SPLIT, 5))):
                if g == 0 and ra > 3:
                    continue
                src = c.psall[:, ra * 512:rb * 512].rearrange("p (a b c) -> p a b c", a=rb - ra, b=4)
                P.op("act", (lambda src, pp, ra, rb: lambda e: e.activation(out=PP[pp][:, ra:rb, :, :], in_=src, func=AF.Exp))(src, pp, ra, rb),
                     reads=[c.bankB[rr] for rr in range(ra, rb)], writes=[PPB[pp][part]])
                P.op("dve", (lambda pp, ra, rb, h8: lambda e: e.tensor_tensor(
                    out=PP[pp][:, ra:rb, :, :], in0=PP[pp][:, ra:rb, :, :],
                    in1=Ep[:, h8, ra:rb, :].unsqueeze(2).to_broadcast([128, rb - ra, 4, 128]), op=ALU.mult))(pp, ra, rb, h8),
                    reads=[PPB[pp][part], c.Bconst], writes=[PPB[pp][part]])

        def emit_pv(h8):
            pp = h8 % 2
            ib = 5 + (h8 % 2)
            bk, bB = c.banks[ib], c.bankB[ib]

            def pv(e, bk=bk, h8=h8, pp=pp, g=g):
                r = None
                for qb in range(4):
                    rs_ = [rr for rr in range(5) if 4 * g + qb - rr >= 0]
                    for i, rr in enumerate(rs_):
                        Tk = 4 * g + qb - rr
                        r = e.matmul(bk[:, qb * 128:qb * 128 + 65], lhsT=PP[pp][:, rr, qb, :],
                                     rhs=vR[:, Tk % 12, h8, :], start=(i == 0), stop=(i == len(rs_) - 1))
                return r
            vth = [vRB[g % 3]] + ([vRB[(g - 1) % 3]] if g > 0 else [])
            P.op("pe", pv, reads=PPB[pp] + vth, writes=[bB])
            rs = h8 % 2
            bk3 = bk.rearrange("p (a b) -> p a b", a=4)
            P.op("dve", (lambda bk3, rs: lambda e: e.tensor_scalar(out=rden[rs], in0=bk3[:, :, 64], scalar1=1e-20, scalar2=None,
                                                                   op0=ALU.max))(bk3, rs), reads=[bB], writes=[rdenB[rs]])
            P.op("dve", (lambda rs: lambda e: e.reciprocal(out=rden[rs], in_=rden[rs]))(rs), reads=[rdenB[rs]], writes=[rdenB[rs]])
            P.op("dve", (lambda bk3, rs, h8: lambda e: e.tensor_tensor(
                out=aout[:, :, h8 * 64:(h8 + 1) * 64], in0=bk3[:, :, 0:64],
                in1=rden[rs].unsqueeze(2).to_broadcast([128, 4, 64]), op=ALU.mult))(bk3, rs, h8),
                reads=[bB, rdenB[rs]], writes=[aoutB])

        emit_scores(0)
        emit_softmax(0)
        for h8 in range(8):
            if h8 < 7:
                emit_scores(h8 + 1)
                emit_softmax(h8 + 1)
            emit_pv(h8)
        if g == 0:
            dbg(c, "ss", c.ss[0], [c.ssB[0]])
            dbg(c, "hT", h, [hB], BF16)
            dbg(c, "uT", uT, [uTB], BF16)
            dbg(c, "vln", vln, vlnB, BF16)
            dbg(c, "Ep", Ep, [c.Bconst], BF16)
            dbg(c, "aout", aout, [aoutB], BF16)
        for hf in range(2):
            bk, bB = c.bank()
            bkb = bk.bitcast(BF16)

            def tr(e, bkb=bkb, hf=hf):
                r = None
                for fb in range(2 * hf, 2 * hf + 2):
                    for t in range(4):
                        i = (fb - 2 * hf) * 4 + t
                        r = e.transpose(bkb[:, i * 128:(i + 1) * 128], aout[:, t, fb * 128:(fb + 1) * 128], c.ident)
                return r
            P.op("pe", tr, reads=[aoutB, c.Bconst], writes=[bB])
            P.op("act", (lambda bkb, hf: lambda e: e.copy(out=catT[:, 2 * hf:2 * hf + 2, :],
                                                          in_=bkb.rearrange("p (a b) -> p a b", a=2)))(bkb, hf),
                 reads=[bB], writes=[catTB])
        for t in range(4):
            bks = [c.bank(), c.bank()]
            for hf in range(2):
                bk = bks[hf][0]

                def mm(e, bk=bk, hf=hf, t=t):
                    r = None
                    for k in range(8):
                        r = e.matmul(bk, lhsT=catT[:, k, t * 128:(t + 1) * 128], rhs=Wout[:, k, hf * 512:(hf + 1) * 512],
                                     start=(k == 0), stop=(k == 7))
                    return r
                P.op("pe", mm, reads=[catTB, catTBb] + WoutB, writes=[bks[hf][1]])
            fin.append(post_norm_residual(c, g, t, [bks[0][0], bks[1][0]], [bks[0][1], bks[1][1]], gpost, src_dram, dst_dram))
        if g == 0:
            dbg(c, "catT", catT, [catTB, catTBb], BF16)
    return fin


def stage_ffn(c, L, src_dram, dst_dram, g0, NG):
    P, A = c.P, c.A
    I = c.inp
    A.off = c.stage_base_small
    c.n_ot = 1
    Wg = A.alloc([8, DFF], BF16)
    Wu = A.alloc([8, DFF], BF16)
    Wd = A.alloc([NFB, 1024], BF16)
    hTs = [A.alloc([8, 512], BF16) for _ in range(2)]
    hTBs = [Buf("f_hT0"), Buf("f_hT1")]
    actT = A.alloc([NFB, 512], BF16)
    actB = [Buf("f_act%d" % i) for i in range(NFB)]
    sg = [A.alloc([512], BF16) for _ in range(2)]
    sgB = [Buf("f_sg0"), Buf("f_sg1")]
    gpost = A.alloc([1024], F32)
    gcol = A.alloc([8], F32)
    P.dma("sp", lambda e: [e.dma_start(out=gpost, in_=I["gpostf%d" % L]), e.dma_start(out=gcol, in_=I["gcolf%d" % L])], 2,
          writes=[c.Bconst], pool="ldc", K=1)
    WgB = load_weight(c, Wg, I["ffn_w_gate%d" % L], 8, DFF, "wg", gcol=gcol)
    WuB = load_weight(c, Wu, I["ffn_w_up%d" % L], 8, DFF, "wu", gcol=gcol)
    WdB = load_weight(c, Wd, I["ffn_w_down%d" % L], NFB, 1024, "wd")
    fin = []
    norm_transpose_group(c, src_dram, g0, hTs[0], hTBs[0])
    for g in range(g0, g0 + NG):
        hT, hTB = hTs[(g - g0) % 2], hTBs[(g - g0) % 2]
        for fb in range(NFB):
            if g + 1 < g0 + NG and fb in (1, 6, 11, 16):
                pend_hb = norm_tile(c, src_dram, g + 1, (fb - 1) // 5)
            if g + 1 < g0 + NG and fb in (5, 10, 15, 20):
                transpose_tile(c, pend_hb, (fb - 5) // 5, hTs[(g + 1 - g0) % 2], hTBs[(g + 1 - g0) % 2])
            bg, bgB = c.bank()
            bu, buB = c.bank()

            def mm(e, fb=fb, bg=bg, bu=bu, hT=hT):
                r = None
                for k in range(8):
                    r = e.matmul(bg, lhsT=Wg[:, k, fb * 128:(fb + 1) * 128], rhs=hT[:, k, :], start=(k == 0), stop=(k == 7))
                for k in range(8):
                    r = e.matmul(bu, lhsT=Wu[:, k, fb * 128:(fb + 1) * 128], rhs=hT[:, k, :], start=(k == 0), stop=(k == 7))
                return r
            P.op("pe", mm, reads=[hTB] + WgB + WuB, writes=[bgB, buB])
            s = fb % 2
            P.op("act", (lambda bg, s: lambda e: e.activation(out=sg[s], in_=bg, func=AF.Silu))(bg, s), reads=[bgB], writes=[sgB[s]])
            P.op("dve", (lambda bu, s, fb: lambda e: e.tensor_tensor(out=actT[:, fb, :], in0=bu, in1=sg[s], op=ALU.mult))(bu, s, fb),
                 reads=[buB, sgB[s]], writes=[actB[fb]])
        for t in range(4):
            bks = [c.bank(), c.bank()]
            for hf in range(2):
                bk = bks[hf][0]

                def mm2(e, bk=bk, hf=hf, t=t):
                    r = None
                    for fb in range(NFB):
                        r = e.matmul(bk, lhsT=actT[:, fb, t * 128:(t + 1) * 128], rhs=Wd[:, fb, hf * 512:(hf + 1) * 512],
                                     start=(fb == 0), stop=(fb == NFB - 1))
                    return r
                P.op("pe", mm2, reads=actB + WdB, writes=[bks[hf][1]])
            fin.append(post_norm_residual(c, g, t, [bks[0][0], bks[1][0]], [bks[0][1], bks[1][1]], gpost, src_dram, dst_dram))
    c.n_ot = 2
    return fin


def stage_l1_mixer(c, src_dram, dst_dram, NG, g_own):
    P, A = c.P, c.A
    I = c.inp
    A.off = c.stage_base
    Win = A.alloc([8, 3088], BF16)
    Wout = A.alloc([8, 1024], BF16)
    hT = [A.alloc([8, 512], BF16) for _ in range(2)]
    hTB = [Buf("g_hT0"), Buf("g_hT1")]
    qTs = A.alloc([4, 512], F32)
    qTsB = Buf("g_qTs")
    kTs = A.alloc([4, 512], F32)
    kTsB = Buf("g_kTs")
    vsb = A.alloc([4, 1024], BF16)
    vsbB = [Buf("g_v%d" % t) for t in range(4)]
    sgb = A.alloc([4, 1024], BF16)
    sgbB = [Buf("g_sg%d" % t) for t in range(4)]
    sp = [A.alloc([512], F32) for _ in range(2)]
    spB = [Buf("g_sp0"), Buf("g_sp1")]
    e1, e1B = sp, spB
    eq = [A.alloc([4, 128], F32) for _ in range(2)]
    eqB = [Buf("g_eq0"), Buf("g_eq1")]
    ek = [A.alloc([4, 128], F32) for _ in range(2)]
    ekB = [Buf("g_ek0"), Buf("g_ek1")]
    qtT = [A.alloc([4, 128], BF16) for _ in range(2)]
    qtTB = [Buf("g_qt0"), Buf("g_qt1")]
    ktT = [A.alloc([4, 128], BF16) for _ in range(2)]
    ktTB = [Buf("g_kt0"), Buf("g_kt1")]
    ktm = [A.alloc([4, 128], BF16) for _ in range(2)]
    ktmB = [Buf("g_ktm0"), Buf("g_ktm1")]
    atT = [A.alloc([4, 128], BF16) for _ in range(2)]
    atTB = [Buf("g_at0"), Buf("g_at1")]
    S = A.alloc([4, 256], F32)
    SB = Buf("g_S")
    Sbf = A.alloc([4, 256], BF16)
    SbfB = Buf("g_Sbf")
    aT = A.alloc([512], F32)
    aTB = Buf("g_aT")
    w2 = A.alloc([512], F32)
    Utri = A.alloc([128], F32)
    tri = A.alloc([128], F32)
    normg = A.alloc([1024], F32)
    gpost = A.alloc([1024], F32)
    gcol = A.alloc([8], F32)
    tmp = A.alloc([1024], F32)
    tmpB = Buf("g_tmp")
    gated = [A.alloc([1024], BF16) for _ in range(2)]
    gatedB = [Buf("g_gated0"), Buf("g_gated1")]
    oT = A.alloc([8, 512], BF16)
    oTB = [Buf("g_oT%d" % t) for t in range(4)]
    r4 = [A.alloc([12], F32) for _ in range(2)]
    r4B = [Buf("g_r40"), Buf("g_r41")]

    def ld_consts(e):
        return [e.dma_start(out=w2[0:33, :], in_=I["w2aug"]), e.dma_start(out=tri, in_=I["tri"]),
                e.dma_start(out=normg, in_=I["normg"]), e.dma_start(out=gpost, in_=I["gpost1"]),
                e.dma_start(out=gcol, in_=I["gcol1"])]
    P.dma("sp", ld_consts, 5, writes=[c.Bconst], pool="ldc", K=1)
    P.op("dve", lambda e: e.tensor_scalar(out=Utri, in0=tri, scalar1=-1.0 / 16.0, scalar2=None, op0=ALU.mult),
         reads=[c.Bconst], writes=[c.Bconst])
    P.op("dve", lambda e: e.memset(aT[0:32, :], 0.0), writes=[aTB])
    P.op("dve", lambda e: e.memset(aT[32:33, :], 1.0), writes=[aTB])
    aT2 = tmp[:, 0:512]
    P.op("dve", lambda e: e.memset(aT2[0:32, :], 0.0), writes=[tmpB])
    P.op("dve", lambda e: e.memset(aT2[32:33, :], 1.0), writes=[tmpB])
    P.op("dve", lambda e: e.memset(S, 0.0), writes=[SB])
    P.op("dve", lambda e: e.memset(Sbf, 0.0), writes=[SbfB])
    WinB = load_weight(c, Win, I["c_w_in"], 8, 3088, "cwin", gcol=gcol)
    WoutB = load_weight(c, Wout, I["c_w_out"], 8, 1024, "cwout")
    QS = 128.0 ** -0.5

    def proj_fm(h, hB, col0, M, evac):
        bk, bB = c.bank()

        def mm(e):
            r = None
            for k in range(8):
                r = e.matmul(bk[0:M, :], lhsT=Win[:, k, col0:col0 + M], rhs=h[:, k, :], start=(k == 0), stop=(k == 7))
            return r
        P.op("pe", mm, reads=[hB] + WinB, writes=[bB])
        evac(bk, bB)

    def proj_tm(h, hB, t, col0, evac):
        bk, bB = c.bank()

        def mm(e):
            r = None
            for k in range(8):
                r = e.matmul(bk, lhsT=h[:, k, t * 128:(t + 1) * 128], rhs=Win[:, k, col0:col0 + 512],
                             start=(k == 0), stop=(k == 7))
            return r
        P.op("pe", mm, reads=[hB] + WinB, writes=[bB])
        evac(bk, bB)

    fin = []
    for g in range(NG):
        full = g >= g_own
        h, hB = hT[g % 2], hTB[g % 2]
        if (not full) and g % 2 == 1:
            kTs_g, kTsB_g, vsb_g, vsbB_g, aT_g, aTB_g = qTs, qTsB, sgb, sgbB, aT2, tmpB
        else:
            kTs_g, kTsB_g, vsb_g, vsbB_g, aT_g, aTB_g = kTs, kTsB, vsb, vsbB, aT, aTB
        norm_transpose_group(c, src_dram, g, h, hB)
        if full:
            for hd in range(4):
                proj_fm(h, hB, hd * 128, 128, lambda bk, bB, hd=hd: P.op(
                    "act", lambda e: e.activation(out=qTs[:, hd, :], in_=bk, func=AF.Copy, scale=QS), reads=[bB], writes=[qTsB]))
        for hd in range(4):
            proj_fm(h, hB, 512 + hd * 128, 128, lambda bk, bB, hd=hd, kTs_g=kTs_g, kTsB_g=kTsB_g: P.op(
                "act", lambda e: e.activation(out=kTs_g[:, hd, :], in_=bk, func=AF.Copy, scale=1.0), reads=[bB], writes=[kTsB_g]))
        proj_fm(h, hB, 3072, 16, lambda bk, bB, aT_g=aT_g, aTB_g=aTB_g: P.op(
            "act", lambda e: e.activation(out=aT_g[0:16, :], in_=bk[0:16, :], func=AF.Copy, scale=1.0), reads=[bB], writes=[aTB_g]))
        for t in range(4):
            for hf in range(2):
                proj_tm(h, hB, t, 1024 + hf * 512, lambda bk, bB, t=t, hf=hf, vsb_g=vsb_g, vsbB_g=vsbB_g: P.op(
                    "dve", lambda e: e.tensor_copy(out=vsb_g[:, t, hf * 512:(hf + 1) * 512], in_=bk), reads=[bB], writes=[vsbB_g[t]]))
            if full:
                for hf in range(2):
                    proj_tm(h, hB, t, 2048 + hf * 512, lambda bk, bB, t=t, hf=hf: P.op(
                        "act", lambda e: e.activation(out=sgb[:, t, hf * 512:(hf + 1) * 512], in_=bk, func=AF.Silu),
                        reads=[bB], writes=[sgbB[t]]))
        def make_steps(t, g=g, full=full, h=h, hB=hB, kTs=kTs_g, kTsB=kTsB_g, vsb=vsb_g, vsbB=vsbB_g, aT=aT_g, aTB=aTB_g):
            s = t % 2
            st8 = {}
            steps = []

            def s0():
                bl, blB = c.bank()
                st8["bl"] = (bl, blB)
                P.op("pe", (lambda bl, t: lambda e: e.matmul(bl, lhsT=aT[0:33, t * 128:(t + 1) * 128], rhs=w2[0:33, :],
                                                             start=True, stop=True))(bl, t),
                     reads=[aTB, c.Bconst], writes=[blB])
            steps.append(s0)

            def s1():
                bl, blB = st8["bl"]
                P.op("act", (lambda bl, s: lambda e: e.activation(out=e1[s], in_=bl, func=AF.Exp, scale=-1.0))(bl, s),
                     reads=[blB], writes=[e1B[s]])
                P.op("act", (lambda s: lambda e: e.activation(out=sp[s], in_=e1[s], func=AF.Ln, bias=1.0))(s),
                     reads=[e1B[s]], writes=[spB[s]])
            steps.append(s1)

            def s2():
                bc, bcB = c.bank()
                st8["bc"] = (bc, bcB)

                def cm(e, bc=bc, s=s):
                    r = None
                    for hd in range(4):
                        r = e.matmul(bc[:, hd * 128:(hd + 1) * 128], lhsT=sp[s][:, hd * 128:(hd + 1) * 128], rhs=Utri,
                                     start=True, stop=True)
                    return r
                P.op("pe", cm, reads=[spB[s], c.Bconst], writes=[bcB])
            steps.append(s2)

            def s3():
                bc, bcB = st8["bc"]
                bc3 = bc.rearrange("p (a b) -> p a b", a=4)
                P.op("act", (lambda bc3, s: lambda e: e.activation(out=ek[s], in_=bc3, func=AF.Exp, scale=-1.0))(bc3, s),
                     reads=[bcB], writes=[ekB[s]])
                if full:
                    P.op("act", (lambda bc3, s: lambda e: e.activation(out=eq[s], in_=bc3, func=AF.Exp))(bc3, s),
                         reads=[bcB], writes=[eqB[s]])
                else:
                    P.op("act", (lambda bc3, s: lambda e: e.activation(out=eq[s][:, :, 127:128], in_=bc3[:, :, 127:128], func=AF.Exp))(bc3, s),
                         reads=[bcB], writes=[eqB[s]])
            steps.append(s3)

            def s4():
                P.op("dve", (lambda s, t: lambda e: e.tensor_tensor(out=ktT[s], in0=kTs[:, :, t * 128:(t + 1) * 128], in1=ek[s],
                                                                    op=ALU.mult))(s, t),
                     reads=[kTsB, ekB[s]], writes=[ktTB[s]])
                if full:
                    P.op("dve", (lambda s, t: lambda e: e.tensor_tensor(out=qtT[s], in0=qTs[:, :, t * 128:(t + 1) * 128], in1=eq[s],
                                                                        op=ALU.mult))(s, t),
                         reads=[qTsB, eqB[s]], writes=[qtTB[s]])
            steps.append(s4)

            def s5():
                if full:
                    ba, baB = c.bank()
                    st8["ba"] = (ba, baB)

                    def am(e, ba=ba, s=s):
                        r = None
                        for hd in range(4):
                            r = e.matmul(ba[:, hd * 128:(hd + 1) * 128], lhsT=ktT[s][:, hd, :], rhs=qtT[s][:, hd, :],
                                         start=True, stop=True)
                        return r
                    P.op("pe", am, reads=[ktTB[s], qtTB[s]], writes=[baB])
                bt, btB = c.bank()
                st8["bt"] = (bt, btB)
                btb = bt.bitcast(BF16)

                def ktr(e, btb=btb, s=s):
                    r = None
                    for hd in range(4):
                        r = e.transpose(btb[:, hd * 128:(hd + 1) * 128], ktT[s][:, hd, :], c.ident)
                    return r
                P.op("pe", ktr, reads=[ktTB[s], c.Bconst], writes=[btB])
            steps.append(s5)

            def s6():
                if full:
                    ba, baB = st8["ba"]
                    P.op("dve", (lambda ba, s: lambda e: e.tensor_tensor(
                        out=atT[s], in0=ba.rearrange("p (a b) -> p a b", a=4),
                        in1=tri.unsqueeze(1).to_broadcast([128, 4, 128]), op=ALU.mult))(ba, s),
                        reads=[baB, c.Bconst], writes=[atTB[s]])
                bt, btB = st8["bt"]
                btb = bt.bitcast(BF16)
                P.op("act", (lambda btb, s: lambda e: e.copy(out=ktm[s], in_=btb[:, 0:512].rearrange("p (a b) -> p a b", a=4)))(btb, s),
                     reads=[btB], writes=[ktmB[s]])
            steps.append(s6)

            def s7():
                kv = [c.bank(), c.bank()]
                st8["kv"] = kv

                def kvm(e, kv=kv, s=s, t=t):
                    r = None
                    for hd in range(4):
                        r = e.matmul(kv[hd // 2][0][:, (hd % 2) * 256:(hd % 2 + 1) * 256], lhsT=ktm[s][:, hd, :],
                                     rhs=vsb[:, t, hd * 256:(hd + 1) * 256], start=True, stop=True)
                    return r
                P.op("pe", kvm, reads=[ktmB[s], vsbB[t]], writes=[kv[0][1], kv[1][1]])
                if full:
                    ob = [c.bank(), c.bank()]
                    st8["ob"] = ob

                    def om(e, ob=ob, s=s, t=t):
                        r = None
                        for hd in range(4):
                            o_ap = ob[hd // 2][0][:, (hd % 2) * 256:(hd % 2 + 1) * 256]
                            e.matmul(o_ap, lhsT=atT[s][:, hd, :], rhs=vsb[:, t, hd * 256:(hd + 1) * 256], start=True, stop=False)
                            r = e.matmul(o_ap, lhsT=qtT[s][:, hd, :], rhs=Sbf[:, hd, :], start=False, stop=True)
                        return r
                    P.op("pe", om, reads=[atTB[s], qtTB[s], vsbB[t], SbfB], writes=[ob[0][1], ob[1][1]])
            steps.append(s7)

            def s8():
                kv = st8["kv"]
                for hf in range(2):
                    P.op("dve", (lambda hf, kv: lambda e: e.tensor_tensor(
                        out=S[:, 2 * hf:2 * hf + 2, :], in0=kv[hf][0].rearrange("p (a b) -> p a b", a=2), in1=S[:, 2 * hf:2 * hf + 2, :],
                        op=ALU.add))(hf, kv), reads=[kv[hf][1], SB], writes=[SB])
                P.op("dve", (lambda s: lambda e: e.tensor_tensor(out=S, in0=S, in1=eq[s][:, :, 127:128].to_broadcast([128, 4, 256]),
                                                                 op=ALU.mult))(s), reads=[SB, eqB[s]], writes=[SB])
                P.op("pool", lambda e: e.tensor_copy(out=Sbf, in_=S), reads=[SB], writes=[SbfB])
            steps.append(s8)
            if not full:
                return steps

            def s9():
                ob = st8["ob"]
                rs = c.r4i % 2
                c.r4i += 1
                r4_, r4B_ = r4[rs], r4B[rs]
                st8["r4"] = (r4_, r4B_)
                for hd in range(4):
                    P.op("act", (lambda hd, ob, r4_: lambda e: e.activation(
                        out=c.junk[:, 0:256], in_=ob[hd // 2][0][:, (hd % 2) * 256:(hd % 2 + 1) * 256], func=AF.Square,
                        accum_out=r4_[:, hd:hd + 1]))(hd, ob, r4_), reads=[ob[hd // 2][1]], writes=[r4B_, c.Bjunk])
                P.op("act", (lambda r4_: lambda e: e.activation(out=r4_[:, 4:8], in_=r4_[:, 0:4], func=AF.Ln, scale=1.0 / 256.0,
                                                                bias=c.epsc))(r4_), reads=[r4B_, c.Bconst], writes=[r4B_])
                P.op("act", (lambda r4_: lambda e: e.activation(out=r4_[:, 8:12], in_=r4_[:, 4:8], func=AF.Exp, scale=-0.5))(r4_),
                     reads=[r4B_], writes=[r4B_])
            steps.append(s9)

            def s10():
                ob = st8["ob"]
                r4_, r4B_ = st8["r4"]
                for hd in range(4):
                    P.op("dve", (lambda hd, ob, r4_: lambda e: e.scalar_tensor_tensor(
                        out=tmp[:, hd * 256:(hd + 1) * 256], in0=ob[hd // 2][0][:, (hd % 2) * 256:(hd % 2 + 1) * 256],
                        scalar=r4_[:, 8 + hd:9 + hd], in1=normg[:, hd * 256:(hd + 1) * 256], op0=ALU.mult, op1=ALU.mult))(hd, ob, r4_),
                        reads=[ob[hd // 2][1], r4B_, c.Bconst], writes=[tmpB])
                gd, gdB = gated[t % 2], gatedB[t % 2]
                P.op("pool", (lambda gd, t: lambda e: e.tensor_tensor(out=gd, in0=tmp, in1=sgb[:, t, :], op=ALU.mult))(gd, t),
                     reads=[tmpB, sgbB[t]], writes=[gdB])
            steps.append(s10)

            def s11():
                gd, gdB = gated[t % 2], gatedB[t % 2]
                bo, boB = c.bank()
                bob = bo.bitcast(BF16)

                def otr(e, bob=bob, gd=gd):
                    r = None
                    for k in range(8):
                        r = e.transpose(bob[:, k * 128:(k + 1) * 128], gd[:, k * 128:(k + 1) * 128], c.ident)
                    return r
                P.op("pe", otr, reads=[gdB, c.Bconst], writes=[boB])
                P.op("act", (lambda bob, t: lambda e: e.copy(out=oT[:, :, t * 128:(t + 1) * 128],
                                                             in_=bob.rearrange("p (k n) -> p k n", k=8)))(bob, t),
                     reads=[boB], writes=[oTB[t]])
            steps.append(s11)

            def s12():
                bks = [c.bank(), c.bank()]
                st8["bks"] = bks
                for hf in range(2):
                    bk = bks[hf][0]

                    def mm3(e, bk=bk, hf=hf, t=t):
                        r = None
                        for k in range(8):
                            r = e.matmul(bk, lhsT=oT[:, k, t * 128:(t + 1) * 128], rhs=Wout[:, k, hf * 512:(hf + 1) * 512],
                                         start=(k == 0), stop=(k == 7))
                        return r
                    P.op("pe", mm3, reads=[oTB[t]] + WoutB, writes=[bks[hf][1]])
            steps.append(s12)

            def s13():
                bks = st8["bks"]
                fin.append(post_norm_residual(c, g, t, [bks[0][0], bks[1][0]], [bks[0][1], bks[1][1]], gpost, src_dram, dst_dram))
            steps.append(s13)
            return steps

        LAG = 3
        for ta, tb in ((0, 1), (2, 3)):
            sa, sb_ = make_steps(ta), make_steps(tb)
            n = len(sa)
            for k in range(n + LAG):
                if k < n:
                    sa[k]()
                if 0 <= k - LAG < n:
                    sb_[k - LAG]()
    return fin

INPUT_SHAPES = {
    "ident": ([128, 128], F32),
}


def build(NT, stages, inputs_np, debug=False, g_own=0):
    nc = bass.Bass("TRN2", target_bir_lowering=False)
    c = Ctx()
    c.nc = nc
    c.debug = debug
    c.g_own = g_own
    c.dbg_ops = []
    c.P = Prog(nc)
    c.NT = NT
    c.NTILES = NT // 128
    c.inp = {}
    for name, arr in inputs_np.items():
        c.inp[name] = nc.dram_tensor(name, list(arr.shape), F32, kind="ExternalInput").ap()
    y = nc.dram_tensor("y", [NT, D], F32, kind="ExternalOutput").ap()
    c.Bstream = [Buf("ys%d" % i) for i in range(c.NTILES)]
    A = Arena(nc, 204 * 1024)
    c.A = A
    c.identf = A.alloc([128], F32)
    c.ident = A.alloc([128], BF16)
    c.junk = A.alloc([1024], BF16)
    c.epsc = A.alloc([1], F32)
    c.Bjunk = Buf("junk")
    c.Bconst = Buf("const")
    c.xt = [A.alloc([1024], F32) for _ in range(3)]
    c.xtB = [Buf("xt%d" % i) for i in range(3)]
    c.ss = [A.alloc([4], F32) for _ in range(3)]
    c.ssB = [Buf("ss%d" % i) for i in range(3)]
    c.hb = [A.alloc([1024], BF16) for _ in range(2)]
    c.hbB = [Buf("hb0"), Buf("hb1")]
    c.pss = [A.alloc([8], F32) for _ in range(2)]
    c.pssB = [Buf("pss0"), Buf("pss1")]
    ot0 = A.alloc([1024], F32)
    xr0 = A.alloc([1024], F32)
    c.stage_base_small = A.off
    c.ot = [ot0, A.alloc([1024], F32)]
    c.otB = [Buf("ot0"), Buf("ot1")]
    c.xr = [xr0, A.alloc([1024], F32)]
    c.xrB = [Buf("xr0"), Buf("xr1")]
    c.n_ot = 2
    c.xi = 0
    c.oi = 0
    c.pei = 0
    c.hbi = 0
    c.r4i = 0
    c.stage_base = A.off
    psall = nc.alloc_psum_tensor("psall", [128, 4096], F32)
    c.psall = psall[:]
    banks = [psall[:, i * 512:(i + 1) * 512] for i in range(8)]
    bankB = [Buf("bank%d" % i, excl=True) for i in range(8)]
    c.banks = banks
    c.bankB = bankB
    c.bi = 0

    def bank():
        i = c.bi % 8
        c.bi += 1
        return banks[i], bankB[i]
    c.bank = bank
    P = c.P
    P.dma("sp", lambda e: [e.dma_start(out=c.identf, in_=c.inp["ident"])], 1, writes=[c.Bconst], pool="ldc", K=1)
    P.op("dve", lambda e: e.tensor_copy(out=c.ident, in_=c.identf), reads=[c.Bconst], writes=[c.Bconst])
    P.op("dve", lambda e: e.memset(c.epsc, EPS), reads=[], writes=[c.Bconst])
    fin = []
    src = c.inp["x"]
    for stg in stages:
        if stg == "l0mix":
            fin = stage_l0_mixer(c, src, y, NT // 512)
        elif stg == "l1mix":
            fin = stage_l1_mixer(c, src, y, NT // 512, c.g_own)
        elif stg == "ffn0":
            fin = stage_ffn(c, 0, src, y, 0, NT // 512)
        elif stg == "ffn1":
            fin = stage_ffn(c, 1, src, y, c.g_own, NT // 512 - c.g_own)
        else:
            raise ValueError(stg)
        src = y
        P.fence()
    P.emit(final_wait_ops=list(fin) + c.dbg_ops)
    return nc


def rep128(v):
    return np.ascontiguousarray(np.broadcast_to(np.asarray(v, np.float32).reshape(1, -1), (128, v.size)))


def col128(v):
    v = np.asarray(v, np.float32)
    return np.ascontiguousarray(v.reshape(-1, 128).T)


def prep_common(inp):
    d = {}
    d["ident"] = np.eye(128, dtype=np.float32)
    rb = np.asarray(inp["a_rel_bias"][0], np.float32)
    kj = np.arange(128)[:, None, None]
    r = np.arange(5)[None, :, None]
    qi = np.arange(128)[None, None, :]
    idx = np.clip(r * 128 + qi - kj, -256, 256) + 256
    d["biasT"] = np.ascontiguousarray(rb[:, idx].transpose(1, 0, 2, 3))
    d["wsT"] = np.ascontiguousarray(np.asarray(inp["b_w_s"][0], np.float32).transpose(2, 0, 1))
    d["tri"] = (np.arange(128)[:, None] <= np.arange(128)[None, :]).astype(np.float32)
    d["bsb"] = np.ascontiguousarray(np.broadcast_to(np.asarray(inp["b_b_s"][0], np.float32)[None], (128, 4, 128)))
    d["lng"] = rep128(inp["b_ln_g"][0])
    d["lnb"] = rep128(inp["b_ln_b"][0])
    d["gpost0"] = rep128(inp["post_mix_g"][0])
    d["gcol0"] = col128(inp["pre_mix_g"][0])
    d["ab_w_in"] = np.ascontiguousarray(inp["ab_w_in"][0], dtype=np.float32)
    d["ab_w_out"] = np.ascontiguousarray(inp["ab_w_out"][0], dtype=np.float32)
    d["gcol1"] = col128(inp["pre_mix_g"][1])
    d["gpost1"] = rep128(inp["post_mix_g"][1])
    d["normg"] = rep128(inp["c_norm_g"][0])
    w2 = np.zeros((33, 512), np.float32)
    w2[0:16] = np.asarray(inp["c_w_a2"][0], np.float32)
    w2[32] = np.asarray(inp["c_b_a"][0], np.float32)
    d["w2aug"] = w2
    d["c_w_in"] = np.ascontiguousarray(inp["c_w_in"][0], dtype=np.float32)
    d["c_w_out"] = np.ascontiguousarray(inp["c_w_out"][0], dtype=np.float32)
    for L in range(2):
        d["gpostf%d" % L] = rep128(inp["post_ffn_g"][L])
        d["gcolf%d" % L] = col128(inp["pre_ffn_g"][L])
        d["ffn_w_gate%d" % L] = np.ascontiguousarray(inp["ffn_w_gate"][L], dtype=np.float32)
        d["ffn_w_up%d" % L] = np.ascontiguousarray(inp["ffn_w_up"][L], dtype=np.float32)
        d["ffn_w_down%d" % L] = np.ascontiguousarray(inp["ffn_w_down"][L], dtype=np.float32)
    return d


STAGES = ["l0mix", "ffn0", "l1mix", "ffn1"]
SEQ_HALF = 4096
_CACHE = {}


def kernel(**inputs):
    inp = {k: np.asarray(v) for k, v in inputs.items()}
    x = np.asarray(inp["x"], np.float32)
    B = x.shape[0]
    common = prep_common(inp)
    NT = 2 * SEQ_HALF
    in_maps = []
    for core in range(8):
        b, half = core // 2, core % 2
        d = dict(common)
        own = x[b, half * SEQ_HALF:(half + 1) * SEQ_HALF]
        if half == 1:
            prev = x[b, 0:SEQ_HALF]
            vprev = np.ones(SEQ_HALF, np.float32)
        else:
            prev = np.zeros_like(own)
            vprev = np.zeros(SEQ_HALF, np.float32)
        d["x"] = np.ascontiguousarray(np.concatenate([prev, own], axis=0))
        d["valid"] = col128(np.concatenate([vprev, np.ones(SEQ_HALF, np.float32)]))
        in_maps.append(d)
    if "nc" not in _CACHE:
        _CACHE["nc"] = build(NT, STAGES, in_maps[0], g_own=SEQ_HALF // 512)
    nc = _CACHE["nc"]
    res = run_bass_kernel_spmd(nc, in_maps, core_ids=list(range(8)))
    out = np.empty((B, 2 * SEQ_HALF, D), np.float32)
    for core in range(8):
        b, half = core // 2, core % 2
        out[b, half * SEQ_HALF:(half + 1) * SEQ_HALF] = np.asarray(res.results[core]["y"])[SEQ_HALF:]
    return out
```

```python
import contextlib
import numpy as np
import concourse.bass as bass
import concourse.mybir as mybir
from concourse.bass_utils import run_bass_kernel_spmd

F32 = mybir.dt.float32
BF16 = mybir.dt.bfloat16
ALU = mybir.AluOpType
AF = mybir.ActivationFunctionType

D = 1024
DFF = 2816
NFB = DFF // 128
EPS = 1e-6
ENGS = ("pe", "act", "dve", "pool", "sp")


class Buf:
    __slots__ = ("name", "excl", "last_w", "readers")

    def __init__(self, name, excl=False):
        self.name = name
        self.excl = excl
        self.last_w = None
        self.readers = []


class Op:
    __slots__ = ("eng", "fn", "deps", "dma", "signal", "ticket", "ndma", "semkey", "odeps", "cost", "table",
                 "seg", "fdep", "idx", "nun", "ready", "fin", "users")

    def __init__(self, eng, fn, dma):
        self.eng = eng
        self.fn = fn
        self.deps = []
        self.dma = dma
        self.signal = False
        self.ticket = None
        self.ndma = 0
        self.semkey = None
        self.odeps = []
        self.cost = 0.5
        self.table = None
        self.seg = 0
        self.fdep = None
        self.idx = 0


class _Dummy:
    def then_inc(self, *a, **k):
        return self


class CostEngine:
    def __init__(self, eng):
        self.eng = eng
        self.cost = 0.0
        self.table = None
        self.bytes = 0

    def _free(self, ap):
        n = 1
        for d in ap.shape[1:]:
            n *= d
        return n

    def __getattr__(self, name):
        def f(*a, **k):
            out = k.get("out", a[0] if a else None)
            cols = self._free(out) if out is not None and hasattr(out, "shape") else 1
            if name in ("matmul", "transpose"):
                lhs = k.get("lhsT", a[1] if len(a) > 1 else None)
                mult = 4.0 if (lhs is not None and lhs.dtype == F32) else 1.0
                self.cost += mult * max(cols, 64) / 2400.0 + 0.035
            elif name == "dma_start":
                src = k.get("in_", a[1] if len(a) > 1 else None)
                nb = out.shape[0] * cols * (4 if out.dtype == F32 else 2)
                self.bytes += nb
                self.cost += 0.05
            elif self.eng == "act":
                self.cost += 0.25 + cols / 1200.0
                fn = k.get("func", None)
                if fn in (AF.Exp, AF.Ln):
                    self.table = "A"
                elif fn == AF.Gelu:
                    self.table = "B"
                elif fn == AF.Silu:
                    self.table = "C"
            elif self.eng == "dve":
                self.cost += 0.13 + cols / 960.0
            else:
                self.cost += 0.2 + cols / 480.0
            return _Dummy()
        return f


class Prog:
    def __init__(self, nc):
        self.nc = nc
        self.ops = []
        self.pools = {}
        self.last = {e: None for e in ENGS}
        self.seg = 0
        self.pending_fence = {e: None for e in ENGS}
        self.reorder = True
        self.pe_mix = 0

    def _add(self, eng, fn, reads, writes, dma):
        op = Op(eng, fn, dma)
        deps = []

        def need(o, kind):
            if o is None or o is op:
                return
            if o.eng == eng and not dma and not o.dma:
                if eng == "pe":
                    op.odeps.append(o)
                    return
            deps.append(o)

        for b in reads:
            need(b.last_w, "raw")
            if b.excl:
                for r in b.readers:
                    need(r, "raw")
        for b in writes:
            need(b.last_w, "waw")
            for r in b.readers:
                need(r, "war")
        for b in reads:
            if b.excl:
                b.last_w = op
                b.readers = []
            else:
                b.readers.append(op)
        for b in writes:
            b.last_w = op
            b.readers = []
        if self.pending_fence[eng] is not None:
            op.fdep = self.pending_fence[eng]
            self.pending_fence[eng] = None
        seen = set()
        for d in deps:
            if id(d) not in seen:
                seen.add(id(d))
                op.deps.append(d)
                d.signal = True
        op.seg = self.seg
        op.idx = len(self.ops)
        if dma:
            lastd = self.last.get(("dma", eng, dma))
            if lastd is not None:
                op.odeps.append(lastd)
            self.last[("dma", eng, dma)] = op
        ce = CostEngine(eng)
        fn(ce)
        op.cost = ce.cost if not dma else (2.0 + ce.bytes / 120000.0)
        op.table = ce.table
        self.ops.append(op)
        return op

    def op(self, eng, fn, reads=(), writes=()):
        return self._add(eng, fn, list(reads), list(writes), False)

    def dma(self, eng, fn, n, reads=(), writes=(), pool="ld", K=6):
        op = self._add(eng, fn, list(reads), list(writes), pool)
        op.ndma = n
        pl = self.pools.setdefault(pool, [K, 0, {}])
        i = pl[1]
        pl[1] += 1
        slot = i % pl[0]
        prev = pl[2].get(slot)
        if prev is not None and prev not in op.deps:
            op.deps.append(prev)
        pl[2][slot] = op
        op.semkey = (pool, slot)
        return op

    def fence(self):
        for e in ENGS:
            self.pending_fence[e] = self.seg
        self.seg += 1

    def schedule(self):
        order = {e: [] for e in ENGS}
        self.seg_tail = {}
        nseg = self.seg + 1
        by_seg = [[] for _ in range(nseg)]
        for op in self.ops:
            by_seg[op.seg].append(op)
        for k, ops in enumerate(by_seg):
            for e in ENGS:
                ops_e = [o for o in ops if o.eng == e]
                if k > 0 and ops_e:
                    assert ops_e[0].fdep == k - 1, (e, k, ops_e[0].fdep)
                    for o in ops_e[1:]:
                        o.odeps.append(ops_e[0])
            if not self.reorder:
                for op in ops:
                    order[op.eng].append(op)
            else:
                self._sched_segment(ops, order)
            tail = []
            for e in ENGS:
                comp = [o for o in order[e] if o.seg == k and not o.dma]
                if comp:
                    tail.append(comp[-1])
            tail += [o for o in ops if o.dma]
            self.seg_tail[k] = tail
        return order

    def _sched_segment(self, ops, order):
        import heapq
        inseg = set(id(o) for o in ops)
        for o in ops:
            o.users = []
            o.nun = 0
            o.ready = 0.0
            o.fin = None
        for o in ops:
            for d in o.deps + o.odeps:
                if id(d) in inseg:
                    d.users.append(o)
                    o.nun += 1
        main = {e: [] for e in ENGS}
        avail = {e: [] for e in ENGS}
        free = {e: 0.0 for e in ENGS}
        cur_table = None
        small_run = [0]
        for o in ops:
            if o.nun == 0:
                heapq.heappush(main[o.eng], (o.ready, o.idx, o))
        nleft = len(ops)
        while nleft:
            best = None
            for e in ENGS:
                m, a = main[e], avail[e]
                while m and m[0][0] <= free[e] + 1e-9:
                    r, ix, o = heapq.heappop(m)
                    heapq.heappush(a, (ix, o))
                if a:
                    cand = (free[e], a[0][0], e, True)
                elif m:
                    cand = (m[0][0], m[0][1], e, False)
                else:
                    continue
                if best is None or cand[:2] < best[:2]:
                    best = cand
            st, _, e, from_avail = best
            if from_avail:
                a = avail[e]
                if e == "act" and len(a) > 1:
                    pick = None
                    small = heapq.nsmallest(6, a)
                    for ix, o in small:
                        if o.table is None or o.table == cur_table:
                            pick = (ix, o)
                            break
                    if pick is None or pick is small[0]:
                        ix, o = heapq.heappop(a)
                    else:
                        a.remove(pick)
                        heapq.heapify(a)
                        ix, o = pick
                elif e == "pe" and len(a) > 1 and self.pe_mix and small_run[0] >= self.pe_mix:
                    pick = None
                    for ix, o in a:
                        if o.cost >= 1.0 and (pick is None or ix < pick[0]):
                            pick = (ix, o)
                    if pick is None:
                        ix, o = heapq.heappop(a)
                    else:
                        a.remove(pick)
                        heapq.heapify(a)
                        ix, o = pick
                else:
                    ix, o = heapq.heappop(a)
            else:
                r, ix, o = heapq.heappop(main[e])
            if e == "pe":
                small_run[0] = small_run[0] + 1 if o.cost < 1.0 else 0
            if e == "act" and o.table is not None:
                if o.table != cur_table:
                    st += 1.3
                cur_table = o.table
            o.fin = st + o.cost
            free[e] = st + (0.06 if o.dma else o.cost)
            order[e].append(o)
            nleft -= 1
            for u in o.users:
                u.nun -= 1
                if u.eng == e and not o.dma and (o not in u.deps):
                    r = st
                else:
                    r = o.fin + (0.3 if (u.eng == e and not o.dma) else 1.2)
                if r > u.ready:
                    u.ready = r
                if u.nun == 0:
                    heapq.heappush(main[u.eng], (u.ready, u.idx, u))

    def emit(self, final_wait_ops=()):
        nc = self.nc
        with contextlib.ExitStack() as st:
            esem = {e: st.enter_context(nc.semaphore("s_" + e)) for e in ENGS}
            dsem = {}
            dcnt = {}
            for op in self.ops:
                if op.dma and op.semkey not in dsem:
                    dsem[op.semkey] = st.enter_context(nc.semaphore("d_%s%d" % op.semkey))
                    dcnt[op.semkey] = 0
            block = st.enter_context(nc.Block())
            order = self.schedule()
            for op in self.ops:
                if op.fdep is not None:
                    for o in self.seg_tail[op.fdep]:
                        if o is not op and not (o.eng == op.eng and not o.dma and not op.dma):
                            op.deps.append(o)
                            o.signal = True
            ecnt = {e: 0 for e in ENGS}
            for e in ENGS:
                for op in order[e]:
                    if op.dma:
                        dcnt[op.semkey] += 16 * op.ndma
                        op.ticket = (dsem[op.semkey], dcnt[op.semkey])
                    elif op.signal:
                        ecnt[op.eng] += 1
                        op.ticket = (esem[op.eng], ecnt[op.eng])
            final = list(final_wait_ops)

            def run(engname):
                def body(eng):
                    waited = {}
                    for op in order[engname]:
                        for d in op.deps:
                            sem, val = d.ticket
                            k = id(sem)
                            if waited.get(k, 0) < val:
                                eng.wait_ge(sem, val)
                                waited[k] = val
                        res = op.fn(eng)
                        if op.dma:
                            assert len(res) == op.ndma, (len(res), op.ndma)
                            for ins in res:
                                ins.then_inc(op.ticket[0], 16)
                        elif op.signal:
                            res.then_inc(op.ticket[0], 1)
                    if engname == "sp":
                        for d in final:
                            sem, val = d.ticket
                            eng.wait_ge(sem, val)
                return body

            block.tensor(run("pe"))
            block.scalar(run("act"))
            block.vector(run("dve"))
            block.gpsimd(run("pool"))
            block.sync(run("sp"))


class Arena:
    def __init__(self, nc, nbytes):
        self.t = nc.alloc_sbuf_tensor("arena", [128, nbytes // 2], BF16)
        self.nbytes = nbytes
        self.off = 0

    def alloc(self, shape, dtype):
        n = int(np.prod(shape))
        sz = n * (4 if dtype == F32 else 2)
        self.off = (self.off + 31) // 32 * 32
        assert self.off + sz <= self.nbytes, ("arena overflow", self.off, sz, self.nbytes)
        o2 = self.off // 2
        ap = self.t[:, o2:o2 + sz // 2]
        if dtype == F32:
            ap = ap.bitcast(F32)
        self.off += sz
        if len(shape) == 2:
            ap = ap.rearrange("p (a b) -> p a b", a=shape[0])
        elif len(shape) == 3:
            ap = ap.rearrange("p (a b c) -> p a b c", a=shape[0], b=shape[1])
        return ap


class Ctx:
    pass


def dbg(c, name, ap, bufs, dtype=F32):
    if not getattr(c, "debug", False):
        return
    shp = list(ap.shape)
    o = c.nc.dram_tensor("dbg_" + name, shp, dtype, kind="ExternalOutput").ap()
    c.dbg_ops.append(c.P.dma("sp", lambda e: [e.dma_start(out=o, in_=ap)], 1, reads=list(bufs), pool="dbg", K=1))


def load_weight(c, dst, src, nk, cols, name, gcol=None, chunk=None):
    P = c.P
    bufs = []
    for k in range(nk):
        b = Buf("%s_k%d" % (name, k))
        bufs.append(b)
        P.dma("pool", (lambda k: lambda e: [e.dma_start(out=dst[:, k, :], in_=src[k * 128:(k + 1) * 128, :])])(k),
              1, writes=[b], pool="w", K=6)
        if gcol is not None:
            P.op("dve", (lambda k: lambda e: e.tensor_scalar(out=dst[:, k, :], in0=dst[:, k, :], scalar1=gcol[:, k:k + 1],
                                                             scalar2=None, op0=ALU.mult))(k),
                 reads=[b, c.Bconst], writes=[b])
    return bufs


def norm_transpose_group(c, src_dram, g, hT, hTbuf):
    for t in range(4):
        norm_transpose_tile(c, src_dram, g, t, hT, hTbuf)


def norm_transpose_tile(c, src_dram, g, t, hT, hTbuf):
    hb = norm_tile(c, src_dram, g, t)
    transpose_tile(c, hb, t, hT, hTbuf)


def norm_tile(c, src_dram, g, t):
    P = c.P
    T = 4 * g + t
    s = c.xi % 3
    c.xi += 1
    xt, xb = c.xt[s], c.xtB[s]
    P.dma("sp", (lambda xt, T: lambda e: [e.dma_start(out=xt, in_=src_dram[T * 128:(T + 1) * 128, :])])(xt, T),
          1, reads=[c.Bstream[T]], writes=[xb], pool="ldx", K=3)
    ss, ssB = c.ss[s], c.ssB[s]
    P.op("act", (lambda xt, ss: lambda e: e.activation(out=c.junk, in_=xt, func=AF.Square, accum_out=ss[:, 0:1]))(xt, ss),
         reads=[xb], writes=[ssB, c.Bjunk])
    P.op("act", (lambda ss: lambda e: e.activation(out=ss[:, 1:2], in_=ss[:, 0:1], func=AF.Ln, scale=1.0 / D, bias=c.epsc))(ss),
         reads=[ssB, c.Bconst], writes=[ssB])
    P.op("act", (lambda ss: lambda e: e.activation(out=ss[:, 2:3], in_=ss[:, 1:2], func=AF.Exp, scale=-0.5))(ss),
         reads=[ssB], writes=[ssB])
    hs = c.hbi % 2
    c.hbi += 1
    hb, hbB = c.hb[hs], c.hbB[hs]
    P.op("dve", (lambda xt, ss, hb: lambda e: e.tensor_scalar(out=hb, in0=xt, scalar1=ss[:, 2:3], scalar2=None,
                                                              op0=ALU.mult))(xt, ss, hb),
         reads=[xb, ssB], writes=[hbB])
    return hb, hbB


def transpose_tile(c, hbp, t, hT, hTbuf):
    P = c.P
    hb, hbB = hbp
    bk, bB = c.bank()
    bkb = bk.bitcast(BF16)

    def tr(e, hb=hb, bkb=bkb):
        r = None
        for k in range(8):
            r = e.transpose(bkb[:, k * 128:(k + 1) * 128], hb[:, k * 128:(k + 1) * 128], c.ident)
        return r
    P.op("pe", tr, reads=[hbB, c.Bconst], writes=[bB])
    P.op("act", (lambda bkb, t: lambda e: e.copy(out=hT[:, :, t * 128:(t + 1) * 128],
                                                 in_=bkb.rearrange("p (k n) -> p k n", k=8)))(bkb, t),
         reads=[bB], writes=[hTbuf])


def post_norm_residual(c, g, t, banks, bankBs, gpost, src_dram, dst_dram):
    P = c.P
    T = 4 * g + t
    s = c.oi % c.n_ot
    c.oi += 1
    ps, psB = c.pss[s], c.pssB[s]
    for hf in range(2):
        P.op("act", (lambda hf, ps: lambda e: e.activation(out=c.junk[:, 0:512], in_=banks[hf], func=AF.Square,
                                                           accum_out=ps[:, hf:hf + 1]))(hf, ps),
             reads=[bankBs[hf]], writes=[psB, c.Bjunk])
    P.op("dve", (lambda ps: lambda e: e.tensor_tensor(out=ps[:, 2:3], in0=ps[:, 0:1], in1=ps[:, 1:2], op=ALU.add))(ps),
         reads=[psB], writes=[psB])
    P.op("act", (lambda ps: lambda e: e.activation(out=ps[:, 3:4], in_=ps[:, 2:3], func=AF.Ln, scale=1.0 / D, bias=c.epsc))(ps),
         reads=[psB, c.Bconst], writes=[psB])
    P.op("act", (lambda ps: lambda e: e.activation(out=ps[:, 4:5], in_=ps[:, 3:4], func=AF.Exp, scale=-0.5))(ps),
         reads=[psB], writes=[psB])
    ot, otB = c.ot[s], c.otB[s]
    xr, xrB = c.xr[s], c.xrB[s]
    P.dma("sp", (lambda xr, T: lambda e: [e.dma_start(out=xr, in_=src_dram[T * 128:(T + 1) * 128, :])])(xr, T),
          1, reads=[c.Bstream[T]], writes=[xrB], pool="ldr", K=2)
    for hf in range(2):
        P.op("dve", (lambda hf, ps, ot: lambda e: e.scalar_tensor_tensor(
            out=ot[:, hf * 512:(hf + 1) * 512], in0=banks[hf], scalar=ps[:, 4:5], in1=gpost[:, hf * 512:(hf + 1) * 512],
            op0=ALU.mult, op1=ALU.mult))(hf, ps, ot), reads=[bankBs[hf], psB, c.Bconst], writes=[otB])
    P.op("pool", (lambda ot, xr: lambda e: e.tensor_tensor(out=ot, in0=ot, in1=xr, op=ALU.add))(ot, xr),
         reads=[otB, xrB], writes=[otB])
    return P.dma("pool", (lambda ot, T: lambda e: [e.dma_start(out=dst_dram[T * 128:(T + 1) * 128, :], in_=ot)])(ot, T),
                 1, reads=[otB], writes=[c.Bstream[T]], pool="st", K=2)


def stage_l0_mixer(c, src_dram, dst_dram, NG):
    P, A, nc = c.P, c.A, c.nc
    I = c.inp
    A.off = c.stage_base
    Win = A.alloc([8, 2560], BF16)
    Wout = A.alloc([8, 1024], BF16)
    hT = [A.alloc([8, 512], BF16) for _ in range(2)]
    hTB = [Buf("hT0"), Buf("hT1")]
    qTs2 = [A.alloc([4, 512], BF16) for _ in range(2)]
    qTBs2 = [Buf("qT0"), Buf("qT1")]
    kR = A.alloc([4, 1536], BF16)
    kRB = [Buf("kR0"), Buf("kR1"), Buf("kR2")]
    vR = A.alloc([12, 8, 65], BF16)
    vRB = [Buf("vR0"), Buf("vR1"), Buf("vR2")]
    Ep = A.alloc([8, 5, 128], BF16)
    uT = A.alloc([4, 512], BF16)
    uTB = Buf("uT")
    vln = A.alloc([4, 512], BF16)
    vlnB = [Buf("vln%d" % t) for t in range(4)]
    vg = [A.alloc([512], F32) for _ in range(2)]
    vgB = [Buf("vg0"), Buf("vg1")]
    vt = [A.alloc([512], F32) for _ in range(2)]
    vtB = [Buf("vt0"), Buf("vt1")]
    st = [A.alloc([16], F32) for _ in range(2)]
    stB = [Buf("st0"), Buf("st1")]
    wsT = A.alloc([4, 128], BF16)
    wsF = c.xr[0][:, 0:512].rearrange("p (a b) -> p a b", a=4)
    tri = A.alloc([128], F32)
    bsb = A.alloc([4, 128], F32)
    lng = A.alloc([512], F32)
    lnb = A.alloc([512], F32)
    gpost = A.alloc([1024], F32)
    gcol = A.alloc([8], F32)
    PP = [A.alloc([5, 4, 128], BF16) for _ in range(2)]
    PPB = [[Buf("PP%d_a" % i), Buf("PP%d_b" % i)] for i in range(2)]
    aout = A.alloc([4, 512], BF16)
    aoutB = Buf("aout")
    rden = [A.alloc([4], F32) for _ in range(2)]
    rdenB = [Buf("rden0"), Buf("rden1")]
    catT = A.alloc([8, 512], BF16)
    catTB = Buf("catTa")
    catTBb = Buf("catTb")
    gt, gtB = vt, vtB
    valid = A.alloc([c.NTILES], F32)

    def ld_consts(e):
        return [e.dma_start(out=wsF, in_=I["wsT"]), e.dma_start(out=tri, in_=I["tri"]),
                e.dma_start(out=bsb, in_=I["bsb"]), e.dma_start(out=lng, in_=I["lng"]),
                e.dma_start(out=lnb, in_=I["lnb"]), e.dma_start(out=gpost, in_=I["gpost0"]),
                e.dma_start(out=gcol, in_=I["gcol0"]),
                e.dma_start(out=valid, in_=I["valid"])]
    P.dma("sp", ld_consts, 8, writes=[c.Bconst, c.xrB[0]], pool="ldc", K=1)
    P.op("dve", lambda e: e.tensor_tensor(out=wsT, in0=wsF, in1=tri.unsqueeze(1).to_broadcast([128, 4, 128]), op=ALU.mult),
         reads=[c.Bconst, c.xrB[0]], writes=[c.Bconst])
    for h8 in range(8):
        stg = c.xt[h8 % 3][:, 0:640].rearrange("p (a b) -> p a b", a=5)
        P.dma("sp", (lambda h8, stg: lambda e: [e.dma_start(out=stg, in_=I["biasT"][:, h8, :, :])])(h8, stg), 1,
              writes=[c.xtB[h8 % 3]], pool="ldx", K=3)
        P.op("act", (lambda h8, stg: lambda e: e.activation(out=Ep[:, h8, :, :], in_=stg, func=AF.Exp))(h8, stg),
             reads=[c.xtB[h8 % 3], c.Bconst], writes=[c.Bconst])
    P.op("act", lambda e: e.activation(out=Ep[64:128, :, 0, 0:64], in_=Ep[64:128, :, 0, 0:64], func=AF.Copy, scale=0.0),
         reads=[c.Bconst], writes=[c.Bconst])
    P.op("act", lambda e: e.activation(out=Ep[0:64, :, 4, 64:128], in_=Ep[0:64, :, 4, 64:128], func=AF.Copy, scale=0.0),
         reads=[c.Bconst], writes=[c.Bconst])
    WinB = load_weight(c, Win, I["ab_w_in"], 8, 2560, "win", gcol=gcol)
    WoutB = load_weight(c, Wout, I["ab_w_out"], 8, 1024, "wout")

    def proj_fm(h, hB, col0, evac):
        bk, bB = c.bank()

        def mm(e):
            r = None
            for k in range(8):
                r = e.matmul(bk, lhsT=Win[:, k, col0:col0 + 128], rhs=h[:, k, :], start=(k == 0), stop=(k == 7))
            return r
        P.op("pe", mm, reads=[hB] + WinB, writes=[bB])
        evac(bk, bB)

    def proj_tm(h, hB, t, col0, evac):
        bk, bB = c.bank()

        def mm(e):
            r = None
            for k in range(8):
                r = e.matmul(bk, lhsT=h[:, k, t * 128:(t + 1) * 128], rhs=Win[:, k, col0:col0 + 512],
                             start=(k == 0), stop=(k == 7))
            return r
        P.op("pe", mm, reads=[hB] + WinB, writes=[bB])
        evac(bk, bB)

    fin = []
    for g in range(NG):
        h, hB = hT[g % 2], hTB[g % 2]
        norm_transpose_group(c, src_dram, g, h, hB)
        half = g % 3
        qT, qTB = qTs2[g % 2], qTBs2[g % 2]
        for ob in range(4):
            proj_fm(h, hB, ob * 128, lambda bk, bB, ob=ob, qT=qT, qTB=qTB: P.op(
                "act", lambda e: e.activation(out=qT[:, ob, :], in_=bk, func=AF.Copy, scale=0.125), reads=[bB], writes=[qTB]))
        for ob in range(4):
            proj_fm(h, hB, 512 + ob * 128, lambda bk, bB, ob=ob, half=half: P.op(
                "act", lambda e: e.activation(out=kR[:, ob, half * 512:(half + 1) * 512], in_=bk, func=AF.Copy), reads=[bB], writes=[kRB[half]]))
        for t in range(4):
            T = 4 * g + t
            sl = T % 12

            def ev(bk, bB, T=T, sl=sl, half=half):
                P.op("dve", lambda e: e.tensor_scalar(out=vR[:, sl, :, 0:64], in0=bk.rearrange("p (h d) -> p h d", h=8),
                                                      scalar1=valid[:, T:T + 1], scalar2=None, op0=ALU.mult),
                     reads=[bB, c.Bconst], writes=[vRB[half]])
                P.op("dve", lambda e: e.tensor_copy(out=vR[:, sl, :, 64:65],
                                                    in_=valid[:, T:T + 1].unsqueeze(1).to_broadcast([128, 8, 1])),
                     reads=[c.Bconst], writes=[vRB[half]])
            proj_tm(h, hB, t, 1024, ev)
        for ob in range(4):
            proj_fm(h, hB, 1536 + ob * 128, lambda bk, bB, ob=ob: P.op(
                "act", lambda e: e.activation(out=uT[:, ob, :], in_=bk, func=AF.Gelu), reads=[bB], writes=[uTB]))
        for t in range(4):
            s = t % 2

            def ev(bk, bB, t=t, s=s):
                P.op("act", lambda e: e.activation(out=vg[s], in_=bk, func=AF.Gelu), reads=[bB], writes=[vgB[s]])
                P.op("dve", lambda e: e.bn_stats(out=st[s][:, 0:6], in_=vg[s]), reads=[vgB[s]], writes=[stB[s]])
                P.op("dve", lambda e: e.bn_aggr(out=st[s][:, 6:8], in_=st[s][:, 0:6]), reads=[stB[s]], writes=[stB[s]])
                P.op("act", lambda e: e.activation(out=st[s][:, 9:10], in_=st[s][:, 7:8], func=AF.Ln, bias=c.epsc),
                     reads=[stB[s], c.Bconst], writes=[stB[s]])
                P.op("act", lambda e: e.activation(out=st[s][:, 8:9], in_=st[s][:, 9:10], func=AF.Exp, scale=-0.5),
                     reads=[stB[s]], writes=[stB[s]])
                P.op("dve", lambda e: e.tensor_scalar(out=vt[s], in0=vg[s], scalar1=st[s][:, 6:7], scalar2=st[s][:, 8:9],
                                                      op0=ALU.subtract, op1=ALU.mult), reads=[vgB[s], stB[s]], writes=[vtB[s]])
                P.op("pool", lambda e: e.tensor_tensor(out=vt[s], in0=vt[s], in1=lng, op=ALU.mult),
                     reads=[vtB[s], c.Bconst], writes=[vtB[s]])
                P.op("pool", lambda e: e.tensor_tensor(out=vln[:, t, :], in0=vt[s], in1=lnb, op=ALU.add),
                     reads=[vtB[s], c.Bconst], writes=[vlnB[t]])
            proj_tm(h, hB, t, 2048, ev)
        for t in range(4):
            bk, bB = c.bank()

            def mm(e, t=t, bk=bk):
                r = None
                for grp in range(4):
                    r = e.matmul(bk[:, grp * 128:(grp + 1) * 128], lhsT=vln[:, t, grp * 128:(grp + 1) * 128],
                                 rhs=wsT[:, grp, :], start=True, stop=True)
                return r
            P.op("pe", mm, reads=[vlnB[t], c.Bconst], writes=[bB])
            s = t % 2
            P.op("dve", (lambda bk, s: lambda e: e.tensor_tensor(out=gt[s], in0=bk, in1=bsb.rearrange("p a b -> p (a b)"),
                                                                 op=ALU.add))(bk, s),
                 reads=[bB, c.Bconst], writes=[gtB[s]])
            P.op("dve", (lambda s, t: lambda e: e.tensor_tensor(out=catT[:, 4:8, t * 128:(t + 1) * 128],
                                                                in0=gt[s].rearrange("p (a b) -> p a b", a=4),
                                                                in1=uT[:, :, t * 128:(t + 1) * 128], op=ALU.mult))(s, t),
                 reads=[gtB[s], uTB], writes=[catTBb])
        SPLIT = 3

        def emit_scores(h8):
            hp, hh = h8 // 2, h8 % 2
            pr = slice(hh * 64, hh * 64 + 64)
            for part, (ra, rb) in enumerate(((0, SPLIT), (SPLIT, 5))):
                def sm(e, ra=ra, rb=rb, pr=pr, hp=hp, g=g, qT=qT):
                    r = None
                    for rr in range(ra, rb):
                        for qb in range(4):
                            Tk = 4 * g + qb - rr
                            if Tk < 0:
                                continue
                            slk = Tk % 12
                            r = e.matmul(c.banks[rr][:, qb * 128:(qb + 1) * 128], lhsT=kR[pr, hp, slk * 128:(slk + 1) * 128],
                                         rhs=qT[pr, hp, qb * 128:(qb + 1) * 128], start=True, stop=True)
                    return r
                if g == 0 and ra > 3:
                    continue
                kth = sorted(set(((4 * g + qb - rr) // 4) % 3 for rr in range(ra, rb) for qb in range(4) if 4 * g + qb - rr >= 0))
                P.op("pe", sm, reads=[kRB[x] for x in kth] + [qTB], writes=[c.bankB[rr] for rr in range(ra, rb)])

        def emit_softmax(h8):
            pp = h8 % 2
            for part, (ra, rb) in enumerate(((0, SPLIT), (SPLIT, 5))):
                if g == 0 and ra > 3:
                    continue
                src = c.psall[:, ra * 512:rb * 512].rearrange("p (a b c) -> p a b c", a=rb - ra, b=4)
                P.op("act", (lambda src, pp, ra, rb: lambda e: e.activation(out=PP[pp][:, ra:rb, :, :], in_=src, func=AF.Exp))(src, pp, ra, rb),
                     reads=[c.bankB[rr] for rr in range(ra, rb)], writes=[PPB[pp][part]])
                P.op("dve", (lambda pp, ra, rb, h8: lambda e: e.tensor_tensor(
                    out=PP[pp][:, ra:rb, :, :], in0=PP[pp][:, ra:rb, :, :],
                    in1=Ep[:, h8, ra:rb, :].unsqueeze(2).to_broadcast([128, rb - ra, 4, 128]), op=ALU.mult))(pp, ra, rb, h8),
                    reads=[PPB[pp][part], c.Bconst], writes=[PPB[pp][part]])

        def emit_pv(h8):
            pp = h8 % 2
            ib = 5 + (h8 % 2)
            bk, bB = c.banks[ib], c.bankB[ib]

            def pv(e, bk=bk, h8=h8, pp=pp, g=g):
                r = None
                for qb in range(4):
                    rs_ = [rr for rr in range(5) if 4 * g + qb - rr >= 0]
                    for i, rr in enumerate(rs_):
                        Tk = 4 * g + qb - rr
                        r = e.matmul(bk[:, qb * 128:qb * 128 + 65], lhsT=PP[pp][:, rr, qb, :],
                                     rhs=vR[:, Tk % 12, h8, :], start=(i == 0), stop=(i == len(rs_) - 1))
                return r
            vth = [vRB[g % 3]] + ([vRB[(g - 1) % 3]] if g > 0 else [])
            P.op("pe", pv, reads=PPB[pp] + vth, writes=[bB])
            rs = h8 % 2
            bk3 = bk.rearrange("p (a b) -> p a b", a=4)
            P.op("dve", (lambda bk3, rs: lambda e: e.tensor_scalar(out=rden[rs], in0=bk3[:, :, 64], scalar1=1e-20, scalar2=None,
                                                                   op0=ALU.max))(bk3, rs), reads=[bB], writes=[rdenB[rs]])
            P.op("dve", (lambda rs: lambda e: e.reciprocal(out=rden[rs], in_=rden[rs]))(rs), reads=[rdenB[rs]], writes=[rdenB[rs]])
            P.op("dve", (lambda bk3, rs, h8: lambda e: e.tensor_tensor(
                out=aout[:, :, h8 * 64:(h8 + 1) * 64], in0=bk3[:, :, 0:64],
                in1=rden[rs].unsqueeze(2).to_broadcast([128, 4, 64]), op=ALU.mult))(bk3, rs, h8),
                reads=[bB, rdenB[rs]], writes=[aoutB])

        emit_scores(0)
        emit_softmax(0)
        for h8 in range(8):
            if h8 < 7:
                emit_scores(h8 + 1)
                emit_softmax(h8 + 1)
            emit_pv(h8)
        if g == 0:
            dbg(c, "ss", c.ss[0], [c.ssB[0]])
            dbg(c, "hT", h, [hB], BF16)
            dbg(c, "uT", uT, [uTB], BF16)
            dbg(c, "vln", vln, vlnB, BF16)
            dbg(c, "Ep", Ep, [c.Bconst], BF16)
            dbg(c, "aout", aout, [aoutB], BF16)
        for hf in range(2):
            bk, bB = c.bank()
            bkb = bk.bitcast(BF16)

            def tr(e, bkb=bkb, hf=hf):
                r = None
                for fb in range(2 * hf, 2 * hf + 2):
                    for t in range(4):
                        i = (fb - 2 * hf) * 4 + t
                        r = e.transpose(bkb[:, i * 128:(i + 1) * 128], aout[:, t, fb * 128:(fb + 1) * 128], c.ident)
                return r
            P.op("pe", tr, reads=[aoutB, c.Bconst], writes=[bB])
            P.op("act", (lambda bkb, hf: lambda e: e.copy(out=catT[:, 2 * hf:2 * hf + 2, :],
                                                          in_=bkb.rearrange("p (a b) -> p a b", a=2)))(bkb, hf),
                 reads=[bB], writes=[catTB])
        for t in range(4):
            bks = [c.bank(), c.bank()]
            for hf in range(2):
                bk = bks[hf][0]

                def mm(e, bk=bk, hf=hf, t=t):
                    r = None
                    for k in range(8):
                        r = e.matmul(bk, lhsT=catT[:, k, t * 128:(t + 1) * 128], rhs=Wout[:, k, hf * 512:(hf + 1) * 512],
                                     start=(k == 0), stop=(k == 7))
                    return r
                P.op("pe", mm, reads=[catTB, catTBb] + WoutB, writes=[bks[hf][1]])
            fin.append(post_norm_residual(c, g, t, [bks[0][0], bks[1][0]], [bks[0][1], bks[1][1]], gpost, src_dram, dst_dram))
        if g == 0:
            dbg(c, "catT", catT, [catTB, catTBb], BF16)
    return fin


def stage_ffn(c, L, src_dram, dst_dram, g0, NG):
    P, A = c.P, c.A
    I = c.inp
    A.off = c.stage_base_small
    c.n_ot = 1
    Wg = A.alloc([8, DFF], BF16)
    Wu = A.alloc([8, DFF], BF16)
    Wd = A.alloc([NFB, 1024], BF16)
    hTs = [A.alloc([8, 512], BF16) for _ in range(2)]
    hTBs = [Buf("f_hT0"), Buf("f_hT1")]
    actT = A.alloc([NFB, 512], BF16)
    actB = [Buf("f_act%d" % i) for i in range(NFB)]
    sg = [A.alloc([512], BF16) for _ in range(2)]
    sgB = [Buf("f_sg0"), Buf("f_sg1")]
    gpost = A.alloc([1024], F32)
    gcol = A.alloc([8], F32)
    P.dma("sp", lambda e: [e.dma_start(out=gpost, in_=I["gpostf%d" % L]), e.dma_start(out=gcol, in_=I["gcolf%d" % L])], 2,
          writes=[c.Bconst], pool="ldc", K=1)
    WgB = load_weight(c, Wg, I["ffn_w_gate%d" % L], 8, DFF, "wg", gcol=gcol)
    WuB = load_weight(c, Wu, I["ffn_w_up%d" % L], 8, DFF, "wu", gcol=gcol)
    WdB = load_weight(c, Wd, I["ffn_w_down%d" % L], NFB, 1024, "wd")
    fin = []
    norm_transpose_group(c, src_dram, g0, hTs[0], hTBs[0])
    for g in range(g0, g0 + NG):
        hT, hTB = hTs[(g - g0) % 2], hTBs[(g - g0) % 2]
        for fb in range(NFB):
            if g + 1 < g0 + NG and fb in (1, 6, 11, 16):
                pend_hb = norm_tile(c, src_dram, g + 1, (fb - 1) // 5)
            if g + 1 < g0 + NG and fb in (5, 10, 15, 20):
                transpose_tile(c, pend_hb, (fb - 5) // 5, hTs[(g + 1 - g0) % 2], hTBs[(g + 1 - g0) % 2])
            bg, bgB = c.bank()
            bu, buB = c.bank()

            def mm(e, fb=fb, bg=bg, bu=bu, hT=hT):
                r = None
                for k in range(8):
                    r = e.matmul(bg, lhsT=Wg[:, k, fb * 128:(fb + 1) * 128], rhs=hT[:, k, :], start=(k == 0), stop=(k == 7))
                for k in range(8):
                    r = e.matmul(bu, lhsT=Wu[:, k, fb * 128:(fb + 1) * 128], rhs=hT[:, k, :], start=(k == 0), stop=(k == 7))
                return r
            P.op("pe", mm, reads=[hTB] + WgB + WuB, writes=[bgB, buB])
            s = fb % 2
            P.op("act", (lambda bg, s: lambda e: e.activation(out=sg[s], in_=bg, func=AF.Silu))(bg, s), reads=[bgB], writes=[sgB[s]])
            P.op("dve", (lambda bu, s, fb: lambda e: e.tensor_tensor(out=actT[:, fb, :], in0=bu, in1=sg[s], op=ALU.mult))(bu, s, fb),
                 reads=[buB, sgB[s]], writes=[actB[fb]])
        for t in range(4):
            bks = [c.bank(), c.bank()]
            for hf in range(2):
                bk = bks[hf][0]

                def mm2(e, bk=bk, hf=hf, t=t):
                    r = None
                    for fb in range(NFB):
                        r = e.matmul(bk, lhsT=actT[:, fb, t * 128:(t + 1) * 128], rhs=Wd[:, fb, hf * 512:(hf + 1) * 512],
                                     start=(fb == 0), stop=(fb == NFB - 1))
                    return r
                P.op("pe", mm2, reads=actB + WdB, writes=[bks[hf][1]])
            fin.append(post_norm_residual(c, g, t, [bks[0][0], bks[1][0]], [bks[0][1], bks[1][1]], gpost, src_dram, dst_dram))
    c.n_ot = 2
    return fin


def stage_l1_mixer(c, src_dram, dst_dram, NG, g_own):
    P, A = c.P, c.A
    I = c.inp
    A.off = c.stage_base
    Win = A.alloc([8, 3088], BF16)
    Wout = A.alloc([8, 1024], BF16)
    hT = [A.alloc([8, 512], BF16) for _ in range(2)]
    hTB = [Buf("g_hT0"), Buf("g_hT1")]
    qTs = A.alloc([4, 512], F32)
    qTsB = Buf("g_qTs")
    kTs = A.alloc([4, 512], F32)
    kTsB = Buf("g_kTs")
    vsb = A.alloc([4, 1024], BF16)
    vsbB = [Buf("g_v%d" % t) for t in range(4)]
    sgb = A.alloc([4, 1024], BF16)
    sgbB = [Buf("g_sg%d" % t) for t in range(4)]
    sp = [A.alloc([512], F32) for _ in range(2)]
    spB = [Buf("g_sp0"), Buf("g_sp1")]
    e1, e1B = sp, spB
    eq = [A.alloc([4, 128], F32) for _ in range(2)]
    eqB = [Buf("g_eq0"), Buf("g_eq1")]
    ek = [A.alloc([4, 128], F32) for _ in range(2)]
    ekB = [Buf("g_ek0"), Buf("g_ek1")]
    qtT = [A.alloc([4, 128], BF16) for _ in range(2)]
    qtTB = [Buf("g_qt0"), Buf("g_qt1")]
    ktT = [A.alloc([4, 128], BF16) for _ in range(2)]
    ktTB = [Buf("g_kt0"), Buf("g_kt1")]
    ktm = [A.alloc([4, 128], BF16) for _ in range(2)]
    ktmB = [Buf("g_ktm0"), Buf("g_ktm1")]
    atT = [A.alloc([4, 128], BF16) for _ in range(2)]
    atTB = [Buf("g_at0"), Buf("g_at1")]
    S = A.alloc([4, 256], F32)
    SB = Buf("g_S")
    Sbf = A.alloc([4, 256], BF16)
    SbfB = Buf("g_Sbf")
    aT = A.alloc([512], F32)
    aTB = Buf("g_aT")
    w2 = A.alloc([512], F32)
    Utri = A.alloc([128], F32)
    tri = A.alloc([128], F32)
    normg = A.alloc([1024], F32)
    gpost = A.alloc([1024], F32)
    gcol = A.alloc([8], F32)
    tmp = A.alloc([1024], F32)
    tmpB = Buf("g_tmp")
    gated = [A.alloc([1024], BF16) for _ in range(2)]
    gatedB = [Buf("g_gated0"), Buf("g_gated1")]
    oT = A.alloc([8, 512], BF16)
    oTB = [Buf("g_oT%d" % t) for t in range(4)]
    r4 = [A.alloc([12], F32) for _ in range(2)]
    r4B = [Buf("g_r40"), Buf("g_r41")]

    def ld_consts(e):
        return [e.dma_start(out=w2[0:33, :], in_=I["w2aug"]), e.dma_start(out=tri, in_=I["tri"]),
                e.dma_start(out=normg, in_=I["normg"]), e.dma_start(out=gpost, in_=I["gpost1"]),
                e.dma_start(out=gcol, in_=I["gcol1"])]
    P.dma("sp", ld_consts, 5, writes=[c.Bconst], pool="ldc", K=1)
    P.op("dve", lambda e: e.tensor_scalar(out=Utri, in0=tri, scalar1=-1.0 / 16.0, scalar2=None, op0=ALU.mult),
         reads=[c.Bconst], writes=[c.Bconst])
    P.op("dve", lambda e: e.memset(aT[0:32, :], 0.0), writes=[aTB])
    P.op("dve", lambda e: e.memset(aT[32:33, :], 1.0), writes=[aTB])
    aT2 = tmp[:, 0:512]
    P.op("dve", lambda e: e.memset(aT2[0:32, :], 0.0), writes=[tmpB])
    P.op("dve", lambda e: e.memset(aT2[32:33, :], 1.0), writes=[tmpB])
    P.op("dve", lambda e: e.memset(S, 0.0), writes=[SB])
    P.op("dve", lambda e: e.memset(Sbf, 0.0), writes=[SbfB])
    WinB = load_weight(c, Win, I["c_w_in"], 8, 3088, "cwin", gcol=gcol)
    WoutB = load_weight(c, Wout, I["c_w_out"], 8, 1024, "cwout")
    QS = 128.0 ** -0.5

    def proj_fm(h, hB, col0, M, evac):
        bk, bB = c.bank()

        def mm(e):
            r = None
            for k in range(8):
                r = e.matmul(bk[0:M, :], lhsT=Win[:, k, col0:col0 + M], rhs=h[:, k, :], start=(k == 0), stop=(k == 7))
            return r
        P.op("pe", mm, reads=[hB] + WinB, writes=[bB])
        evac(bk, bB)

    def proj_tm(h, hB, t, col0, evac):
        bk, bB = c.bank()

        def mm(e):
            r = None
            for k in range(8):
                r = e.matmul(bk, lhsT=h[:, k, t * 128:(t + 1) * 128], rhs=Win[:, k, col0:col0 + 512],
                             start=(k == 0), stop=(k == 7))
            return r
        P.op("pe", mm, reads=[hB] + WinB, writes=[bB])
        evac(bk, bB)

    fin = []
    for g in range(NG):
        full = g >= g_own
        h, hB = hT[g % 2], hTB[g % 2]
        if (not full) and g % 2 == 1:
            kTs_g, kTsB_g, vsb_g, vsbB_g, aT_g, aTB_g = qTs, qTsB, sgb, sgbB, aT2, tmpB
        else:
            kTs_g, kTsB_g, vsb_g, vsbB_g, aT_g, aTB_g = kTs, kTsB, vsb, vsbB, aT, aTB
        norm_transpose_group(c, src_dram, g, h, hB)
        if full:
            for hd in range(4):
                proj_fm(h, hB, hd * 128, 128, lambda bk, bB, hd=hd: P.op(
                    "act", lambda e: e.activation(out=qTs[:, hd, :], in_=bk, func=AF.Copy, scale=QS), reads=[bB], writes=[qTsB]))
        for hd in range(4):
            proj_fm(h, hB, 512 + hd * 128, 128, lambda bk, bB, hd=hd, kTs_g=kTs_g, kTsB_g=kTsB_g: P.op(
                "act", lambda e: e.activation(out=kTs_g[:, hd, :], in_=bk, func=AF.Copy, scale=1.0), reads=[bB], writes=[kTsB_g]))
        proj_fm(h, hB, 3072, 16, lambda bk, bB, aT_g=aT_g, aTB_g=aTB_g: P.op(
            "act", lambda e: e.activation(out=aT_g[0:16, :], in_=bk[0:16, :], func=AF.Copy, scale=1.0), reads=[bB], writes=[aTB_g]))
        for t in range(4):
            for hf in range(2):
                proj_tm(h, hB, t, 1024 + hf * 512, lambda bk, bB, t=t, hf=hf, vsb_g=vsb_g, vsbB_g=vsbB_g: P.op(
                    "dve", lambda e: e.tensor_copy(out=vsb_g[:, t, hf * 512:(hf + 1) * 512], in_=bk), reads=[bB], writes=[vsbB_g[t]]))
            if full:
                for hf in range(2):
                    proj_tm(h, hB, t, 2048 + hf * 512, lambda bk, bB, t=t, hf=hf: P.op(
                        "act", lambda e: e.activation(out=sgb[:, t, hf * 512:(hf + 1) * 512], in_=bk, func=AF.Silu),
                        reads=[bB], writes=[sgbB[t]]))
        def make_steps(t, g=g, full=full, h=h, hB=hB, kTs=kTs_g, kTsB=kTsB_g, vsb=vsb_g, vsbB=vsbB_g, aT=aT_g, aTB=aTB_g):
            s = t % 2
            st8 = {}
            steps = []

            def s0():
                bl, blB = c.bank()
                st8["bl"] = (bl, blB)
                P.op("pe", (lambda bl, t: lambda e: e.matmul(bl, lhsT=aT[0:33, t * 128:(t + 1) * 128], rhs=w2[0:33, :],
                                                             start=True, stop=True))(bl, t),
                     reads=[aTB, c.Bconst], writes=[blB])
            steps.append(s0)

            def s1():
                bl, blB = st8["bl"]
                P.op("act", (lambda bl, s: lambda e: e.activation(out=e1[s], in_=bl, func=AF.Exp, scale=-1.0))(bl, s),
                     reads=[blB], writes=[e1B[s]])
                P.op("act", (lambda s: lambda e: e.activation(out=sp[s], in_=e1[s], func=AF.Ln, bias=1.0))(s),
                     reads=[e1B[s]], writes=[spB[s]])
            steps.append(s1)

            def s2():
                bc, bcB = c.bank()
                st8["bc"] = (bc, bcB)

                def cm(e, bc=bc, s=s):
                    r = None
                    for hd in range(4):
                        r = e.matmul(bc[:, hd * 128:(hd + 1) * 128], lhsT=sp[s][:, hd * 128:(hd + 1) * 128], rhs=Utri,
                                     start=True, stop=True)
                    return r
                P.op("pe", cm, reads=[spB[s], c.Bconst], writes=[bcB])
            steps.append(s2)

            def s3():
                bc, bcB = st8["bc"]
                bc3 = bc.rearrange("p (a b) -> p a b", a=4)
                P.op("act", (lambda bc3, s: lambda e: e.activation(out=ek[s], in_=bc3, func=AF.Exp, scale=-1.0))(bc3, s),
                     reads=[bcB], writes=[ekB[s]])
                if full:
                    P.op("act", (lambda bc3, s: lambda e: e.activation(out=eq[s], in_=bc3, func=AF.Exp))(bc3, s),
                         reads=[bcB], writes=[eqB[s]])
                else:
                    P.op("act", (lambda bc3, s: lambda e: e.activation(out=eq[s][:, :, 127:128], in_=bc3[:, :, 127:128], func=AF.Exp))(bc3, s),
                         reads=[bcB], writes=[eqB[s]])
            steps.append(s3)

            def s4():
                P.op("dve", (lambda s, t: lambda e: e.tensor_tensor(out=ktT[s], in0=kTs[:, :, t * 128:(t + 1) * 128], in1=ek[s],
                                                                    op=ALU.mult))(s, t),
                     reads=[kTsB, ekB[s]], writes=[ktTB[s]])
                if full:
                    P.op("dve", (lambda s, t: lambda e: e.tensor_tensor(out=qtT[s], in0=qTs[:, :, t * 128:(t + 1) * 128], in1=eq[s],
                                                                        op=ALU.mult))(s, t),
                         reads=[qTsB, eqB[s]], writes=[qtTB[s]])
            steps.append(s4)

            def s5():
                if full:
                    ba, baB = c.bank()
                    st8["ba"] = (ba, baB)

                    def am(e, ba=ba, s=s):
                        r = None
                        for hd in range(4):
                            r = e.matmul(ba[:, hd * 128:(hd + 1) * 128], lhsT=ktT[s][:, hd, :], rhs=qtT[s][:, hd, :],
                                         start=True, stop=True)
                        return r
                    P.op("pe", am, reads=[ktTB[s], qtTB[s]], writes=[baB])
                bt, btB = c.bank()
                st8["bt"] = (bt, btB)
                btb = bt.bitcast(BF16)

                def ktr(e, btb=btb, s=s):
                    r = None
                    for hd in range(4):
                        r = e.transpose(btb[:, hd * 128:(hd + 1) * 128], ktT[s][:, hd, :], c.ident)
                    return r
                P.op("pe", ktr, reads=[ktTB[s], c.Bconst], writes=[btB])
            steps.append(s5)

            def s6():
                if full:
                    ba, baB = st8["ba"]
                    P.op("dve", (lambda ba, s: lambda e: e.tensor_tensor(
                        out=atT[s], in0=ba.rearrange("p (a b) -> p a b", a=4),
                        in1=tri.unsqueeze(1).to_broadcast([128, 4, 128]), op=ALU.mult))(ba, s),
                        reads=[baB, c.Bconst], writes=[atTB[s]])
                bt, btB = st8["bt"]
                btb = bt.bitcast(BF16)
                P.op("act", (lambda btb, s: lambda e: e.copy(out=ktm[s], in_=btb[:, 0:512].rearrange("p (a b) -> p a b", a=4)))(btb, s),
                     reads=[btB], writes=[ktmB[s]])
            steps.append(s6)

            def s7():
                kv = [c.bank(), c.bank()]
                st8["kv"] = kv

                def kvm(e, kv=kv, s=s, t=t):
                    r = None
                    for hd in range(4):
                        r = e.matmul(kv[hd // 2][0][:, (hd % 2) * 256:(hd % 2 + 1) * 256], lhsT=ktm[s][:, hd, :],
                                     rhs=vsb[:, t, hd * 256:(hd + 1) * 256], start=True, stop=True)
                    return r
                P.op("pe", kvm, reads=[ktmB[s], vsbB[t]], writes=[kv[0][1], kv[1][1]])
                if full:
                    ob = [c.bank(), c.bank()]
                    st8["ob"] = ob

                    def om(e, ob=ob, s=s, t=t):
                        r = None
                        for hd in range(4):
                            o_ap = ob[hd // 2][0][:, (hd % 2) * 256:(hd % 2 + 1) * 256]
                            e.matmul(o_ap, lhsT=atT[s][:, hd, :], rhs=vsb[:, t, hd * 256:(hd + 1) * 256], start=True, stop=False)
                            r = e.matmul(o_ap, lhsT=qtT[s][:, hd, :], rhs=Sbf[:, hd, :], start=False, stop=True)
                        return r
                    P.op("pe", om, reads=[atTB[s], qtTB[s], vsbB[t], SbfB], writes=[ob[0][1], ob[1][1]])
            steps.append(s7)

            def s8():
                kv = st8["kv"]
                for hf in range(2):
                    P.op("dve", (lambda hf, kv: lambda e: e.tensor_tensor(
                        out=S[:, 2 * hf:2 * hf + 2, :], in0=kv[hf][0].rearrange("p (a b) -> p a b", a=2), in1=S[:, 2 * hf:2 * hf + 2, :],
                        op=ALU.add))(hf, kv), reads=[kv[hf][1], SB], writes=[SB])
                P.op("dve", (lambda s: lambda e: e.tensor_tensor(out=Sbf, in0=S, in1=eq[s][:, :, 127:128].to_broadcast([128, 4, 256]),
                                                                 op=ALU.mult))(s), reads=[SB, eqB[s]], writes=[SbfB])
                P.op("dve", (lambda s: lambda e: e.tensor_tensor(out=S, in0=S, in1=eq[s][:, :, 127:128].to_broadcast([128, 4, 256]),
                                                                 op=ALU.mult))(s), reads=[SB, eqB[s]], writes=[SB])
            steps.append(s8)
            if not full:
                return steps

            def s9():
                ob = st8["ob"]
                rs = c.r4i % 2
                c.r4i += 1
                r4_, r4B_ = r4[rs], r4B[rs]
                st8["r4"] = (r4_, r4B_)
                for hd in range(4):
                    P.op("act", (lambda hd, ob, r4_: lambda e: e.activation(
                        out=c.junk[:, 0:256], in_=ob[hd // 2][0][:, (hd % 2) * 256:(hd % 2 + 1) * 256], func=AF.Square,
                        accum_out=r4_[:, hd:hd + 1]))(hd, ob, r4_), reads=[ob[hd // 2][1]], writes=[r4B_, c.Bjunk])
                P.op("act", (lambda r4_: lambda e: e.activation(out=r4_[:, 4:8], in_=r4_[:, 0:4], func=AF.Ln, scale=1.0 / 256.0,
                                                                bias=c.epsc))(r4_), reads=[r4B_, c.Bconst], writes=[r4B_])
                P.op("act", (lambda r4_: lambda e: e.activation(out=r4_[:, 8:12], in_=r4_[:, 4:8], func=AF.Exp, scale=-0.5))(r4_),
                     reads=[r4B_], writes=[r4B_])
            steps.append(s9)

            def s10():
                ob = st8["ob"]
                r4_, r4B_ = st8["r4"]
                for hd in range(4):
                    P.op("dve", (lambda hd, ob, r4_: lambda e: e.scalar_tensor_tensor(
                        out=tmp[:, hd * 256:(hd + 1) * 256], in0=ob[hd // 2][0][:, (hd % 2) * 256:(hd % 2 + 1) * 256],
                        scalar=r4_[:, 8 + hd:9 + hd], in1=normg[:, hd * 256:(hd + 1) * 256], op0=ALU.mult, op1=ALU.mult))(hd, ob, r4_),
                        reads=[ob[hd // 2][1], r4B_, c.Bconst], writes=[tmpB])
                gd, gdB = gated[t % 2], gatedB[t % 2]
                P.op("pool", (lambda gd, t: lambda e: e.tensor_tensor(out=gd, in0=tmp, in1=sgb[:, t, :], op=ALU.mult))(gd, t),
                     reads=[tmpB, sgbB[t]], writes=[gdB])
            steps.append(s10)

            def s11():
                gd, gdB = gated[t % 2], gatedB[t % 2]
                bo, boB = c.bank()
                bob = bo.bitcast(BF16)

                def otr(e, bob=bob, gd=gd):
                    r = None
                    for k in range(8):
                        r = e.transpose(bob[:, k * 128:(k + 1) * 128], gd[:, k * 128:(k + 1) * 128], c.ident)
                    return r
                P.op("pe", otr, reads=[gdB, c.Bconst], writes=[boB])
                P.op("act", (lambda bob, t: lambda e: e.copy(out=oT[:, :, t * 128:(t + 1) * 128],
                                                             in_=bob.rearrange("p (k n) -> p k n", k=8)))(bob, t),
                     reads=[boB], writes=[oTB[t]])
            steps.append(s11)

            def s12():
                bks = [c.bank(), c.bank()]
                st8["bks"] = bks
                for hf in range(2):
                    bk = bks[hf][0]

                    def mm3(e, bk=bk, hf=hf, t=t):
                        r = None
                        for k in range(8):
                            r = e.matmul(bk, lhsT=oT[:, k, t * 128:(t + 1) * 128], rhs=Wout[:, k, hf * 512:(hf + 1) * 512],
                                         start=(k == 0), stop=(k == 7))
                        return r
                    P.op("pe", mm3, reads=[oTB[t]] + WoutB, writes=[bks[hf][1]])
            steps.append(s12)

            def s13():
                bks = st8["bks"]
                fin.append(post_norm_residual(c, g, t, [bks[0][0], bks[1][0]], [bks[0][1], bks[1][1]], gpost, src_dram, dst_dram))
            steps.append(s13)
            return steps

        LAG = 3
        for ta, tb in ((0, 1), (2, 3)):
            sa, sb_ = make_steps(ta), make_steps(tb)
            n = len(sa)
            for k in range(n + LAG):
                if k < n:
                    sa[k]()
                if 0 <= k - LAG < n:
                    sb_[k - LAG]()
    return fin

INPUT_SHAPES = {
    "ident": ([128, 128], F32),
}


def build(NT, stages, inputs_np, debug=False, g_own=0):
    nc = bass.Bass("TRN2", target_bir_lowering=False)
    c = Ctx()
    c.nc = nc
    c.debug = debug
    c.g_own = g_own
    c.dbg_ops = []
    c.P = Prog(nc)
    c.NT = NT
    c.NTILES = NT // 128
    c.inp = {}
    for name, arr in inputs_np.items():
        c.inp[name] = nc.dram_tensor(name, list(arr.shape), F32, kind="ExternalInput").ap()
    y = nc.dram_tensor("y", [NT, D], F32, kind="ExternalOutput").ap()
    c.Bstream = [Buf("ys%d" % i) for i in range(c.NTILES)]
    A = Arena(nc, 204 * 1024)
    c.A = A
    c.identf = A.alloc([128], F32)
    c.ident = A.alloc([128], BF16)
    c.junk = A.alloc([1024], BF16)
    c.epsc = A.alloc([1], F32)
    c.Bjunk = Buf("junk")
    c.Bconst = Buf("const")
    c.xt = [A.alloc([1024], F32) for _ in range(3)]
    c.xtB = [Buf("xt%d" % i) for i in range(3)]
    c.ss = [A.alloc([4], F32) for _ in range(3)]
    c.ssB = [Buf("ss%d" % i) for i in range(3)]
    c.hb = [A.alloc([1024], BF16) for _ in range(2)]
    c.hbB = [Buf("hb0"), Buf("hb1")]
    c.pss = [A.alloc([8], F32) for _ in range(2)]
    c.pssB = [Buf("pss0"), Buf("pss1")]
    ot0 = A.alloc([1024], F32)
    xr0 = A.alloc([1024], F32)
    c.stage_base_small = A.off
    c.ot = [ot0, A.alloc([1024], F32)]
    c.otB = [Buf("ot0"), Buf("ot1")]
    c.xr = [xr0, A.alloc([1024], F32)]
    c.xrB = [Buf("xr0"), Buf("xr1")]
    c.n_ot = 2
    c.xi = 0
    c.oi = 0
    c.pei = 0
    c.hbi = 0
    c.r4i = 0
    c.stage_base = A.off
    psall = nc.alloc_psum_tensor("psall", [128, 4096], F32)
    c.psall = psall[:]
    banks = [psall[:, i * 512:(i + 1) * 512] for i in range(8)]
    bankB = [Buf("bank%d" % i, excl=True) for i in range(8)]
    c.banks = banks
    c.bankB = bankB
    c.bi = 0

    def bank():
        i = c.bi % 8
        c.bi += 1
        return banks[i], bankB[i]
    c.bank = bank
    P = c.P
    P.dma("sp", lambda e: [e.dma_start(out=c.identf, in_=c.inp["ident"])], 1, writes=[c.Bconst], pool="ldc", K=1)
    P.op("dve", lambda e: e.tensor_copy(out=c.ident, in_=c.identf), reads=[c.Bconst], writes=[c.Bconst])
    P.op("dve", lambda e: e.memset(c.epsc, EPS), reads=[], writes=[c.Bconst])
    fin = []
    src = c.inp["x"]
    for stg in stages:
        if stg == "l0mix":
            fin = stage_l0_mixer(c, src, y, NT // 512)
        elif stg == "l1mix":
            fin = stage_l1_mixer(c, src, y, NT // 512, c.g_own)
        elif stg == "ffn0":
            fin = stage_ffn(c, 0, src, y, 0, NT // 512)
        elif stg == "ffn1":
            fin = stage_ffn(c, 1, src, y, c.g_own, NT // 512 - c.g_own)
        else:
            raise ValueError(stg)
        src = y
        P.fence()
    P.emit(final_wait_ops=list(fin) + c.dbg_ops)
    return nc


def rep128(v):
    return np.ascontiguousarray(np.broadcast_to(np.asarray(v, np.float32).reshape(1, -1), (128, v.size)))


def col128(v):
    v = np.asarray(v, np.float32)
    return np.ascontiguousarray(v.reshape(-1, 128).T)


def prep_common(inp):
    d = {}
    d["ident"] = np.eye(128, dtype=np.float32)
    rb = np.asarray(inp["a_rel_bias"][0], np.float32)
    kj = np.arange(128)[:, None, None]
    r = np.arange(5)[None, :, None]
    qi = np.arange(128)[None, None, :]
    idx = np.clip(r * 128 + qi - kj, -256, 256) + 256
    d["biasT"] = np.ascontiguousarray(rb[:, idx].transpose(1, 0, 2, 3))
    d["wsT"] = np.ascontiguousarray(np.asarray(inp["b_w_s"][0], np.float32).transpose(2, 0, 1))
    d["tri"] = (np.arange(128)[:, None] <= np.arange(128)[None, :]).astype(np.float32)
    d["bsb"] = np.ascontiguousarray(np.broadcast_to(np.asarray(inp["b_b_s"][0], np.float32)[None], (128, 4, 128)))
    d["lng"] = rep128(inp["b_ln_g"][0])
    d["lnb"] = rep128(inp["b_ln_b"][0])
    d["gpost0"] = rep128(inp["post_mix_g"][0])
    d["gcol0"] = col128(inp["pre_mix_g"][0])
    d["ab_w_in"] = np.ascontiguousarray(inp["ab_w_in"][0], dtype=np.float32)
    d["ab_w_out"] = np.ascontiguousarray(inp["ab_w_out"][0], dtype=np.float32)
    d["gcol1"] = col128(inp["pre_mix_g"][1])
    d["gpost1"] = rep128(inp["post_mix_g"][1])
    d["normg"] = rep128(inp["c_norm_g"][0])
    w2 = np.zeros((33, 512), np.float32)
    w2[0:16] = np.asarray(inp["c_w_a2"][0], np.float32)
    w2[32] = np.asarray(inp["c_b_a"][0], np.float32)
    d["w2aug"] = w2
    d["c_w_in"] = np.ascontiguousarray(inp["c_w_in"][0], dtype=np.float32)
    d["c_w_out"] = np.ascontiguousarray(inp["c_w_out"][0], dtype=np.float32)
    for L in range(2):
        d["gpostf%d" % L] = rep128(inp["post_ffn_g"][L])
        d["gcolf%d" % L] = col128(inp["pre_ffn_g"][L])
        d["ffn_w_gate%d" % L] = np.ascontiguousarray(inp["ffn_w_gate"][L], dtype=np.float32)
        d["ffn_w_up%d" % L] = np.ascontiguousarray(inp["ffn_w_up"][L], dtype=np.float32)
        d["ffn_w_down%d" % L] = np.ascontiguousarray(inp["ffn_w_down"][L], dtype=np.float32)
    return d


STAGES = ["l0mix", "ffn0", "l1mix", "ffn1"]
SEQ_HALF = 4096
_CACHE = {}


def kernel(**inputs):
    inp = {k: np.asarray(v) for k, v in inputs.items()}
    x = np.asarray(inp["x"], np.float32)
    B = x.shape[0]
    common = prep_common(inp)
    NT = 2 * SEQ_HALF
    in_maps = []
    for core in range(8):
        b, half = core // 2, core % 2
        d = dict(common)
        own = x[b, half * SEQ_HALF:(half + 1) * SEQ_HALF]
        if half == 1:
            prev = x[b, 0:SEQ_HALF]
            vprev = np.ones(SEQ_HALF, np.float32)
        else:
            prev = np.zeros_like(own)
            vprev = np.zeros(SEQ_HALF, np.float32)
        d["x"] = np.ascontiguousarray(np.concatenate([prev, own], axis=0))
        d["valid"] = col128(np.concatenate([vprev, np.ones(SEQ_HALF, np.float32)]))
        in_maps.append(d)
    if "nc" not in _CACHE:
        _CACHE["nc"] = build(NT, STAGES, in_maps[0], g_own=SEQ_HALF // 512)
    nc = _CACHE["nc"]
    res = run_bass_kernel_spmd(nc, in_maps, core_ids=list(range(8)))
    out = np.empty((B, 2 * SEQ_HALF, D), np.float32)
    for core in range(8):
        b, half = core // 2, core % 2
        out[b, half * SEQ_HALF:(half + 1) * SEQ_HALF] = np.asarray(res.results[core]["y"])[SEQ_HALF:]
    return out
```

```python
import contextlib
import numpy as np
import concourse.bass as bass
import concourse.mybir as mybir
from concourse.bass_utils import run_bass_kernel_spmd

F32 = mybir.dt.float32
BF16 = mybir.dt.bfloat16
ALU = mybir.AluOpType
AF = mybir.ActivationFunctionType

D = 1024
DFF = 2816
NFB = DFF // 128
EPS = 1e-6
ENGS = ("pe", "act", "dve", "pool", "sp")


class Buf:
    __slots__ = ("name", "excl", "last_w", "readers")

    def __init__(self, name, excl=False):
        self.name = name
        self.excl = excl
        self.last_w = None
        self.readers = []


class Op:
    __slots__ = ("eng", "fn", "deps", "dma", "signal", "ticket", "ndma", "semkey", "odeps", "cost", "table",
                 "seg", "fdep", "idx", "nun", "ready", "fin", "users")

    def __init__(self, eng, fn, dma):
        self.eng = eng
        self.fn = fn
        self.deps = []
        self.dma = dma
        self.signal = False
        self.ticket = None
        self.ndma = 0
        self.semkey = None
        self.odeps = []
        self.cost = 0.5
        self.table = None
        self.seg = 0
        self.fdep = None
        self.idx = 0


class _Dummy:
    def then_inc(self, *a, **k):
        return self


class CostEngine:
    def __init__(self, eng):
        self.eng = eng
        self.cost = 0.0
        self.table = None
        self.bytes = 0

    def _free(self, ap):
        n = 1
        for d in ap.shape[1:]:
            n *= d
        return n

    def __getattr__(self, name):
        def f(*a, **k):
            out = k.get("out", a[0] if a else None)
            cols = self._free(out) if out is not None and hasattr(out, "shape") else 1
            if name in ("matmul", "transpose"):
                lhs = k.get("lhsT", a[1] if len(a) > 1 else None)
                mult = 4.0 if (lhs is not None and lhs.dtype == F32) else 1.0
                self.cost += mult * max(cols, 64) / 2400.0 + 0.035
            elif name == "dma_start":
                src = k.get("in_", a[1] if len(a) > 1 else None)
                nb = out.shape[0] * cols * (4 if out.dtype == F32 else 2)
                self.bytes += nb
                self.cost += 0.05
            elif self.eng == "act":
                self.cost += 0.25 + cols / 1200.0
                fn = k.get("func", None)
                if fn in (AF.Exp, AF.Ln):
                    self.table = "A"
                elif fn == AF.Gelu:
                    self.table = "B"
                elif fn == AF.Silu:
                    self.table = "C"
            elif self.eng == "dve":
                self.cost += 0.13 + cols / 960.0
            else:
                self.cost += 0.2 + cols / 480.0
            return _Dummy()
        return f


class Prog:
    def __init__(self, nc):
        self.nc = nc
        self.ops = []
        self.pools = {}
        self.last = {e: None for e in ENGS}
        self.seg = 0
        self.pending_fence = {e: None for e in ENGS}
        self.reorder = True
        self.pe_mix = 0

    def _add(self, eng, fn, reads, writes, dma):
        op = Op(eng, fn, dma)
        deps = []

        def need(o, kind):
            if o is None or o is op:
                return
            if o.eng == eng and not dma and not o.dma:
                if eng == "pe":
                    op.odeps.append(o)
                    return
            deps.append(o)

        for b in reads:
            need(b.last_w, "raw")
            if b.excl:
                for r in b.readers:
                    need(r, "raw")
        for b in writes:
            need(b.last_w, "waw")
            for r in b.readers:
                need(r, "war")
        for b in reads:
            if b.excl:
                b.last_w = op
                b.readers = []
            else:
                b.readers.append(op)
        for b in writes:
            b.last_w = op
            b.readers = []
        if self.pending_fence[eng] is not None:
            op.fdep = self.pending_fence[eng]
            self.pending_fence[eng] = None
        seen = set()
        for d in deps:
            if id(d) not in seen:
                seen.add(id(d))
                op.deps.append(d)
                d.signal = True
        op.seg = self.seg
        op.idx = len(self.ops)
        if dma:
            lastd = self.last.get(("dma", eng, dma))
            if lastd is not None:
                op.odeps.append(lastd)
            self.last[("dma", eng, dma)] = op
        ce = CostEngine(eng)
        fn(ce)
        op.cost = ce.cost if not dma else (2.0 + ce.bytes / 120000.0)
        op.table = ce.table
        self.ops.append(op)
        return op

    def op(self, eng, fn, reads=(), writes=()):
        return self._add(eng, fn, list(reads), list(writes), False)

    def dma(self, eng, fn, n, reads=(), writes=(), pool="ld", K=6):
        op = self._add(eng, fn, list(reads), list(writes), pool)
        op.ndma = n
        pl = self.pools.setdefault(pool, [K, 0, {}])
        i = pl[1]
        pl[1] += 1
        slot = i % pl[0]
        prev = pl[2].get(slot)
        if prev is not None and prev not in op.deps:
            op.deps.append(prev)
        pl[2][slot] = op
        op.semkey = (pool, slot)
        return op

    def fence(self):
        for e in ENGS:
            self.pending_fence[e] = self.seg
        self.seg += 1

    def schedule(self):
        order = {e: [] for e in ENGS}
        self.seg_tail = {}
        nseg = self.seg + 1
        by_seg = [[] for _ in range(nseg)]
        for op in self.ops:
            by_seg[op.seg].append(op)
        for k, ops in enumerate(by_seg):
            for e in ENGS:
                ops_e = [o for o in ops if o.eng == e]
                if k > 0 and ops_e:
                    assert ops_e[0].fdep == k - 1, (e, k, ops_e[0].fdep)
                    for o in ops_e[1:]:
                        o.odeps.append(ops_e[0])
            if not self.reorder:
                for op in ops:
                    order[op.eng].append(op)
            else:
                self._sched_segment(ops, order)
            tail = []
            for e in ENGS:
                comp = [o for o in order[e] if o.seg == k and not o.dma]
                if comp:
                    tail.append(comp[-1])
            tail += [o for o in ops if o.dma]
            self.seg_tail[k] = tail
        return order

    def _sched_segment(self, ops, order):
        import heapq
        inseg = set(id(o) for o in ops)
        for o in ops:
            o.users = []
            o.nun = 0
            o.ready = 0.0
            o.fin = None
        for o in ops:
            for d in o.deps + o.odeps:
                if id(d) in inseg:
                    d.users.append(o)
                    o.nun += 1
        main = {e: [] for e in ENGS}
        avail = {e: [] for e in ENGS}
        free = {e: 0.0 for e in ENGS}
        cur_table = None
        small_run = [0]
        for o in ops:
            if o.nun == 0:
                heapq.heappush(main[o.eng], (o.ready, o.idx, o))
        nleft = len(ops)
        while nleft:
            best = None
            for e in ENGS:
                m, a = main[e], avail[e]
                while m and m[0][0] <= free[e] + 1e-9:
                    r, ix, o = heapq.heappop(m)
                    heapq.heappush(a, (ix, o))
                if a:
                    cand = (free[e], a[0][0], e, True)
                elif m:
                    cand = (m[0][0], m[0][1], e, False)
                else:
                    continue
                if best is None or cand[:2] < best[:2]:
                    best = cand
            st, _, e, from_avail = best
            if from_avail:
                a = avail[e]
                if e == "act" and len(a) > 1:
                    pick = None
                    small = heapq.nsmallest(6, a)
                    for ix, o in small:
                        if o.table is None or o.table == cur_table:
                            pick = (ix, o)
                            break
                    if pick is None or pick is small[0]:
                        ix, o = heapq.heappop(a)
                    else:
                        a.remove(pick)
                        heapq.heapify(a)
                        ix, o = pick
                elif e == "pe" and len(a) > 1 and self.pe_mix and small_run[0] >= self.pe_mix:
                    pick = None
                    for ix, o in a:
                        if o.cost >= 1.0 and (pick is None or ix < pick[0]):
                            pick = (ix, o)
                    if pick is None:
                        ix, o = heapq.heappop(a)
                    else:
                        a.remove(pick)
                        heapq.heapify(a)
                        ix, o = pick
                else:
                    ix, o = heapq.heappop(a)
            else:
                r, ix, o = heapq.heappop(main[e])
            if e == "pe":
                small_run[0] = small_run[0] + 1 if o.cost < 1.0 else 0
            if e == "act" and o.table is not None:
                if o.table != cur_table:
                    st += 1.3
                cur_table = o.table
            o.fin = st + o.cost
            free[e] = st + (0.06 if o.dma else o.cost)
            order[e].append(o)
            nleft -= 1
            for u in o.users:
                u.nun -= 1
                if u.eng == e and not o.dma and (o not in u.deps):
                    r = st
                else:
                    r = o.fin + (0.3 if (u.eng == e and not o.dma) else 1.2)
                if r > u.ready:
                    u.ready = r
                if u.nun == 0:
                    heapq.heappush(main[u.eng], (u.ready, u.idx, u))

    def emit(self, final_wait_ops=()):
        nc = self.nc
        with contextlib.ExitStack() as st:
            esem = {e: st.enter_context(nc.semaphore("s_" + e)) for e in ENGS}
            dsem = {}
            dcnt = {}
            for op in self.ops:
                if op.dma and op.semkey not in dsem:
                    dsem[op.semkey] = st.enter_context(nc.semaphore("d_%s%d" % op.semkey))
                    dcnt[op.semkey] = 0
            block = st.enter_context(nc.Block())
            order = self.schedule()
            for op in self.ops:
                if op.fdep is not None:
                    for o in self.seg_tail[op.fdep]:
                        if o is not op and not (o.eng == op.eng and not o.dma and not op.dma):
                            op.deps.append(o)
                            o.signal = True
            ecnt = {e: 0 for e in ENGS}
            for e in ENGS:
                for op in order[e]:
                    if op.dma:
                        dcnt[op.semkey] += 16 * op.ndma
                        op.ticket = (dsem[op.semkey], dcnt[op.semkey])
                    elif op.signal:
                        ecnt[op.eng] += 1
                        op.ticket = (esem[op.eng], ecnt[op.eng])
            final = list(final_wait_ops)

            def run(engname):
                def body(eng):
                    waited = {}
                    for op in order[engname]:
                        for d in op.deps:
                            sem, val = d.ticket
                            k = id(sem)
                            if waited.get(k, 0) < val:
                                eng.wait_ge(sem, val)
                                waited[k] = val
                        res = op.fn(eng)
                        if op.dma:
                            assert len(res) == op.ndma, (len(res), op.ndma)
                            for ins in res:
                                ins.then_inc(op.ticket[0], 16)
                        elif op.signal:
                            res.then_inc(op.ticket[0], 1)
                    if engname == "sp":
                        for d in final:
                            sem, val = d.ticket
                            eng.wait_ge(sem, val)
                return body

            block.tensor(run("pe"))
            block.scalar(run("act"))
            block.vector(run("dve"))
            block.gpsimd(run("pool"))
            block.sync(run("sp"))


class Arena:
    def __init__(self, nc, nbytes):
        self.t = nc.alloc_sbuf_tensor("arena", [128, nbytes // 2], BF16)
        self.nbytes = nbytes
        self.off = 0

    def alloc(self, shape, dtype):
        n = int(np.prod(shape))
        sz = n * (4 if dtype == F32 else 2)
        self.off = (self.off + 31) // 32 * 32
        assert self.off + sz <= self.nbytes, ("arena overflow", self.off, sz, self.nbytes)
        o2 = self.off // 2
        ap = self.t[:, o2:o2 + sz // 2]
        if dtype == F32:
            ap = ap.bitcast(F32)
        self.off += sz
        if len(shape) == 2:
            ap = ap.rearrange("p (a b) -> p a b", a=shape[0])
        elif len(shape) == 3:
            ap = ap.rearrange("p (a b c) -> p a b c", a=shape[0], b=shape[1])
        return ap


class Ctx:
    pass


def dbg(c, name, ap, bufs, dtype=F32):
    if not getattr(c, "debug", False):
        return
    shp = list(ap.shape)
    o = c.nc.dram_tensor("dbg_" + name, shp, dtype, kind="ExternalOutput").ap()
    c.dbg_ops.append(c.P.dma("sp", lambda e: [e.dma_start(out=o, in_=ap)], 1, reads=list(bufs), pool="dbg", K=1))


def load_weight(c, dst, src, nk, cols, name, gcol=None, chunk=None):
    P = c.P
    bufs = []
    for k in range(nk):
        b = Buf("%s_k%d" % (name, k))
        bufs.append(b)
        P.dma("pool", (lambda k: lambda e: [e.dma_start(out=dst[:, k, :], in_=src[k * 128:(k + 1) * 128, :])])(k),
              1, writes=[b], pool="w", K=6)
        if gcol is not None:
            P.op("dve", (lambda k: lambda e: e.tensor_scalar(out=dst[:, k, :], in0=dst[:, k, :], scalar1=gcol[:, k:k + 1],
                                                             scalar2=None, op0=ALU.mult))(k),
                 reads=[b, c.Bconst], writes=[b])
    return bufs


def load_weight_cols(c, dst, src, nk, chunks, name, gcol=None):
    P = c.P
    bufs = []
    for ci, (c0, w) in enumerate(chunks):
        b = Buf("%s_c%d" % (name, ci))
        bufs.append(b)

        def ld(e, c0=c0, w=w):
            return [e.dma_start(out=dst[:, k, c0:c0 + w], in_=src[k * 128:(k + 1) * 128, c0:c0 + w]) for k in range(nk)]
        P.dma("pool", ld, nk, writes=[b], pool="w", K=6)
        if gcol is not None:
            for k in range(nk):
                P.op("dve", (lambda k, c0, w: lambda e: e.tensor_scalar(out=dst[:, k, c0:c0 + w], in0=dst[:, k, c0:c0 + w],
                                                                        scalar1=gcol[:, k:k + 1], scalar2=None, op0=ALU.mult))(k, c0, w),
                     reads=[b, c.Bconst], writes=[b])
    return bufs


def norm_transpose_group(c, src_dram, g, hT, hTbuf):
    for t in range(4):
        norm_transpose_tile(c, src_dram, g, t, hT, hTbuf)


def norm_transpose_tile(c, src_dram, g, t, hT, hTbuf):
    hb = norm_tile(c, src_dram, g, t)
    transpose_tile(c, hb, t, hT, hTbuf)


def norm_tile(c, src_dram, g, t):
    P = c.P
    T = 4 * g + t
    s = c.xi % 3
    c.xi += 1
    xt, xb = c.xt[s], c.xtB[s]
    P.dma("sp", (lambda xt, T: lambda e: [e.dma_start(out=xt, in_=src_dram[T * 128:(T + 1) * 128, :])])(xt, T),
          1, reads=[c.Bstream[T]], writes=[xb], pool="ldx", K=3)
    ss, ssB = c.ss[s], c.ssB[s]
    P.op("act", (lambda xt, ss: lambda e: e.activation(out=c.junk, in_=xt, func=AF.Square, accum_out=ss[:, 0:1]))(xt, ss),
         reads=[xb], writes=[ssB, c.Bjunk])
    P.op("act", (lambda ss: lambda e: e.activation(out=ss[:, 1:2], in_=ss[:, 0:1], func=AF.Ln, scale=1.0 / D, bias=c.epsc))(ss),
         reads=[ssB, c.Bconst], writes=[ssB])
    P.op("act", (lambda ss: lambda e: e.activation(out=ss[:, 2:3], in_=ss[:, 1:2], func=AF.Exp, scale=-0.5))(ss),
         reads=[ssB], writes=[ssB])
    hs = c.hbi % 2
    c.hbi += 1
    hb, hbB = c.hb[hs], c.hbB[hs]
    P.op("dve", (lambda xt, ss, hb: lambda e: e.tensor_scalar(out=hb, in0=xt, scalar1=ss[:, 2:3], scalar2=None,
                                                              op0=ALU.mult))(xt, ss, hb),
         reads=[xb, ssB], writes=[hbB])
    return hb, hbB


def transpose_tile(c, hbp, t, hT, hTbuf):
    P = c.P
    hb, hbB = hbp
    bk, bB = c.bank()
    bkb = bk.bitcast(BF16)

    def tr(e, hb=hb, bkb=bkb):
        r = None
        for k in range(8):
            r = e.transpose(bkb[:, k * 128:(k + 1) * 128], hb[:, k * 128:(k + 1) * 128], c.ident)
        return r
    P.op("pe", tr, reads=[hbB, c.Bconst], writes=[bB])
    P.op("act", (lambda bkb, t: lambda e: e.copy(out=hT[:, :, t * 128:(t + 1) * 128],
                                                 in_=bkb.rearrange("p (k n) -> p k n", k=8)))(bkb, t),
         reads=[bB], writes=[hTbuf])


def post_norm_residual(c, g, t, banks, bankBs, gpost, src_dram, dst_dram):
    P = c.P
    T = 4 * g + t
    s = c.oi % c.n_ot
    c.oi += 1
    ps, psB = c.pss[s], c.pssB[s]
    for hf in range(2):
        P.op("act", (lambda hf, ps: lambda e: e.activation(out=c.junk[:, 0:512], in_=banks[hf], func=AF.Square,
                                                           accum_out=ps[:, hf:hf + 1]))(hf, ps),
             reads=[bankBs[hf]], writes=[psB, c.Bjunk])
    P.op("dve", (lambda ps: lambda e: e.tensor_tensor(out=ps[:, 2:3], in0=ps[:, 0:1], in1=ps[:, 1:2], op=ALU.add))(ps),
         reads=[psB], writes=[psB])
    P.op("act", (lambda ps: lambda e: e.activation(out=ps[:, 3:4], in_=ps[:, 2:3], func=AF.Ln, scale=1.0 / D, bias=c.epsc))(ps),
         reads=[psB, c.Bconst], writes=[psB])
    P.op("act", (lambda ps: lambda e: e.activation(out=ps[:, 4:5], in_=ps[:, 3:4], func=AF.Exp, scale=-0.5))(ps),
         reads=[psB], writes=[psB])
    ot, otB = c.ot[s], c.otB[s]
    xr, xrB = c.xr[s], c.xrB[s]
    P.dma("sp", (lambda xr, T: lambda e: [e.dma_start(out=xr, in_=src_dram[T * 128:(T + 1) * 128, :])])(xr, T),
          1, reads=[c.Bstream[T]], writes=[xrB], pool="ldr", K=2)
    for hf in range(2):
        P.op("dve", (lambda hf, ps, ot: lambda e: e.scalar_tensor_tensor(
            out=ot[:, hf * 512:(hf + 1) * 512], in0=banks[hf], scalar=ps[:, 4:5], in1=gpost[:, hf * 512:(hf + 1) * 512],
            op0=ALU.mult, op1=ALU.mult))(hf, ps, ot), reads=[bankBs[hf], psB, c.Bconst], writes=[otB])
    P.op("pool", (lambda ot, xr: lambda e: e.tensor_tensor(out=ot, in0=ot, in1=xr, op=ALU.add))(ot, xr),
         reads=[otB, xrB], writes=[otB])
    return P.dma("pool", (lambda ot, T: lambda e: [e.dma_start(out=dst_dram[T * 128:(T + 1) * 128, :], in_=ot)])(ot, T),
                 1, reads=[otB], writes=[c.Bstream[T]], pool="st", K=2)


def stage_l0_mixer(c, src_dram, dst_dram, NG):
    P, A, nc = c.P, c.A, c.nc
    I = c.inp
    A.off = c.stage_base
    Win = A.alloc([8, 2560], BF16)
    Wout = A.alloc([8, 1024], BF16)
    hT = [A.alloc([8, 512], BF16) for _ in range(2)]
    hTB = [Buf("hT0"), Buf("hT1")]
    qTs2 = [A.alloc([4, 512], BF16) for _ in range(2)]
    qTBs2 = [Buf("qT0"), Buf("qT1")]
    kR = A.alloc([4, 1536], BF16)
    kRB = [Buf("kR0"), Buf("kR1"), Buf("kR2")]
    vR = A.alloc([12, 8, 65], BF16)
    vRB = [Buf("vR0"), Buf("vR1"), Buf("vR2")]
    Ep = A.alloc([8, 5, 128], BF16)
    uT = A.alloc([4, 512], BF16)
    uTB = Buf("uT")
    vln = A.alloc([4, 512], BF16)
    vlnB = [Buf("vln%d" % t) for t in range(4)]
    vg = [A.alloc([512], F32) for _ in range(2)]
    vgB = [Buf("vg0"), Buf("vg1")]
    vt = [A.alloc([512], F32) for _ in range(2)]
    vtB = [Buf("vt0"), Buf("vt1")]
    st = [A.alloc([16], F32) for _ in range(2)]
    stB = [Buf("st0"), Buf("st1")]
    wsT = A.alloc([4, 128], BF16)
    wsF = c.xr[0][:, 0:512].rearrange("p (a b) -> p a b", a=4)
    tri = A.alloc([128], F32)
    bsb = A.alloc([4, 128], F32)
    lng = A.alloc([512], F32)
    lnb = A.alloc([512], F32)
    gpost = A.alloc([1024], F32)
    gcol = A.alloc([8], F32)
    PP = [A.alloc([5, 4, 128], BF16) for _ in range(2)]
    PPB = [[Buf("PP%d_a" % i), Buf("PP%d_b" % i)] for i in range(2)]
    aout = A.alloc([4, 512], BF16)
    aoutB = Buf("aout")
    rden = [A.alloc([4], F32) for _ in range(2)]
    rdenB = [Buf("rden0"), Buf("rden1")]
    catT = A.alloc([8, 512], BF16)
    catTB = Buf("catTa")
    catTBb = Buf("catTb")
    gt, gtB = vt, vtB
    valid = A.alloc([c.NTILES], F32)

    def ld_consts(e):
        return [e.dma_start(out=wsF, in_=I["wsT"]), e.dma_start(out=tri, in_=I["tri"]),
                e.dma_start(out=bsb, in_=I["bsb"]), e.dma_start(out=lng, in_=I["lng"]),
                e.dma_start(out=lnb, in_=I["lnb"]), e.dma_start(out=gpost, in_=I["gpost0"]),
                e.dma_start(out=gcol, in_=I["gcol0"]),
                e.dma_start(out=valid, in_=I["valid"])]
    P.dma("sp", ld_consts, 8, writes=[c.Bconst, c.xrB[0]], pool="ldc", K=1)
    P.op("dve", lambda e: e.tensor_tensor(out=wsT, in0=wsF, in1=tri.unsqueeze(1).to_broadcast([128, 4, 128]), op=ALU.mult),
         reads=[c.Bconst, c.xrB[0]], writes=[c.Bconst])
    for h8 in range(8):
        stg = c.xt[h8 % 3][:, 0:640].rearrange("p (a b) -> p a b", a=5)
        P.dma("sp", (lambda h8, stg: lambda e: [e.dma_start(out=stg, in_=I["biasT"][:, h8, :, :])])(h8, stg), 1,
              writes=[c.xtB[h8 % 3]], pool="ldx", K=3)
        P.op("act", (lambda h8, stg: lambda e: e.activation(out=Ep[:, h8, :, :], in_=stg, func=AF.Exp))(h8, stg),
             reads=[c.xtB[h8 % 3], c.Bconst], writes=[c.Bconst])
    P.op("act", lambda e: e.activation(out=Ep[64:128, :, 0, 0:64], in_=Ep[64:128, :, 0, 0:64], func=AF.Copy, scale=0.0),
         reads=[c.Bconst], writes=[c.Bconst])
    P.op("act", lambda e: e.activation(out=Ep[0:64, :, 4, 64:128], in_=Ep[0:64, :, 4, 64:128], func=AF.Copy, scale=0.0),
         reads=[c.Bconst], writes=[c.Bconst])
    WinB = load_weight(c, Win, I["ab_w_in"], 8, 2560, "win", gcol=gcol)
    WoutB = load_weight(c, Wout, I["ab_w_out"], 8, 1024, "wout")

    def proj_fm(h, hB, col0, evac):
        bk, bB = c.bank()

        def mm(e):
            r = None
            for k in range(8):
                r = e.matmul(bk, lhsT=Win[:, k, col0:col0 + 128], rhs=h[:, k, :], start=(k == 0), stop=(k == 7))
            return r
        P.op("pe", mm, reads=[hB] + WinB, writes=[bB])
        evac(bk, bB)

    def proj_tm(h, hB, t, col0, evac):
        bk, bB = c.bank()

        def mm(e):
            r = None
            for k in range(8):
                r = e.matmul(bk, lhsT=h[:, k, t * 128:(t + 1) * 128], rhs=Win[:, k, col0:col0 + 512],
                             start=(k == 0), stop=(k == 7))
            return r
        P.op("pe", mm, reads=[hB] + WinB, writes=[bB])
        evac(bk, bB)

    fin = []
    for g in range(NG):
        h, hB = hT[g % 2], hTB[g % 2]
        norm_transpose_group(c, src_dram, g, h, hB)
        half = g % 3
        qT, qTB = qTs2[g % 2], qTBs2[g % 2]
        for ob in range(4):
            proj_fm(h, hB, ob * 128, lambda bk, bB, ob=ob, qT=qT, qTB=qTB: P.op(
                "act", lambda e: e.activation(out=qT[:, ob, :], in_=bk, func=AF.Copy, scale=0.125), reads=[bB], writes=[qTB]))
        for ob in range(4):
            proj_fm(h, hB, 512 + ob * 128, lambda bk, bB, ob=ob, half=half: P.op(
                "act", lambda e: e.activation(out=kR[:, ob, half * 512:(half + 1) * 512], in_=bk, func=AF.Copy), reads=[bB], writes=[kRB[half]]))
        for t in range(4):
            T = 4 * g + t
            sl = T % 12

            def ev(bk, bB, T=T, sl=sl, half=half):
                P.op("dve", lambda e: e.tensor_scalar(out=vR[:, sl, :, 0:64], in0=bk.rearrange("p (h d) -> p h d", h=8),
                                                      scalar1=valid[:, T:T + 1], scalar2=None, op0=ALU.mult),
                     reads=[bB, c.Bconst], writes=[vRB[half]])
                P.op("dve", lambda e: e.tensor_copy(out=vR[:, sl, :, 64:65],
                                                    in_=valid[:, T:T + 1].unsqueeze(1).to_broadcast([128, 8, 1])),
                     reads=[c.Bconst], writes=[vRB[half]])
            proj_tm(h, hB, t, 1024, ev)
        for ob in range(4):
            proj_fm(h, hB, 1536 + ob * 128, lambda bk, bB, ob=ob: P.op(
                "act", lambda e: e.activation(out=uT[:, ob, :], in_=bk, func=AF.Gelu), reads=[bB], writes=[uTB]))
        for t in range(4):
            s = t % 2

            def ev(bk, bB, t=t, s=s):
                P.op("act", lambda e: e.activation(out=vg[s], in_=bk, func=AF.Gelu), reads=[bB], writes=[vgB[s]])
                P.op("dve", lambda e: e.bn_stats(out=st[s][:, 0:6], in_=vg[s]), reads=[vgB[s]], writes=[stB[s]])
                P.op("dve", lambda e: e.bn_aggr(out=st[s][:, 6:8], in_=st[s][:, 0:6]), reads=[stB[s]], writes=[stB[s]])
                P.op("act", lambda e: e.activation(out=st[s][:, 9:10], in_=st[s][:, 7:8], func=AF.Ln, bias=c.epsc),
                     reads=[stB[s], c.Bconst], writes=[stB[s]])
                P.op("act", lambda e: e.activation(out=st[s][:, 8:9], in_=st[s][:, 9:10], func=AF.Exp, scale=-0.5),
                     reads=[stB[s]], writes=[stB[s]])
                P.op("dve", lambda e: e.tensor_scalar(out=vt[s], in0=vg[s], scalar1=st[s][:, 6:7], scalar2=st[s][:, 8:9],
                                                      op0=ALU.subtract, op1=ALU.mult), reads=[vgB[s], stB[s]], writes=[vtB[s]])
                P.op("pool", lambda e: e.tensor_tensor(out=vt[s], in0=vt[s], in1=lng, op=ALU.mult),
                     reads=[vtB[s], c.Bconst], writes=[vtB[s]])
                P.op("pool", lambda e: e.tensor_tensor(out=vln[:, t, :], in0=vt[s], in1=lnb, op=ALU.add),
                     reads=[vtB[s], c.Bconst], writes=[vlnB[t]])
            proj_tm(h, hB, t, 2048, ev)
        for t in range(4):
            bk, bB = c.bank()

            def mm(e, t=t, bk=bk):
                r = None
                for grp in range(4):
                    r = e.matmul(bk[:, grp * 128:(grp + 1) * 128], lhsT=vln[:, t, grp * 128:(grp + 1) * 128],
                                 rhs=wsT[:, grp, :], start=True, stop=True)
                return r
            P.op("pe", mm, reads=[vlnB[t], c.Bconst], writes=[bB])
            s = t % 2
            P.op("dve", (lambda bk, s: lambda e: e.tensor_tensor(out=gt[s], in0=bk, in1=bsb.rearrange("p a b -> p (a b)"),
                                                                 op=ALU.add))(bk, s),
                 reads=[bB, c.Bconst], writes=[gtB[s]])
            P.op("dve", (lambda s, t: lambda e: e.tensor_tensor(out=catT[:, 4:8, t * 128:(t + 1) * 128],
                                                                in0=gt[s].rearrange("p (a b) -> p a b", a=4),
                                                                in1=uT[:, :, t * 128:(t + 1) * 128], op=ALU.mult))(s, t),
                 reads=[gtB[s], uTB], writes=[catTBb])
        SPLIT = 3

        def emit_scores(h8):
            hp, hh = h8 // 2, h8 % 2
            pr = slice(hh * 64, hh * 64 + 64)
            for part, (ra, rb) in enumerate(((0, SPLIT), (SPLIT, 5))):
                def sm(e, ra=ra, rb=rb, pr=pr, hp=hp, g=g, qT=qT):
                    r = None
                    for rr in range(ra, rb):
                        for qb in range(4):
                            Tk = 4 * g + qb - rr
                            if Tk < 0:
                                continue
                            slk = Tk % 12
                            r = e.matmul(c.banks[rr][:, qb * 128:(qb + 1) * 128], lhsT=kR[pr, hp, slk * 128:(slk + 1) * 128],
                                         rhs=qT[pr, hp, qb * 128:(qb + 1) * 128], start=True, stop=True)
                    return r
                if g == 0 and ra > 3:
                    continue
                kth = sorted(set(((4 * g + qb - rr) // 4) % 3 for rr in range(ra, rb) for qb in range(4) if 4 * g + qb - rr >= 0))
                P.op("pe", sm, reads=[kRB[x] for x in kth] + [qTB], writes=[c.bankB[rr] for rr in range(ra, rb)])

        def emit_softmax(h8):
            pp = h8 % 2
            for part, (ra, rb) in enumerate(((0, SPLIT), (SPLIT, 5))):
                if g == 0 and ra > 3:
                    continue
                src = c.psall[:, ra * 512:rb * 512].rearrange("p (a b c) -> p a b c", a=rb - ra, b=4)
                P.op("act", (lambda src, pp, ra, rb: lambda e: e.activation(out=PP[pp][:, ra:rb, :, :], in_=src, func=AF.Exp))(src, pp, ra, rb),
                     reads=[c.bankB[rr] for rr in range(ra, rb)], writes=[PPB[pp][part]])
                P.op("dve", (lambda pp, ra, rb, h8: lambda e: e.tensor_tensor(
                    out=PP[pp][:, ra:rb, :, :], in0=PP[pp][:, ra:rb, :, :],
                    in1=Ep[:, h8, ra:rb, :].unsqueeze(2).to_broadcast([128, rb - ra, 4, 128]), op=ALU.mult))(pp, ra, rb, h8),
                    reads=[PPB[pp][part], c.Bconst], writes=[PPB[pp][part]])

        def emit_pv(h8):
            pp = h8 % 2
            ib = 5 + (h8 % 2)
            bk, bB = c.banks[ib], c.bankB[ib]

            def pv(e, bk=bk, h8=h8, pp=pp, g=g):
                r = None
                for qb in range(4):
                    rs_ = [rr for rr in range(5) if 4 * g + qb - rr >= 0]
                    for i, rr in enumerate(rs_):
                        Tk = 4 * g + qb - rr
                        r = e.matmul(bk[:, qb * 128:qb * 128 + 65], lhsT=PP[pp][:, rr, qb, :],
                                     rhs=vR[:, Tk % 12, h8, :], start=(i == 0), stop=(i == len(rs_) - 1))
                return r
            vth = [vRB[g % 3]] + ([vRB[(g - 1) % 3]] if g > 0 else [])
            P.op("pe", pv, reads=PPB[pp] + vth, writes=[bB])
            rs = h8 % 2
            bk3 = bk.rearrange("p (a b) -> p a b", a=4)
            P.op("dve", (lambda bk3, rs: lambda e: e.tensor_scalar(out=rden[rs], in0=bk3[:, :, 64], scalar1=1e-20, scalar2=None,
                                                                   op0=ALU.max))(bk3, rs), reads=[bB], writes=[rdenB[rs]])
            P.op("dve", (lambda rs: lambda e: e.reciprocal(out=rden[rs], in_=rden[rs]))(rs), reads=[rdenB[rs]], writes=[rdenB[rs]])
            P.op("dve", (lambda bk3, rs, h8: lambda e: e.tensor_tensor(
                out=aout[:, :, h8 * 64:(h8 + 1) * 64], in0=bk3[:, :, 0:64],
                in1=rden[rs].unsqueeze(2).to_broadcast([128, 4, 64]), op=ALU.mult))(bk3, rs, h8),
                reads=[bB, rdenB[rs]], writes=[aoutB])

        emit_scores(0)
        emit_softmax(0)
        for h8 in range(8):
            if h8 < 7:
                emit_scores(h8 + 1)
                emit_softmax(h8 + 1)
            emit_pv(h8)
        if g == 0:
            dbg(c, "ss", c.ss[0], [c.ssB[0]])
            dbg(c, "hT", h, [hB], BF16)
            dbg(c, "uT", uT, [uTB], BF16)
            dbg(c, "vln", vln, vlnB, BF16)
            dbg(c, "Ep", Ep, [c.Bconst], BF16)
            dbg(c, "aout", aout, [aoutB], BF16)
        for hf in range(2):
            bk, bB = c.bank()
            bkb = bk.bitcast(BF16)

            def tr(e, bkb=bkb, hf=hf):
                r = None
                for fb in range(2 * hf, 2 * hf + 2):
                    for t in range(4):
                        i = (fb - 2 * hf) * 4 + t
                        r = e.transpose(bkb[:, i * 128:(i + 1) * 128], aout[:, t, fb * 128:(fb + 1) * 128], c.ident)
                return r
            P.op("pe", tr, reads=[aoutB, c.Bconst], writes=[bB])
            P.op("act", (lambda bkb, hf: lambda e: e.copy(out=catT[:, 2 * hf:2 * hf + 2, :],
                                                          in_=bkb.rearrange("p (a b) -> p a b", a=2)))(bkb, hf),
                 reads=[bB], writes=[catTB])
        for t in range(4):
            bks = [c.bank(), c.bank()]
            for hf in range(2):
                bk = bks[hf][0]

                def mm(e, bk=bk, hf=hf, t=t):
                    r = None
                    for k in range(8):
                        r = e.matmul(bk, lhsT=catT[:, k, t * 128:(t + 1) * 128], rhs=Wout[:, k, hf * 512:(hf + 1) * 512],
                                     start=(k == 0), stop=(k == 7))
                    return r
                P.op("pe", mm, reads=[catTB, catTBb] + WoutB, writes=[bks[hf][1]])
            fin.append(post_norm_residual(c, g, t, [bks[0][0], bks[1][0]], [bks[0][1], bks[1][1]], gpost, src_dram, dst_dram))
        if g == 0:
            dbg(c, "catT", catT, [catTB, catTBb], BF16)
    return fin


def stage_ffn(c, L, src_dram, dst_dram, g0, NG):
    P, A = c.P, c.A
    I = c.inp
    A.off = c.stage_base_small
    c.n_ot = 1
    Wg = A.alloc([8, DFF], BF16)
    Wu = A.alloc([8, DFF], BF16)
    Wd = A.alloc([NFB, 1024], BF16)
    hTs = [A.alloc([8, 512], BF16) for _ in range(2)]
    hTBs = [Buf("f_hT0"), Buf("f_hT1")]
    actT = A.alloc([NFB, 512], BF16)
    actB = [Buf("f_act%d" % i) for i in range(NFB)]
    sg = [A.alloc([512], BF16) for _ in range(2)]
    sgB = [Buf("f_sg0"), Buf("f_sg1")]
    gpost = A.alloc([1024], F32)
    gcol = A.alloc([8], F32)
    P.dma("sp", lambda e: [e.dma_start(out=gpost, in_=I["gpostf%d" % L]), e.dma_start(out=gcol, in_=I["gcolf%d" % L])], 2,
          writes=[c.Bconst], pool="ldc", K=1)
    fchunks = [(0, 256), (256, 512), (768, 1024), (1792, 1024)]
    fb_chunk = [next(i for i, (c0, w) in enumerate(fchunks) if c0 <= fb * 128 < c0 + w) for fb in range(NFB)]
    WgB, WuB = [], []
    for ci in range(len(fchunks)):
        WgB += load_weight_cols(c, Wg, I["ffn_w_gate%d" % L], 8, fchunks[ci:ci + 1], "wg%d" % ci, gcol=gcol)
        WuB += load_weight_cols(c, Wu, I["ffn_w_up%d" % L], 8, fchunks[ci:ci + 1], "wu%d" % ci, gcol=gcol)
    WdB = load_weight(c, Wd, I["ffn_w_down%d" % L], NFB, 1024, "wd")
    fin = []
    norm_transpose_group(c, src_dram, g0, hTs[0], hTBs[0])
    for g in range(g0, g0 + NG):
        hT, hTB = hTs[(g - g0) % 2], hTBs[(g - g0) % 2]
        for fb in range(NFB):
            if g + 1 < g0 + NG and fb in (1, 6, 11, 16):
                pend_hb = norm_tile(c, src_dram, g + 1, (fb - 1) // 5)
            if g + 1 < g0 + NG and fb in (5, 10, 15, 20):
                transpose_tile(c, pend_hb, (fb - 5) // 5, hTs[(g + 1 - g0) % 2], hTBs[(g + 1 - g0) % 2])
            bg, bgB = c.bank()
            bu, buB = c.bank()

            def mm(e, fb=fb, bg=bg, bu=bu, hT=hT):
                r = None
                for k in range(8):
                    r = e.matmul(bg, lhsT=Wg[:, k, fb * 128:(fb + 1) * 128], rhs=hT[:, k, :], start=(k == 0), stop=(k == 7))
                for k in range(8):
                    r = e.matmul(bu, lhsT=Wu[:, k, fb * 128:(fb + 1) * 128], rhs=hT[:, k, :], start=(k == 0), stop=(k == 7))
                return r
            P.op("pe", mm, reads=[hTB, WgB[fb_chunk[fb]], WuB[fb_chunk[fb]]], writes=[bgB, buB])
            s = fb % 2
            P.op("act", (lambda bg, s: lambda e: e.activation(out=sg[s], in_=bg, func=AF.Silu))(bg, s), reads=[bgB], writes=[sgB[s]])
            P.op("dve", (lambda bu, s, fb: lambda e: e.tensor_tensor(out=actT[:, fb, :], in0=bu, in1=sg[s], op=ALU.mult))(bu, s, fb),
                 reads=[buB, sgB[s]], writes=[actB[fb]])
        for t in range(4):
            bks = [c.bank(), c.bank()]
            for hf in range(2):
                bk = bks[hf][0]

                def mm2(e, bk=bk, hf=hf, t=t):
                    r = None
                    for fb in range(NFB):
                        r = e.matmul(bk, lhsT=actT[:, fb, t * 128:(t + 1) * 128], rhs=Wd[:, fb, hf * 512:(hf + 1) * 512],
                                     start=(fb == 0), stop=(fb == NFB - 1))
                    return r
                P.op("pe", mm2, reads=actB + WdB, writes=[bks[hf][1]])
            fin.append(post_norm_residual(c, g, t, [bks[0][0], bks[1][0]], [bks[0][1], bks[1][1]], gpost, src_dram, dst_dram))
    c.n_ot = 2
    return fin


def stage_l1_mixer(c, src_dram, dst_dram, NG, g_own):
    P, A = c.P, c.A
    I = c.inp
    A.off = c.stage_base
    Win = A.alloc([8, 3088], BF16)
    Wout = A.alloc([8, 1024], BF16)
    hT = [A.alloc([8, 512], BF16) for _ in range(2)]
    hTB = [Buf("g_hT0"), Buf("g_hT1")]
    qTs = A.alloc([4, 512], F32)
    qTsB = Buf("g_qTs")
    kTs = A.alloc([4, 512], F32)
    kTsB = Buf("g_kTs")
    vsb = A.alloc([4, 1024], BF16)
    vsbB = [Buf("g_v%d" % t) for t in range(4)]
    sgb = A.alloc([4, 1024], BF16)
    sgbB = [Buf("g_sg%d" % t) for t in range(4)]
    sp = [A.alloc([512], F32) for _ in range(2)]
    spB = [Buf("g_sp0"), Buf("g_sp1")]
    e1, e1B = sp, spB
    eq = [A.alloc([4, 128], F32) for _ in range(2)]
    eqB = [Buf("g_eq0"), Buf("g_eq1")]
    ek = [A.alloc([4, 128], F32) for _ in range(2)]
    ekB = [Buf("g_ek0"), Buf("g_ek1")]
    qtT = [A.alloc([4, 128], BF16) for _ in range(2)]
    qtTB = [Buf("g_qt0"), Buf("g_qt1")]
    ktT = [A.alloc([4, 128], BF16) for _ in range(2)]
    ktTB = [Buf("g_kt0"), Buf("g_kt1")]
    ktm = [A.alloc([4, 128], BF16) for _ in range(2)]
    ktmB = [Buf("g_ktm0"), Buf("g_ktm1")]
    atT = [A.alloc([4, 128], BF16) for _ in range(2)]
    atTB = [Buf("g_at0"), Buf("g_at1")]
    S = A.alloc([4, 256], F32)
    SB = Buf("g_S")
    Sbf = A.alloc([4, 256], BF16)
    SbfB = Buf("g_Sbf")
    aT = A.alloc([512], F32)
    aTB = Buf("g_aT")
    w2 = A.alloc([512], F32)
    Utri = A.alloc([128], F32)
    tri = A.alloc([128], F32)
    normg = A.alloc([1024], F32)
    gpost = A.alloc([1024], F32)
    gcol = A.alloc([8], F32)
    tmp = A.alloc([1024], F32)
    tmpB = Buf("g_tmp")
    gated = [A.alloc([1024], BF16) for _ in range(2)]
    gatedB = [Buf("g_gated0"), Buf("g_gated1")]
    oT = A.alloc([8, 512], BF16)
    oTB = [Buf("g_oT%d" % t) for t in range(4)]
    r4 = [A.alloc([12], F32) for _ in range(2)]
    r4B = [Buf("g_r40"), Buf("g_r41")]

    def ld_consts(e):
        return [e.dma_start(out=w2[0:33, :], in_=I["w2aug"]), e.dma_start(out=tri, in_=I["tri"]),
                e.dma_start(out=normg, in_=I["normg"]), e.dma_start(out=gpost, in_=I["gpost1"]),
                e.dma_start(out=gcol, in_=I["gcol1"])]
    P.dma("sp", ld_consts, 5, writes=[c.Bconst], pool="ldc", K=1)
    P.op("dve", lambda e: e.tensor_scalar(out=Utri, in0=tri, scalar1=-1.0 / 16.0, scalar2=None, op0=ALU.mult),
         reads=[c.Bconst], writes=[c.Bconst])
    P.op("dve", lambda e: e.memset(aT[0:32, :], 0.0), writes=[aTB])
    P.op("dve", lambda e: e.memset(aT[32:33, :], 1.0), writes=[aTB])
    aT2 = tmp[:, 0:512]
    P.op("dve", lambda e: e.memset(aT2[0:32, :], 0.0), writes=[tmpB])
    P.op("dve", lambda e: e.memset(aT2[32:33, :], 1.0), writes=[tmpB])
    P.op("dve", lambda e: e.memset(S, 0.0), writes=[SB])
    P.op("dve", lambda e: e.memset(Sbf, 0.0), writes=[SbfB])
    WinB = load_weight(c, Win, I["c_w_in"], 8, 3088, "cwin", gcol=gcol)
    WoutB = load_weight(c, Wout, I["c_w_out"], 8, 1024, "cwout")
    QS = 128.0 ** -0.5

    def proj_fm(h, hB, col0, M, evac):
        bk, bB = c.bank()

        def mm(e):
            r = None
            for k in range(8):
                r = e.matmul(bk[0:M, :], lhsT=Win[:, k, col0:col0 + M], rhs=h[:, k, :], start=(k == 0), stop=(k == 7))
            return r
        P.op("pe", mm, reads=[hB] + WinB, writes=[bB])
        evac(bk, bB)

    def proj_tm(h, hB, t, col0, evac):
        bk, bB = c.bank()

        def mm(e):
            r = None
            for k in range(8):
                r = e.matmul(bk, lhsT=h[:, k, t * 128:(t + 1) * 128], rhs=Win[:, k, col0:col0 + 512],
                             start=(k == 0), stop=(k == 7))
            return r
        P.op("pe", mm, reads=[hB] + WinB, writes=[bB])
        evac(bk, bB)

    fin = []
    for g in range(NG):
        full = g >= g_own
        h, hB = hT[g % 2], hTB[g % 2]
        if (not full) and g % 2 == 1:
            kTs_g, kTsB_g, vsb_g, vsbB_g, aT_g, aTB_g = qTs, qTsB, sgb, sgbB, aT2, tmpB
        else:
            kTs_g, kTsB_g, vsb_g, vsbB_g, aT_g, aTB_g = kTs, kTsB, vsb, vsbB, aT, aTB
        norm_transpose_group(c, src_dram, g, h, hB)
        if full:
            for hd in range(4):
                proj_fm(h, hB, hd * 128, 128, lambda bk, bB, hd=hd: P.op(
                    "act", lambda e: e.activation(out=qTs[:, hd, :], in_=bk, func=AF.Copy, scale=QS), reads=[bB], writes=[qTsB]))
        for hd in range(4):
            proj_fm(h, hB, 512 + hd * 128, 128, lambda bk, bB, hd=hd, kTs_g=kTs_g, kTsB_g=kTsB_g: P.op(
                "act", lambda e: e.activation(out=kTs_g[:, hd, :], in_=bk, func=AF.Copy, scale=1.0), reads=[bB], writes=[kTsB_g]))
        proj_fm(h, hB, 3072, 16, lambda bk, bB, aT_g=aT_g, aTB_g=aTB_g: P.op(
            "act", lambda e: e.activation(out=aT_g[0:16, :], in_=bk[0:16, :], func=AF.Copy, scale=1.0), reads=[bB], writes=[aTB_g]))
        for t in range(4):
            for hf in range(2):
                proj_tm(h, hB, t, 1024 + hf * 512, lambda bk, bB, t=t, hf=hf, vsb_g=vsb_g, vsbB_g=vsbB_g: P.op(
                    "dve", lambda e: e.tensor_copy(out=vsb_g[:, t, hf * 512:(hf + 1) * 512], in_=bk), reads=[bB], writes=[vsbB_g[t]]))
            if full:
                for hf in range(2):
                    proj_tm(h, hB, t, 2048 + hf * 512, lambda bk, bB, t=t, hf=hf: P.op(
                        "act", lambda e: e.activation(out=sgb[:, t, hf * 512:(hf + 1) * 512], in_=bk, func=AF.Silu),
                        reads=[bB], writes=[sgbB[t]]))
        def make_steps(t, g=g, full=full, h=h, hB=hB, kTs=kTs_g, kTsB=kTsB_g, vsb=vsb_g, vsbB=vsbB_g, aT=aT_g, aTB=aTB_g):
            s = t % 2
            st8 = {}
            steps = []

            def s0():
                bl, blB = c.bank()
                st8["bl"] = (bl, blB)
                P.op("pe", (lambda bl, t: lambda e: e.matmul(bl, lhsT=aT[0:33, t * 128:(t + 1) * 128], rhs=w2[0:33, :],
                                                             start=True, stop=True))(bl, t),
                     reads=[aTB, c.Bconst], writes=[blB])
            steps.append(s0)

            def s1():
                bl, blB = st8["bl"]
                P.op("act", (lambda bl, s: lambda e: e.activation(out=e1[s], in_=bl, func=AF.Exp, scale=-1.0))(bl, s),
                     reads=[blB], writes=[e1B[s]])
                P.op("act", (lambda s: lambda e: e.activation(out=sp[s], in_=e1[s], func=AF.Ln, bias=1.0))(s),
                     reads=[e1B[s]], writes=[spB[s]])
            steps.append(s1)

            def s2():
                bc, bcB = c.bank()
                st8["bc"] = (bc, bcB)

                def cm(e, bc=bc, s=s):
                    r = None
                    for hd in range(4):
                        r = e.matmul(bc[:, hd * 128:(hd + 1) * 128], lhsT=sp[s][:, hd * 128:(hd + 1) * 128], rhs=Utri,
                                     start=True, stop=True)
                    return r
                P.op("pe", cm, reads=[spB[s], c.Bconst], writes=[bcB])
            steps.append(s2)

            def s3():
                bc, bcB = st8["bc"]
                bc3 = bc.rearrange("p (a b) -> p a b", a=4)
                P.op("act", (lambda bc3, s: lambda e: e.activation(out=ek[s], in_=bc3, func=AF.Exp, scale=-1.0))(bc3, s),
                     reads=[bcB], writes=[ekB[s]])
                if full:
                    P.op("act", (lambda bc3, s: lambda e: e.activation(out=eq[s], in_=bc3, func=AF.Exp))(bc3, s),
                         reads=[bcB], writes=[eqB[s]])
                else:
                    P.op("act", (lambda bc3, s: lambda e: e.activation(out=eq[s][:, :, 127:128], in_=bc3[:, :, 127:128], func=AF.Exp))(bc3, s),
                         reads=[bcB], writes=[eqB[s]])
            steps.append(s3)

            def s4():
                P.op("dve", (lambda s, t: lambda e: e.tensor_tensor(out=ktT[s], in0=kTs[:, :, t * 128:(t + 1) * 128], in1=ek[s],
                                                                    op=ALU.mult))(s, t),
                     reads=[kTsB, ekB[s]], writes=[ktTB[s]])
                if full:
                    P.op("dve", (lambda s, t: lambda e: e.tensor_tensor(out=qtT[s], in0=qTs[:, :, t * 128:(t + 1) * 128], in1=eq[s],
                                                                        op=ALU.mult))(s, t),
                         reads=[qTsB, eqB[s]], writes=[qtTB[s]])
            steps.append(s4)

            def s5():
                if full:
                    ba, baB = c.bank()
                    st8["ba"] = (ba, baB)

                    def am(e, ba=ba, s=s):
                        r = None
                        for hd in range(4):
                            r = e.matmul(ba[:, hd * 128:(hd + 1) * 128], lhsT=ktT[s][:, hd, :], rhs=qtT[s][:, hd, :],
                                         start=True, stop=True)
                        return r
                    P.op("pe", am, reads=[ktTB[s], qtTB[s]], writes=[baB])
                bt, btB = c.bank()
                st8["bt"] = (bt, btB)
                btb = bt.bitcast(BF16)

                def ktr(e, btb=btb, s=s):
                    r = None
                    for hd in range(4):
                        r = e.transpose(btb[:, hd * 128:(hd + 1) * 128], ktT[s][:, hd, :], c.ident)
                    return r
                P.op("pe", ktr, reads=[ktTB[s], c.Bconst], writes=[btB])
            steps.append(s5)

            def s6():
                if full:
                    ba, baB = st8["ba"]
                    P.op("dve", (lambda ba, s: lambda e: e.tensor_tensor(
                        out=atT[s], in0=ba.rearrange("p (a b) -> p a b", a=4),
                        in1=tri.unsqueeze(1).to_broadcast([128, 4, 128]), op=ALU.mult))(ba, s),
                        reads=[baB, c.Bconst], writes=[atTB[s]])
                bt, btB = st8["bt"]
                btb = bt.bitcast(BF16)
                P.op("act", (lambda btb, s: lambda e: e.copy(out=ktm[s], in_=btb[:, 0:512].rearrange("p (a b) -> p a b", a=4)))(btb, s),
                     reads=[btB], writes=[ktmB[s]])
            steps.append(s6)

            def s7():
                kv = [c.bank(), c.bank()]
                st8["kv"] = kv

                def kvm(e, kv=kv, s=s, t=t):
                    r = None
                    for hd in range(4):
                        r = e.matmul(kv[hd // 2][0][:, (hd % 2) * 256:(hd % 2 + 1) * 256], lhsT=ktm[s][:, hd, :],
                                     rhs=vsb[:, t, hd * 256:(hd + 1) * 256], start=True, stop=True)
                    return r
                P.op("pe", kvm, reads=[ktmB[s], vsbB[t]], writes=[kv[0][1], kv[1][1]])
                if full:
                    ob = [c.bank(), c.bank()]
                    st8["ob"] = ob

                    def om(e, ob=ob, s=s, t=t):
                        r = None
                        for hd in range(4):
                            o_ap = ob[hd // 2][0][:, (hd % 2) * 256:(hd % 2 + 1) * 256]
                            e.matmul(o_ap, lhsT=atT[s][:, hd, :], rhs=vsb[:, t, hd * 256:(hd + 1) * 256], start=True, stop=False)
                            r = e.matmul(o_ap, lhsT=qtT[s][:, hd, :], rhs=Sbf[:, hd, :], start=False, stop=True)
                        return r
                    P.op("pe", om, reads=[atTB[s], qtTB[s], vsbB[t], SbfB], writes=[ob[0][1], ob[1][1]])
            steps.append(s7)

            def s8():
                kv = st8["kv"]
                for hf in range(2):
                    P.op("dve", (lambda hf, kv: lambda e: e.tensor_tensor(
                        out=S[:, 2 * hf:2 * hf + 2, :], in0=kv[hf][0].rearrange("p (a b) -> p a b", a=2), in1=S[:, 2 * hf:2 * hf + 2, :],
                        op=ALU.add))(hf, kv), reads=[kv[hf][1], SB], writes=[SB])
                P.op("dve", (lambda s: lambda e: e.tensor_tensor(out=Sbf, in0=S, in1=eq[s][:, :, 127:128].to_broadcast([128, 4, 256]),
                                                                 op=ALU.mult))(s), reads=[SB, eqB[s]], writes=[SbfB])
                P.op("dve", (lambda s: lambda e: e.tensor_tensor(out=S, in0=S, in1=eq[s][:, :, 127:128].to_broadcast([128, 4, 256]),
                                                                 op=ALU.mult))(s), reads=[SB, eqB[s]], writes=[SB])
            steps.append(s8)
            if not full:
                return steps

            def s9():
                ob = st8["ob"]
                rs = c.r4i % 2
                c.r4i += 1
                r4_, r4B_ = r4[rs], r4B[rs]
                st8["r4"] = (r4_, r4B_)
                for hd in range(4):
                    P.op("act", (lambda hd, ob, r4_: lambda e: e.activation(
                        out=c.junk[:, 0:256], in_=ob[hd // 2][0][:, (hd % 2) * 256:(hd % 2 + 1) * 256], func=AF.Square,
                        accum_out=r4_[:, hd:hd + 1]))(hd, ob, r4_), reads=[ob[hd // 2][1]], writes=[r4B_, c.Bjunk])
                P.op("act", (lambda r4_: lambda e: e.activation(out=r4_[:, 4:8], in_=r4_[:, 0:4], func=AF.Ln, scale=1.0 / 256.0,
                                                                bias=c.epsc))(r4_), reads=[r4B_, c.Bconst], writes=[r4B_])
                P.op("act", (lambda r4_: lambda e: e.activation(out=r4_[:, 8:12], in_=r4_[:, 4:8], func=AF.Exp, scale=-0.5))(r4_),
                     reads=[r4B_], writes=[r4B_])
            steps.append(s9)

            def s10():
                ob = st8["ob"]
                r4_, r4B_ = st8["r4"]
                for hd in range(4):
                    P.op("dve", (lambda hd, ob, r4_: lambda e: e.scalar_tensor_tensor(
                        out=tmp[:, hd * 256:(hd + 1) * 256], in0=ob[hd // 2][0][:, (hd % 2) * 256:(hd % 2 + 1) * 256],
                        scalar=r4_[:, 8 + hd:9 + hd], in1=normg[:, hd * 256:(hd + 1) * 256], op0=ALU.mult, op1=ALU.mult))(hd, ob, r4_),
                        reads=[ob[hd // 2][1], r4B_, c.Bconst], writes=[tmpB])
                gd, gdB = gated[t % 2], gatedB[t % 2]
                P.op("pool", (lambda gd, t: lambda e: e.tensor_tensor(out=gd, in0=tmp, in1=sgb[:, t, :], op=ALU.mult))(gd, t),
                     reads=[tmpB, sgbB[t]], writes=[gdB])
            steps.append(s10)

            def s11():
                gd, gdB = gated[t % 2], gatedB[t % 2]
                bo, boB = c.bank()
                bob = bo.bitcast(BF16)

                def otr(e, bob=bob, gd=gd):
                    r = None
                    for k in range(8):
                        r = e.transpose(bob[:, k * 128:(k + 1) * 128], gd[:, k * 128:(k + 1) * 128], c.ident)
                    return r
                P.op("pe", otr, reads=[gdB, c.Bconst], writes=[boB])
                P.op("act", (lambda bob, t: lambda e: e.copy(out=oT[:, :, t * 128:(t + 1) * 128],
                                                             in_=bob.rearrange("p (k n) -> p k n", k=8)))(bob, t),
                     reads=[boB], writes=[oTB[t]])
            steps.append(s11)

            def s12():
                bks = [c.bank(), c.bank()]
                st8["bks"] = bks
                for hf in range(2):
                    bk = bks[hf][0]

                    def mm3(e, bk=bk, hf=hf, t=t):
                        r = None
                        for k in range(8):
                            r = e.matmul(bk, lhsT=oT[:, k, t * 128:(t + 1) * 128], rhs=Wout[:, k, hf * 512:(hf + 1) * 512],
                                         start=(k == 0), stop=(k == 7))
                        return r
                    P.op("pe", mm3, reads=[oTB[t]] + WoutB, writes=[bks[hf][1]])
            steps.append(s12)

            def s13():
                bks = st8["bks"]
                fin.append(post_norm_residual(c, g, t, [bks[0][0], bks[1][0]], [bks[0][1], bks[1][1]], gpost, src_dram, dst_dram))
            steps.append(s13)
            return steps

        LAG = 3
        for ta, tb in ((0, 1), (2, 3)):
            sa, sb_ = make_steps(ta), make_steps(tb)
            n = len(sa)
            for k in range(n + LAG):
                if k < n:
                    sa[k]()
                if 0 <= k - LAG < n:
                    sb_[k - LAG]()
    return fin

INPUT_SHAPES = {
    "ident": ([128, 128], F32),
}


def build(NT, stages, inputs_np, debug=False, g_own=0):
    nc = bass.Bass("TRN2", target_bir_lowering=False)
    c = Ctx()
    c.nc = nc
    c.debug = debug
    c.g_own = g_own
    c.dbg_ops = []
    c.P = Prog(nc)
    c.NT = NT
    c.NTILES = NT // 128
    c.inp = {}
    for name, arr in inputs_np.items():
        c.inp[name] = nc.dram_tensor(name, list(arr.shape), F32, kind="ExternalInput").ap()
    y = nc.dram_tensor("y", [NT, D], F32, kind="ExternalOutput").ap()
    c.Bstream = [Buf("ys%d" % i) for i in range(c.NTILES)]
    A = Arena(nc, 204 * 1024)
    c.A = A
    c.identf = A.alloc([128], F32)
    c.ident = A.alloc([128], BF16)
    c.junk = A.alloc([1024], BF16)
    c.epsc = A.alloc([1], F32)
    c.Bjunk = Buf("junk")
    c.Bconst = Buf("const")
    c.xt = [A.alloc([1024], F32) for _ in range(3)]
    c.xtB = [Buf("xt%d" % i) for i in range(3)]
    c.ss = [A.alloc([4], F32) for _ in range(3)]
    c.ssB = [Buf("ss%d" % i) for i in range(3)]
    c.hb = [A.alloc([1024], BF16) for _ in range(2)]
    c.hbB = [Buf("hb0"), Buf("hb1")]
    c.pss = [A.alloc([8], F32) for _ in range(2)]
    c.pssB = [Buf("pss0"), Buf("pss1")]
    ot0 = A.alloc([1024], F32)
    xr0 = A.alloc([1024], F32)
    c.stage_base_small = A.off
    c.ot = [ot0, A.alloc([1024], F32)]
    c.otB = [Buf("ot0"), Buf("ot1")]
    c.xr = [xr0, A.alloc([1024], F32)]
    c.xrB = [Buf("xr0"), Buf("xr1")]
    c.n_ot = 2
    c.xi = 0
    c.oi = 0
    c.pei = 0
    c.hbi = 0
    c.r4i = 0
    c.stage_base = A.off
    psall = nc.alloc_psum_tensor("psall", [128, 4096], F32)
    c.psall = psall[:]
    banks = [psall[:, i * 512:(i + 1) * 512] for i in range(8)]
    bankB = [Buf("bank%d" % i, excl=True) for i in range(8)]
    c.banks = banks
    c.bankB = bankB
    c.bi = 0

    def bank():
        i = c.bi % 8
        c.bi += 1
        return banks[i], bankB[i]
    c.bank = bank
    P = c.P
    P.dma("sp", lambda e: [e.dma_start(out=c.identf, in_=c.inp["ident"])], 1, writes=[c.Bconst], pool="ldc", K=1)
    P.op("dve", lambda e: e.tensor_copy(out=c.ident, in_=c.identf), reads=[c.Bconst], writes=[c.Bconst])
    P.op("dve", lambda e: e.memset(c.epsc, EPS), reads=[], writes=[c.Bconst])
    fin = []
    src = c.inp["x"]
    for stg in stages:
        if stg == "l0mix":
            fin = stage_l0_mixer(c, src, y, NT // 512)
        elif stg == "l1mix":
            fin = stage_l1_mixer(c, src, y, NT // 512, c.g_own)
        elif stg == "ffn0":
            fin = stage_ffn(c, 0, src, y, 0, NT // 512)
        elif stg == "ffn1":
            fin = stage_ffn(c, 1, src, y, c.g_own, NT // 512 - c.g_own)
        else:
            raise ValueError(stg)
        src = y
        P.fence()
    P.emit(final_wait_ops=list(fin) + c.dbg_ops)
    return nc


def rep128(v):
    return np.ascontiguousarray(np.broadcast_to(np.asarray(v, np.float32).reshape(1, -1), (128, v.size)))


def col128(v):
    v = np.asarray(v, np.float32)
    return np.ascontiguousarray(v.reshape(-1, 128).T)


def prep_common(inp):
    d = {}
    d["ident"] = np.eye(128, dtype=np.float32)
    rb = np.asarray(inp["a_rel_bias"][0], np.float32)
    kj = np.arange(128)[:, None, None]
    r = np.arange(5)[None, :, None]
    qi = np.arange(128)[None, None, :]
    idx = np.clip(r * 128 + qi - kj, -256, 256) + 256
    d["biasT"] = np.ascontiguousarray(rb[:, idx].transpose(1, 0, 2, 3))
    d["wsT"] = np.ascontiguousarray(np.asarray(inp["b_w_s"][0], np.float32).transpose(2, 0, 1))
    d["tri"] = (np.arange(128)[:, None] <= np.arange(128)[None, :]).astype(np.float32)
    d["bsb"] = np.ascontiguousarray(np.broadcast_to(np.asarray(inp["b_b_s"][0], np.float32)[None], (128, 4, 128)))
    d["lng"] = rep128(inp["b_ln_g"][0])
    d["lnb"] = rep128(inp["b_ln_b"][0])
    d["gpost0"] = rep128(inp["post_mix_g"][0])
    d["gcol0"] = col128(inp["pre_mix_g"][0])
    d["ab_w_in"] = np.ascontiguousarray(inp["ab_w_in"][0], dtype=np.float32)
    d["ab_w_out"] = np.ascontiguousarray(inp["ab_w_out"][0], dtype=np.float32)
    d["gcol1"] = col128(inp["pre_mix_g"][1])
    d["gpost1"] = rep128(inp["post_mix_g"][1])
    d["normg"] = rep128(inp["c_norm_g"][0])
    w2 = np.zeros((33, 512), np.float32)
    w2[0:16] = np.asarray(inp["c_w_a2"][0], np.float32)
    w2[32] = np.asarray(inp["c_b_a"][0], np.float32)
    d["w2aug"] = w2
    d["c_w_in"] = np.ascontiguousarray(inp["c_w_in"][0], dtype=np.float32)
    d["c_w_out"] = np.ascontiguousarray(inp["c_w_out"][0], dtype=np.float32)
    for L in range(2):
        d["gpostf%d" % L] = rep128(inp["post_ffn_g"][L])
        d["gcolf%d" % L] = col128(inp["pre_ffn_g"][L])
        d["ffn_w_gate%d" % L] = np.ascontiguousarray(inp["ffn_w_gate"][L], dtype=np.float32)
        d["ffn_w_up%d" % L] = np.ascontiguousarray(inp["ffn_w_up"][L], dtype=np.float32)
        d["ffn_w_down%d" % L] = np.ascontiguousarray(inp["ffn_w_down"][L], dtype=np.float32)
    return d


STAGES = ["l0mix", "ffn0", "l1mix", "ffn1"]
SEQ_HALF = 4096
_CACHE = {}


def kernel(**inputs):
    inp = {k: np.asarray(v) for k, v in inputs.items()}
    x = np.asarray(inp["x"], np.float32)
    B = x.shape[0]
    common = prep_common(inp)
    NT = 2 * SEQ_HALF
    in_maps = []
    for core in range(8):
        b, half = core // 2, core % 2
        d = dict(common)
        own = x[b, half * SEQ_HALF:(half + 1) * SEQ_HALF]
        if half == 1:
            prev = x[b, 0:SEQ_HALF]
            vprev = np.ones(SEQ_HALF, np.float32)
        else:
            prev = np.zeros_like(own)
            vprev = np.zeros(SEQ_HALF, np.float32)
        d["x"] = np.ascontiguousarray(np.concatenate([prev, own], axis=0))
        d["valid"] = col128(np.concatenate([vprev, np.ones(SEQ_HALF, np.float32)]))
        in_maps.append(d)
    if "nc" not in _CACHE:
        _CACHE["nc"] = build(NT, STAGES, in_maps[0], g_own=SEQ_HALF // 512)
    nc = _CACHE["nc"]
    res = run_bass_kernel_spmd(nc, in_maps, core_ids=list(range(8)))
    out = np.empty((B, 2 * SEQ_HALF, D), np.float32)
    for core in range(8):
        b, half = core // 2, core % 2
        out[b, half * SEQ_HALF:(half + 1) * SEQ_HALF] = np.asarray(res.results[core]["y"])[SEQ_HALF:]
    return out
```
